# Optimizing a Trainium2 kernel written in Bass

```python
import jax, jax.numpy as jnp
from jax import lax
import numpy as np

D_MODEL = 1024
BATCH = 8
SEQ = 2048
DEPTH = 4

HEAD_DIM = 64
ROPE_THETA = 10000.0
GM_HEADS = D_MODEL // 2 // HEAD_DIM
GM_CHUNK = 128
NSA_HEADS = D_MODEL // 2 // HEAD_DIM
NSA_KV = NSA_HEADS // 4
NSA_HPG = NSA_HEADS // NSA_KV
CMP_LEN = 32
CMP_STRIDE = 16
SEL_BLOCK = 64
SEL_TOPN = 8
WINDOW = 512
Q_BLOCK = 128
POOL_WIDTH = D_MODEL // 2
POOL_WINDOWS = (2, 4, 8, 16)
POOL_GROUP = POOL_WIDTH // len(POOL_WINDOWS)
S5_WIDTH = D_MODEL // 2
S5_GROUP_CH = 16
S5_GROUPS = S5_WIDTH // S5_GROUP_CH
S5_STATE = 64
FFN_DIM = ((8 * D_MODEL // 3 + 255) // 256) * 256
N_EVEN = (DEPTH + 1) // 2
N_ODD = DEPTH // 2
GM_W = GM_HEADS * HEAD_DIM
NSA_W = NSA_HEADS * HEAD_DIM
KV_W = NSA_KV * HEAD_DIM
EVEN_SPLITS = (GM_W, GM_W, NSA_W, KV_W, KV_W, KV_W, KV_W, KV_W, KV_W, 3 * NSA_HEADS)
EVEN_IN = sum(EVEN_SPLITS)
EVEN_OUT = GM_W + NSA_W
ODD_IN = POOL_WIDTH + S5_WIDTH
ODD_OUT = POOL_WIDTH + S5_WIDTH
RMS_EPS = 1e-6
LN_EPS = 1e-5
NEG_INF = -1e30
SEL_FORCE = 1e4

kernel_name = 'hybrid_gmlp_nsa_pool_s5_macaron'


def rmsnorm(x, g):
    xf = x.astype(jnp.float32)
    y = xf * lax.rsqrt(jnp.mean(xf * xf, axis=-1, keepdims=True) + RMS_EPS)
    return (y * g.astype(jnp.float32)).astype(x.dtype)


def swiglu(h, w_gate, w_up, w_down):
    return (jax.nn.silu(h @ w_gate) * (h @ w_up)) @ w_down


def rope(x, pos):
    half = HEAD_DIM // 2
    inv = ROPE_THETA ** (-jnp.arange(half, dtype=jnp.float32) / half)
    ang = pos[:, None] * inv[None, :]
    cos, sin = jnp.cos(ang)[:, None, :], jnp.sin(ang)[:, None, :]
    xf = x.astype(jnp.float32)
    x1, x2 = xf[..., :half], xf[..., half:]
    return jnp.concatenate([x1 * cos - x2 * sin, x2 * cos + x1 * sin], axis=-1).astype(x.dtype)


def gmlp_spatial_gate(u, v, w_s, b):
    B, S = u.shape[0], u.shape[1]
    u = jax.nn.gelu(u)
    v = jax.nn.gelu(v).astype(jnp.float32)
    mu = jnp.mean(v, axis=-1, keepdims=True)
    var = jnp.mean(jnp.square(v - mu), axis=-1, keepdims=True)
    v = ((v - mu) * lax.rsqrt(var + LN_EPS)).astype(u.dtype)
    causal = jnp.tril(jnp.ones((GM_CHUNK, GM_CHUNK), dtype=w_s.dtype))
    vc = v.reshape(B, S // GM_CHUNK, GM_CHUNK, GM_HEADS, HEAD_DIM)
    s = jnp.einsum('hts,bcshd->bcthd', w_s * causal, vc) + b.T[None, None, :, :, None]
    return u * s.reshape(B, S, GM_HEADS, HEAD_DIM)


def nsa_compress(t, pe, w1, w2, idx):
    B = t.shape[0]
    nc = idx.shape[0]
    blocks = t[:, idx] + pe[:, None, :]
    flat = blocks.transpose(0, 1, 3, 2, 4).reshape(B, nc, NSA_KV, CMP_LEN * HEAD_DIM)
    return jax.nn.gelu(flat @ w1) @ w2


def nsa_attention(q, k_c, v_c, k_s, v_s, k_w, v_w, gate_logits, pe, w1, w2):
    B, S = q.shape[0], q.shape[1]
    G, HPG = NSA_KV, NSA_HPG
    scale = HEAD_DIM ** -0.5
    t = jnp.arange(S, dtype=jnp.int32)
    qg = q.reshape(B, S, G, HPG, HEAD_DIM)

    nc = (S - CMP_LEN) // CMP_STRIDE + 1
    cmp_start = jnp.arange(nc, dtype=jnp.int32) * CMP_STRIDE
    cidx = cmp_start[:, None] + jnp.arange(CMP_LEN, dtype=jnp.int32)[None, :]
    kc = nsa_compress(k_c, pe[0], w1[0], w2[0], cidx)
    vc = nsa_compress(v_c, pe[1], w1[1], w2[1], cidx)
    end_pos = cmp_start + (CMP_LEN - 1)
    kc = rope(kc, end_pos.astype(jnp.float32))
    s_c = jnp.einsum('btgqd,bngd->bgqtn', qg, kc).astype(jnp.float32) * scale
    valid_c = end_pos[None, :] <= t[:, None]
    s_c = jnp.where(valid_c, s_c, NEG_INF)
    p_c = jax.nn.softmax(s_c, axis=-1) * jnp.any(valid_c, axis=-1)[:, None].astype(jnp.float32)
    o_c = jnp.einsum('bgqtn,bngd->btgqd', p_c.astype(vc.dtype), vc)

    nsel = S // SEL_BLOCK
    blk_start = jnp.arange(nsel, dtype=jnp.int32) * SEL_BLOCK
    overlap = ((cmp_start[:, None] < blk_start[None, :] + SEL_BLOCK)
               & (cmp_start[:, None] + CMP_LEN > blk_start[None, :])).astype(jnp.float32)
    imp = jnp.einsum('bgqtn,nj->btgj', p_c, overlap)
    cur = t // SEL_BLOCK
    j = jnp.arange(nsel, dtype=jnp.int32)
    forced = (j[None, :] == 0) | (j[None, :] == cur[:, None]) | (j[None, :] == cur[:, None] - 1)
    causal_blk = blk_start[None, :] <= t[:, None]
    imp = jnp.where(forced[:, None, :], SEL_FORCE,
                    jnp.where(causal_blk[:, None, :], imp, -SEL_FORCE))
    n_top = min(SEL_TOPN, nsel)
    _, sel_idx = lax.top_k(imp, n_top)

    kb = k_s.reshape(B, nsel, SEL_BLOCK, G, HEAD_DIM).transpose(0, 3, 1, 2, 4)
    vb = v_s.reshape(B, nsel, SEL_BLOCK, G, HEAD_DIM).transpose(0, 3, 1, 2, 4)
    nq = S // Q_BLOCK
    q_chunks = qg.reshape(B, nq, Q_BLOCK, G, HPG, HEAD_DIM).transpose(1, 0, 2, 3, 4, 5)
    i_chunks = sel_idx.reshape(B, nq, Q_BLOCK, G, n_top).transpose(1, 0, 2, 3, 4)
    t_chunks = t.reshape(nq, Q_BLOCK)
    bi = jnp.arange(B)[:, None, None, None]
    gi = jnp.arange(G)[None, None, :, None]
    in_blk = jnp.arange(SEL_BLOCK, dtype=jnp.int32)

    def selected_block(args):
        q_i, idx_i, t_i = args
        ks = kb[bi, gi, idx_i]
        vs = vb[bi, gi, idx_i]
        kpos = idx_i[..., None] * SEL_BLOCK + in_blk
        s = jnp.einsum('btgqd,btgnld->btgqnl', q_i, ks).astype(jnp.float32) * scale
        mask = kpos <= t_i[None, :, None, None, None]
        s = jnp.where(mask[:, :, :, None], s, NEG_INF)
        s = s.reshape(B, Q_BLOCK, G, HPG, n_top * SEL_BLOCK)
        p = jax.nn.softmax(s, axis=-1).astype(vs.dtype)
        return jnp.einsum('btgqk,btgkd->btgqd', p, vs.reshape(B, Q_BLOCK, G, n_top * SEL_BLOCK, HEAD_DIM))

    o_s = lax.map(selected_block, (q_chunks, i_chunks, t_chunks))
    o_s = o_s.transpose(1, 0, 2, 3, 4, 5).reshape(B, S, G, HPG, HEAD_DIM)

    span = WINDOW + Q_BLOCK
    kp = jnp.pad(k_w, ((0, 0), (WINDOW, 0), (0, 0), (0, 0)))
    vp = jnp.pad(v_w, ((0, 0), (WINDOW, 0), (0, 0), (0, 0)))
    widx = jnp.arange(nq, dtype=jnp.int32)[:, None] * Q_BLOCK + jnp.arange(span, dtype=jnp.int32)[None, :]
    kw = kp[:, widx]
    vw = vp[:, widx]
    kpos = widx - WINDOW
    qw = qg.reshape(B, nq, Q_BLOCK, G, HPG, HEAD_DIM)
    s_w = jnp.einsum('bitgqd,bikgd->bigqtk', qw, kw).astype(jnp.float32) * scale
    diff = t_chunks[:, :, None] - kpos[:, None, :]
    wmask = (kpos[:, None, :] >= 0) & (diff >= 0) & (diff < WINDOW)
    s_w = jnp.where(wmask[None, :, None, None], s_w, NEG_INF)
    p_w = jax.nn.softmax(s_w, axis=-1).astype(vw.dtype)
    o_w = jnp.einsum('bigqtk,bikgd->bitgqd', p_w, vw).reshape(B, S, G, HPG, HEAD_DIM)

    g = jax.nn.sigmoid(gate_logits.astype(jnp.float32)).reshape(B, S, G, HPG, 3)
    o = g[..., 0:1] * o_c + g[..., 1:2] * o_s + g[..., 2:3] * o_w
    return o.reshape(B, S, NSA_W).astype(q.dtype)


def even_mixer(h, w_in, w_out, gm_w_s, gm_b, cmp_pe, cmp_w1, cmp_w2):
    B, S = h.shape[0], h.shape[1]
    z = h @ w_in
    cuts = list(np.cumsum(EVEN_SPLITS)[:-1])
    u, v, q, kc, vc, ks, vs, kw, vw, gl = jnp.split(z, cuts, axis=-1)
    heads = lambda a, n: a.reshape(B, S, n, HEAD_DIM)
    pos = jnp.arange(S, dtype=jnp.float32)
    out_a = gmlp_spatial_gate(heads(u, GM_HEADS), heads(v, GM_HEADS), gm_w_s, gm_b).reshape(B, S, GM_W)
    out_b = nsa_attention(rope(heads(q, NSA_HEADS), pos),
                          heads(kc, NSA_KV), heads(vc, NSA_KV),
                          rope(heads(ks, NSA_KV), pos), heads(vs, NSA_KV),
                          rope(heads(kw, NSA_KV), pos), heads(vw, NSA_KV),
                          gl, cmp_pe, cmp_w1, cmp_w2)
    return jnp.concatenate([out_a, out_b.astype(out_a.dtype)], axis=-1) @ w_out


def odd_mixer(h, w_in, w_out, pool_w, pool_scale, lam_re, lam_im, b_re, b_im, c_re, c_im, d_skip, log_dt, w_glu):
    B, S = h.shape[0], h.shape[1]
    z = h @ w_in
    zc, zd = z[..., :POOL_WIDTH], z[..., POOL_WIDTH:]
    t = jnp.arange(S, dtype=jnp.int32)

    n_grp = len(POOL_WINDOWS)
    zg = zc.reshape(B, S, n_grp, POOL_GROUP).astype(jnp.float32)
    cs = jnp.concatenate([jnp.zeros((B, 1, n_grp, POOL_GROUP), jnp.float32), jnp.cumsum(zg, axis=1)], axis=1)
    win = jnp.array(POOL_WINDOWS, dtype=jnp.int32)
    lo = jnp.maximum(t[:, None] + 1 - win[None, :], 0)
    lower = cs[:, lo, jnp.arange(n_grp)[None, :]]
    cnt = (t[:, None] + 1 - lo).astype(jnp.float32)
    pooled = (cs[:, 1:] - lower) / cnt[..., None] - zg
    y_c = jnp.einsum('bsgc,gcd->bsgd', pooled, pool_w.astype(jnp.float32))
    y_c = (y_c.reshape(B, S, POOL_WIDTH) * pool_scale.astype(jnp.float32)).astype(h.dtype)

    u = zd.reshape(B, S, S5_GROUPS, S5_GROUP_CH).astype(jnp.float32)
    lam = lax.complex(lam_re.astype(jnp.float32), lam_im.astype(jnp.float32))
    dt = jnp.exp(log_dt.astype(jnp.float32))[:, None]
    lam_bar = jnp.exp(lam * dt)
    b_c = lax.complex(b_re.astype(jnp.float32), b_im.astype(jnp.float32))
    c_c = lax.complex(c_re.astype(jnp.float32), c_im.astype(jnp.float32))
    b_bar = ((lam_bar - 1.0) / lam)[..., None] * b_c
    bu = jnp.einsum('gpc,bsgc->sbgp', b_bar, u.astype(jnp.complex64))
    a = jnp.broadcast_to(lam_bar, (S, 1, S5_GROUPS, S5_STATE))

    def combine(e1, e2):
        a1, b1 = e1
        a2, b2 = e2
        return a2 * a1, a2 * b1 + b2

    _, states = lax.associative_scan(combine, (a, bu), axis=0)
    y = jnp.einsum('gcp,sbgp->bsgc', c_c, states).real + d_skip.astype(jnp.float32) * u
    y = jax.nn.gelu(y.reshape(B, S, S5_WIDTH))
    ab = y @ w_glu.astype(jnp.float32)
    y_d = (ab[..., :S5_WIDTH] * jax.nn.sigmoid(ab[..., S5_WIDTH:])).astype(h.dtype)

    return jnp.concatenate([y_c, y_d], axis=-1) @ w_out


def setup_inputs(seed: int = 0) -> dict:
    key = jax.random.key(seed)
    k = jax.random.split(key, 26)
    f32 = jnp.float32
    nrm = lambda kk, shape, sc: jax.random.normal(kk, shape, f32) * sc
    lam_im = jnp.pi * jnp.broadcast_to(jnp.arange(S5_STATE, dtype=f32), (N_ODD, S5_GROUPS, S5_STATE))
    return {
        'x': nrm(k[0], (BATCH, SEQ, D_MODEL), 1.0),
        'norm_w': 1.0 + nrm(k[1], (DEPTH, 3, D_MODEL), 0.01),
        'ffn_w_gate': nrm(k[2], (DEPTH, 2, D_MODEL, FFN_DIM), D_MODEL ** -0.5),
        'ffn_w_up': nrm(k[3], (DEPTH, 2, D_MODEL, FFN_DIM), D_MODEL ** -0.5),
        'ffn_w_down': nrm(k[4], (DEPTH, 2, FFN_DIM, D_MODEL), FFN_DIM ** -0.5),
        'final_norm_w': 1.0 + nrm(k[5], (D_MODEL,), 0.01),
        'ev_w_in': nrm(k[6], (N_EVEN, D_MODEL, EVEN_IN), D_MODEL ** -0.5),
        'ev_w_out': nrm(k[7], (N_EVEN, EVEN_OUT, D_MODEL), EVEN_OUT ** -0.5),
        'gm_w_s': nrm(k[8], (N_EVEN, GM_HEADS, GM_CHUNK, GM_CHUNK), GM_CHUNK ** -0.5),
        'gm_b': 1.0 + nrm(k[9], (N_EVEN, GM_HEADS, GM_CHUNK), 0.01),
        'nsa_cmp_pe': nrm(k[10], (N_EVEN, 2, CMP_LEN, HEAD_DIM), 0.02),
        'nsa_cmp_w1': nrm(k[11], (N_EVEN, 2, CMP_LEN * HEAD_DIM, HEAD_DIM), (CMP_LEN * HEAD_DIM) ** -0.5),
        'nsa_cmp_w2': nrm(k[12], (N_EVEN, 2, HEAD_DIM, HEAD_DIM), HEAD_DIM ** -0.5),
        'od_w_in': nrm(k[13], (N_ODD, D_MODEL, ODD_IN), D_MODEL ** -0.5),
        'od_w_out': nrm(k[14], (N_ODD, ODD_OUT, D_MODEL), ODD_OUT ** -0.5),
        'pool_w': nrm(k[15], (N_ODD, len(POOL_WINDOWS), POOL_GROUP, POOL_GROUP), POOL_GROUP ** -0.5),
        'pool_scale': 1.0 + nrm(k[16], (N_ODD, POOL_WIDTH), 0.02),
        's5_lam_re': -0.5 + nrm(k[17], (N_ODD, S5_GROUPS, S5_STATE), 0.01),
        's5_lam_im': lam_im + nrm(k[18], (N_ODD, S5_GROUPS, S5_STATE), 0.01),
        's5_b_re': nrm(k[19], (N_ODD, S5_GROUPS, S5_STATE, S5_GROUP_CH), (2 * S5_GROUP_CH) ** -0.5),
        's5_b_im': nrm(k[20], (N_ODD, S5_GROUPS, S5_STATE, S5_GROUP_CH), (2 * S5_GROUP_CH) ** -0.5),
        's5_c_re': nrm(k[21], (N_ODD, S5_GROUPS, S5_GROUP_CH, S5_STATE), (2 * S5_STATE) ** -0.5),
        's5_c_im': nrm(k[22], (N_ODD, S5_GROUPS, S5_GROUP_CH, S5_STATE), (2 * S5_STATE) ** -0.5),
        's5_d': nrm(k[23], (N_ODD, S5_GROUPS, S5_GROUP_CH), 1.0),
        's5_log_dt': jax.random.uniform(k[24], (N_ODD, S5_GROUPS), f32, float(np.log(1e-3)), float(np.log(1e-1))),
        's5_w_glu': nrm(k[25], (N_ODD, S5_WIDTH, 2 * S5_WIDTH), S5_WIDTH ** -0.5),
    }


def reference(x, norm_w, ffn_w_gate, ffn_w_up, ffn_w_down, final_norm_w, ev_w_in, ev_w_out,
              gm_w_s, gm_b, nsa_cmp_pe, nsa_cmp_w1, nsa_cmp_w2, od_w_in, od_w_out, pool_w, pool_scale,
              s5_lam_re, s5_lam_im, s5_b_re, s5_b_im, s5_c_re, s5_c_im, s5_d, s5_log_dt, s5_w_glu):
    for l in range(DEPTH):
        x = x + 0.5 * swiglu(rmsnorm(x, norm_w[l, 0]), ffn_w_gate[l, 0], ffn_w_up[l, 0], ffn_w_down[l, 0])
        h = rmsnorm(x, norm_w[l, 1])
        i = l // 2
        if l % 2 == 0:
            m = even_mixer(h, ev_w_in[i], ev_w_out[i], gm_w_s[i], gm_b[i],
                           nsa_cmp_pe[i], nsa_cmp_w1[i], nsa_cmp_w2[i])
        else:
            m = odd_mixer(h, od_w_in[i], od_w_out[i], pool_w[i], pool_scale[i],
                          s5_lam_re[i], s5_lam_im[i], s5_b_re[i], s5_b_im[i], s5_c_re[i], s5_c_im[i],
                          s5_d[i], s5_log_dt[i], s5_w_glu[i])
        x = x + m.astype(x.dtype)
        x = x + 0.5 * swiglu(rmsnorm(x, norm_w[l, 2]), ffn_w_gate[l, 1], ffn_w_up[l, 1], ffn_w_down[l, 1])
    return rmsnorm(x, final_norm_w)
```

```python
import numpy as np
import concourse.bass as bass
import concourse.mybir as mybir
from concourse.bass_utils import run_bass_kernel_spmd

F32 = mybir.dt.float32
BF16 = mybir.dt.bfloat16
AF = mybir.ActivationFunctionType
ALU = mybir.AluOpType
AX = mybir.AxisListType

D = 1024
S = 2048
NT = S // 128
FF = 2816
NFC = FF // 128
DEPTH = 4
RMS_EPS = 1e-6


class Res:
    __slots__ = ("name", "w", "r", "excl")

    def __init__(self, name="", excl=False):
        self.name = name
        self.w = None
        self.r = {}
        self.excl = excl


class Sched:
    ENG = ("pe", "act", "dve", "pool", "sp")

    def __init__(self, nc, dma_keys=()):
        self.nc = nc
        self.dkeys = []
        self.e = dict(pe=nc.tensor, act=nc.scalar, dve=nc.vector, pool=nc.gpsimd, sp=nc.sync)
        self.sem = {}
        self.cnt = {}
        for k in self.ENG + tuple(dma_keys):
            self.sem[k] = nc.alloc_semaphore("s_" + k)
            self.cnt[k] = 0
        self.seen = {k: {} for k in self.ENG}
        self.nins = 0

    def _wait(self, eng, key, val):
        if val <= self.seen[eng].get(key, 0):
            return
        self.e[eng].wait_ge(self.sem[key], val)
        self.seen[eng][key] = val
        self.nins += 1

    def _deps(self, eng, reads, writes, async_=False):
        need = {}

        def add(t):
            if t is None:
                return
            k, v = t
            if need.get(k, 0) < v:
                need[k] = v

        for r in reads:
            add(r.w)
            if r.excl:
                for k, v in r.r.items():
                    if k != eng:
                        add((k, v))
        for w in writes:
            if async_ or (w.w is not None and w.w[0] != eng):
                add(w.w)
            for k, v in w.r.items():
                if async_ or k != eng:
                    add((k, v))
        for k, v in need.items():
            if k == eng and eng == "pe":
                continue
            self._wait(eng, k, v)

    def op(self, eng, fn, reads=(), writes=(), count=True):
        self._deps(eng, reads, writes)
        ins = fn(self.e[eng])
        self.nins += 1
        if count:
            self.cnt[eng] += 1
            ins.then_inc(self.sem[eng], 1)
            v = self.cnt[eng]
        else:
            v = self.cnt[eng] + 1
        for r in reads:
            if r.r.get(eng, 0) < v:
                r.r[eng] = v
        for w in writes:
            w.w = (eng, v)
            w.r = {}
        return ins

    def dma(self, q, dkey, out, in_, reads=(), writes=()):
        res0 = writes[0] if len(writes) > 0 else reads[0]
        dkey = "d_" + res0.name
        if dkey not in self.sem:
            self.sem[dkey] = self.nc.alloc_semaphore("s_" + dkey)
            self.cnt[dkey] = 0
            self.dkeys.append(dkey)
        self._deps(q, reads, writes, async_=True)
        ins = self.e[q].dma_start(out=out, in_=in_)
        self.nins += 1
        self.cnt[dkey] += 16
        ins.then_inc(self.sem[dkey], 16)
        v = self.cnt[dkey]
        for r in reads:
            if r.r.get(dkey, 0) < v:
                r.r[dkey] = v
        for w in writes:
            w.w = (dkey, v)
            w.r = {}
        return ins

    def barrier(self):
        snap = dict(self.cnt)
        for eng in self.ENG:
            for k, v in snap.items():
                if v > 0 and k != eng:
                    self._wait(eng, k, v)
            if eng != "pe" and snap[eng] > 0:
                self._wait(eng, eng, snap[eng])

    def finish(self, keys=None):
        for k in self.dkeys:
            if self.cnt[k] > 0:
                self._wait("sp", k, self.cnt[k])


class Ctx:
    pass


ARENA = 140 * 1024


def aview(C, off, shape, dtype):
    esz = 4 if dtype == F32 else 2
    n = 1
    for d in shape[1:]:
        n *= d
    nb = n * esz
    assert off % 4 == 0 and nb % 4 == 0 and off + nb <= ARENA, (off, shape)
    v = C.AR[0:shape[0], off // 4:(off + nb) // 4]
    if dtype != F32:
        v = v.bitcast(dtype)
    if len(shape) == 2:
        return v
    names = ["a", "b", "c", "d"][:len(shape) - 1]
    pat = "p (" + " ".join(names) + ") -> p " + " ".join(names)
    return v.rearrange(pat, **{nm: d for nm, d in zip(names[1:], shape[2:])})


def build(stages, n_layers_inputs=True):
    nc = bass.Bass("TRN2", target_bir_lowering=False)
    C = Ctx()
    C.nc = nc
    sc = Sched(nc)
    C.sc = sc

    def din(name, shape):
        return nc.dram_tensor(name, list(shape), F32, kind="ExternalInput").ap()

    C.x_in = din("x", [S, D])
    C.gT_in = din("gT_h", [128, 13 * 8])
    C.fin_g_in = din("fin_g_h", [128, D])
    C.ident_in = din("ident_h", [128, 128])
    C.wg = din("ffn_w_gate", [DEPTH, 2, D, FF])
    C.wu = din("ffn_w_up", [DEPTH, 2, D, FF])
    C.wd = din("ffn_w_down", [DEPTH, 2, FF, D])
    C.y_out = nc.dram_tensor("y", [S, D], F32, kind="ExternalOutput").ap()
    C.ev_w_in = din("ev_w_in", [2, D, 2328])
    C.ev_w_out = din("ev_w_out", [2, D, D])
    C.nsa_w1 = din("nsa_cmp_w1", [2, 2, 2048, 64])
    C.nsa_w2 = din("nsa_cmp_w2", [2, 2, 64, 64])
    C.nsa_pe_in = din("nsa_pe_h", [2, 128, 2, 32])
    C.rope_in = din("rope_h", [2, 128, NT, 32])
    C.ropec_in = din("ropec_h", [2, 128, 32])
    C.gm_ws_in = din("gm_ws_h", [2, 128, 8, 128])
    C.gm_b_in = din("gm_b_h", [2, 128, 8])
    C.fsel_in = din("fsel_h", [2, 128, NT, 32])
    C.eb_in = din("eb_h", [100, S])
    C.trib_in = din("trib_h", [128, 512])
    C.bandb_in = din("bandb_h", [128, 512])
    C.bm_in = din("bm_h", [100, 512])
    C.mpat_in = din("mpat_h", [73, 512])
    C.bwin_in = din("bwin_h", [73, 256])
    C.ovl_in = din("ovl_h", [127, 32])
    C.od_w_in = din("od_w_in", [2, D, D])
    C.od_w_out = din("od_w_out", [2, D, D])
    C.pool_w = din("pool_w", [2, 4, 128, 128])
    C.s5_w_glu = din("s5_w_glu", [2, 512, 1024])
    C.psc_in = din("psc_h", [128, 8])
    C.s5lam_in = din("s5lam_h", [128, 3, 32])
    C.s5d_in = din("s5d_h", [128, 8])
    C.inv16_in = din("inv16_h", [128, 16])
    C.triT_in = din("triT_h", [128, 128])
    C.s5b = din("s5b_h", [2, 4, 128, 16, 16])

    C.X = nc.alloc_sbuf_tensor("X", [128, NT, D], F32)
    C.xr = [Res(f"x{t}") for t in range(NT)]
    C.identf = nc.alloc_sbuf_tensor("identf", [128, 128], F32)
    C.identb = nc.alloc_sbuf_tensor("identb", [128, 128], BF16)
    C.gT = nc.alloc_sbuf_tensor("gT", [128, 13 * 8], F32)
    C.cm05 = nc.alloc_sbuf_tensor("cm05", [128, 1], F32)
    C.stat = nc.alloc_sbuf_tensor("stat", [128, 64], F32)
    C.r_const = Res("const")
    C.PS = nc.alloc_psum_tensor("PS", [128, 8, 512], F32)
    C.pr = [Res(f"ps{b}", excl=True) for b in range(8)]

    C.AR = nc.alloc_sbuf_tensor("AR", [128, ARENA // 4], F32)
    C.hT = aview(C, 0, [128, 8, 1024], BF16)
    C.r_hT = [Res(f"hT{t}") for t in range(8)]
    C.hT2 = aview(C, 116736, [128, 8, 1024], BF16)
    C.r_hT2 = [Res(f"hTb{t}") for t in range(8)]
    C.aT = aview(C, 16384, [128, NFC, 1024], BF16)
    C.r_aT = [[Res(f"aT{f}_{b}") for b in range(2)] for f in range(NFC)]
    C.NWS = 2
    C.wgb = [aview(C, 61440 + 8192 * i, [128, 8, 512], BF16) for i in range(C.NWS)]
    C.wub = [aview(C, 77824 + 8192 * i, [128, 8, 512], BF16) for i in range(C.NWS)]
    C.r_wg = [Res(f"wg{i}") for i in range(C.NWS)]
    C.r_wu = [Res(f"wu{i}") for i in range(C.NWS)]
    C.NWD = 4
    C.wdb = [aview(C, 94208 + 2048 * i, [128, 2, 512], BF16) for i in range(C.NWD)]
    C.r_wd = [Res(f"wd{i}") for i in range(C.NWD)]
    C.xs = [aview(C, 102400 + 4096 * i, [128, D], F32) for i in range(2)]
    C.r_xs = [Res(f"xs{i}") for i in range(2)]
    C.sg = [aview(C, 110592 + 2048 * i, [128, 512], F32) for i in range(2)]
    C.r_sg = [Res(f"sg{i}") for i in range(2)]
    C.junk = aview(C, 114688, [128, D], BF16)
    C.r_junk = Res("junk")
    C.ffn_xs, C.ffn_junk = C.xs, C.junk
    C.psc = nc.alloc_sbuf_tensor("psc", [128, 8], F32)
    C.s5lam = nc.alloc_sbuf_tensor("s5lam", [128, 3, 32], F32)
    C.s5d = nc.alloc_sbuf_tensor("s5d", [128, 8], F32)
    C.inv16 = nc.alloc_sbuf_tensor("inv16", [128, 16], F32)
    C.triT = nc.alloc_sbuf_tensor("triT", [128, 128], BF16)
    C.halfpi = nc.alloc_sbuf_tensor("halfpi", [128, 1], F32)
    C.c05 = nc.alloc_sbuf_tensor("c05", [128, 1], F32)
    C.r_stat = [Res(f"stat{i}") for i in range(4)]
    C.ctr = dict(wgu=0, wd=0, xs=0, sg=0, st=0, tp=0, gu=0, ev=0)
    C.evt = [aview(C, 133120 + 2048 * k, [128, 512], F32) for k in range(2)]
    C.r_evt = [Res(f"evt{k}") for k in range(2)]

    r_c1, r_c2 = Res("c_ident"), Res("c_gT")
    sc.dma("sp", "dio", C.identf[:], C.ident_in, writes=[r_c1])
    sc.dma("sp", "dio", C.gT[:], C.gT_in, writes=[r_c2])
    r_cs = [Res(f"c_s{k}") for k in range(5)]
    sc.dma("sp", "dio", C.psc[:], C.psc_in, writes=[r_cs[0]])
    sc.dma("sp", "dio", C.s5lam[:], C.s5lam_in, writes=[r_cs[1]])
    sc.dma("sp", "dio", C.s5d[:], C.s5d_in, writes=[r_cs[2]])
    sc.dma("sp", "dio", C.inv16[:], C.inv16_in, writes=[r_cs[3]])
    sc.dma("pool", "dio", C.triT[:], C.triT_in, writes=[r_cs[4]])
    sc.op("dve", lambda e: e.memset(C.halfpi[:], float(np.pi / 2)), reads=r_cs, writes=[C.r_const])
    sc.op("dve", lambda e: e.memset(C.c05[:], 0.5), writes=[C.r_const])
    sc.op("dve", lambda e: e.memset(C.cm05[:], -0.5), reads=[r_c1, r_c2], writes=[C.r_const])
    sc.op("dve", lambda e: e.tensor_copy(out=C.identb[:], in_=C.identf[:]), reads=[C.r_const], writes=[C.r_const])
    for t in range(NT):
        q = "sp" if t % 2 == 0 else "act"
        sc.dma(q, "dx", C.X[:, t, :], C.x_in[t * 128:(t + 1) * 128, :], writes=[C.xr[t]])

    prev_kind = None
    for st in stages:
        if not (prev_kind == "ffn" and st[0] in ("ffn", "final")):
            sc.barrier()
        prev_kind = st[0]
        C.xs, C.junk = C.ffn_xs, C.ffn_junk
        if st[0] == "ffn":
            ffn_block(C, st[1], st[2])
        elif st[0] == "mix" and st[1] % 2 == 1:
            odd_mixer(C, st[1])
        elif st[0] == "mix":
            even_mixer(C, st[1])
        elif st[0] == "final":
            final_norm(C)
        else:
            raise ValueError(st)

    for t in range(NT):
        q = "sp" if t % 2 == 0 else "act"
        sc.dma(q, "dx", C.y_out[t * 128:(t + 1) * 128, :], C.X[:, t, :], reads=[C.xr[t]])
    sc.finish()
    return nc, C


def rstd_of_tile(C, t):
    sc = C.sc
    i = C.ctr["st"] % 4
    C.ctr["st"] += 1
    r = C.r_stat[i]
    ss = C.stat[:, 2 * i:2 * i + 1]
    rs = C.stat[:, 2 * i + 1:2 * i + 2]
    sc.op("act", lambda e: e.activation(out=C.junk[:], in_=C.X[:, t, :], func=AF.Square, accum_out=ss),
          reads=[C.xr[t], C.r_junk], writes=[C.r_junk, r])
    sc.op("dve", lambda e: e.tensor_scalar(out=ss, in0=ss, scalar1=1.0 / D, scalar2=RMS_EPS, op0=ALU.mult, op1=ALU.add),
          reads=[r], writes=[r])
    sc.op("pool", lambda e: e.tensor_tensor(out=rs, in0=ss, in1=C.cm05[:], op=ALU.pow),
          reads=[r, C.r_const], writes=[r])
    return rs, r


def norm_transpose(C, t, gcol, dst, r_dst):
    st = norm_part1(C, t)
    norm_part2(C, st, gcol, dst, r_dst)


def norm_part1(C, t):
    sc = C.sc
    rs, r = rstd_of_tile(C, t)
    i = C.ctr["xs"] % 2
    C.ctr["xs"] += 1
    xs, rxs = C.xs[i], C.r_xs[i]
    sc.op("act", lambda e: e.activation(out=xs, in_=C.X[:, t, :], func=AF.Copy, scale=rs),
          reads=[C.xr[t], r], writes=[rxs])
    return xs, rxs


def norm_part2(C, st, gcol, dst, r_dst):
    sc = C.sc
    xs, rxs = st
    j = C.ctr["tp"] % 2
    C.ctr["tp"] += 1
    b0 = 4 + 2 * j
    pst = C.PS[:, b0:b0 + 2, :].rearrange("p b (k t) -> p (b k) t", t=128)
    for kc in range(8):
        sc.op("pe", lambda e, kc=kc: e.transpose(out=pst[:, kc, :], in_=xs[:, kc * 128:(kc + 1) * 128], identity=C.identf[:]),
              reads=[rxs, C.r_const], writes=[C.pr[b0], C.pr[b0 + 1]], count=(kc == 7))
    sc.op("dve", lambda e: e.tensor_tensor(out=dst, in0=pst, in1=gcol.unsqueeze(2).to_broadcast([128, 8, 128]), op=ALU.mult),
          reads=[C.pr[b0], C.pr[b0 + 1], C.r_const], writes=[r_dst])


def ffn_block(C, l, j):
    sc = C.sc
    nc = C.nc
    ni = l * 3 + (0 if j == 0 else 2)
    gcol = C.gT[:, ni * 8:(ni + 1) * 8]
    wgv = C.wg[l, j].rearrange("(kc p) f -> p kc f", p=128)
    wuv = C.wu[l, j].rearrange("(kc p) f -> p kc f", p=128)
    wdv = C.wd[l, j].rearrange("(fc p) d -> p fc d", p=128)
    groups = [(g * 512, 512) for g in range(5)] + [(2560, 256)]
    hTs = [C.hT, C.hT2]
    r_hTs = [C.r_hT, C.r_hT2]

    def phase_a(tg):
        hT_, r_hT_ = hTs[tg % 2], r_hTs[tg % 2]
        for tt in range(8):
            st = norm_part1(C, tg * 8 + tt)
            yield
            norm_part2(C, st, gcol, hT_[:, :, tt * 128:(tt + 1) * 128], r_hT_[tt])
            yield

    def pass1(tg):
        hT_, r_hT_ = hTs[tg % 2], r_hTs[tg % 2]
        for (f0, fw) in groups:
            s = C.ctr["wgu"] % C.NWS
            C.ctr["wgu"] += 1
            sc.dma("pool", "dw", C.wgb[s][:, :, 0:fw], wgv[:, :, f0:f0 + fw], writes=[C.r_wg[s]])
            sc.dma("pool", "dw", C.wub[s][:, :, 0:fw], wuv[:, :, f0:f0 + fw], writes=[C.r_wu[s]])
            for tb in range(2):
                for fci in range(fw // 128):
                    fc = f0 // 128 + fci
                    gi = C.ctr["gu"] % 2
                    C.ctr["gu"] += 1
                    bG, bU = gi, 2 + gi
                    hr = r_hT_[tb * 4:(tb + 1) * 4]
                    for kc in range(8):
                        sc.op("pe", lambda e, kc=kc: e.matmul(C.PS[:, bG, :], lhsT=C.wgb[s][:, kc, fci * 128:(fci + 1) * 128],
                                                            rhs=hT_[:, kc, tb * 512:(tb + 1) * 512], start=(kc == 0), stop=(kc == 7)),
                              reads=[C.r_wg[s]] + hr, writes=[C.pr[bG]], count=(kc == 7))
                    for kc in range(8):
                        sc.op("pe", lambda e, kc=kc: e.matmul(C.PS[:, bU, :], lhsT=C.wub[s][:, kc, fci * 128:(fci + 1) * 128],
                                                            rhs=hT_[:, kc, tb * 512:(tb + 1) * 512], start=(kc == 0), stop=(kc == 7)),
                              reads=[C.r_wu[s]] + hr, writes=[C.pr[bU]], count=(kc == 7))
                    si = C.ctr["sg"] % 2
                    C.ctr["sg"] += 1
                    sc.op("act", lambda e: e.activation(out=C.sg[si], in_=C.PS[:, bG, :], func=AF.Silu),
                          reads=[C.pr[bG]], writes=[C.r_sg[si]])
                    sc.op("dve", lambda e: e.tensor_tensor(out=C.aT[:, fc, tb * 512:(tb + 1) * 512], in0=C.sg[si], in1=C.PS[:, bU, :], op=ALU.mult),
                          reads=[C.r_sg[si], C.pr[bU]], writes=[C.r_aT[fc][tb]])
                    yield

    side0 = phase_a(0)
    for _ in range(8):
        next(side0)
    for tg in range(2):
        main = pass1(tg)
        side = phase_a(tg + 1) if tg + 1 < 2 else iter(())
        nmain = 0
        for _ in main:
            nmain += 1
            if tg == 0 and nmain <= 4:
                next(side0, None)
                next(side0, None)
            elif nmain % 5 in (0, 3):
                next(side, None)
        for _ in side:
            pass
        for db in range(2):
            for fp in range(NFC // 2):
                s = C.ctr["wd"] % C.NWD
                C.ctr["wd"] += 1
                sc.dma("pool", "dw", C.wdb[s], wdv[:, 2 * fp:2 * fp + 2, db * 512:(db + 1) * 512], writes=[C.r_wd[s]])
                for fi in range(2):
                    fc = 2 * fp + fi
                    for tt in range(8):
                        sc.op("pe", lambda e, fi=fi, tt=tt, fc=fc: e.matmul(C.PS[:, tt, :], lhsT=C.aT[:, fc, tt * 128:(tt + 1) * 128], rhs=C.wdb[s][:, fi, :],
                                                                          start=(fc == 0), stop=(fc == NFC - 1)),
                              reads=[C.r_wd[s], C.r_aT[fc][tt // 4]], writes=[C.pr[tt]], count=(fc == NFC - 1 or (fi == 1 and tt == 7)))
            for tt in range(8):
                t = tg * 8 + tt
                xv = C.X[:, t, db * 512:(db + 1) * 512]
                if tt % 2 == 1:
                    k = C.ctr["ev"] % 2
                    C.ctr["ev"] += 1
                    tmp, rt = C.evt[k], C.r_evt[k]
                    sc.op("act", lambda e, tt=tt, tmp=tmp: e.activation(out=tmp, in_=C.PS[:, tt, :], func=AF.Copy, scale=0.5),
                          reads=[C.pr[tt]], writes=[rt])
                    sc.op("pool", lambda e, xv=xv, tmp=tmp: e.tensor_tensor(out=xv, in0=xv, in1=tmp, op=ALU.add),
                          reads=[rt, C.xr[t]], writes=[C.xr[t]])
                else:
                    sc.op("dve", lambda e, tt=tt, xv=xv: e.scalar_tensor_tensor(out=xv, in0=C.PS[:, tt, :], scalar=0.5, in1=xv, op0=ALU.mult, op1=ALU.add),
                          reads=[C.pr[tt], C.xr[t]], writes=[C.xr[t]])


def final_norm(C):
    sc = C.sc
    fg = C.xs[0]
    sc.dma("sp", "dio", fg, C.fin_g_in, writes=[C.r_xs[0]])
    for t in range(NT):
        rs, r = rstd_of_tile(C, t)
        sc.op("dve", lambda e, t=t, rs=rs: e.scalar_tensor_tensor(out=C.X[:, t, :], in0=C.X[:, t, :], scalar=rs, in1=fg, op0=ALU.mult, op1=ALU.mult),
              reads=[C.xr[t], r, C.r_xs[0]], writes=[C.xr[t]])


def host_consts(inputs):
    nw = np.concatenate([inputs["norm_w"].reshape(12, D), inputs["final_norm_w"].reshape(1, D)], axis=0)
    gT = np.ascontiguousarray(nw.reshape(13, 8, 128).transpose(2, 0, 1).reshape(128, 13 * 8)).astype(np.float32)
    fin_g = np.ascontiguousarray(np.broadcast_to(inputs["final_norm_w"].reshape(1, D), (128, D))).astype(np.float32)
    f = lambda a: np.ascontiguousarray(a, dtype=np.float32)
    out = dict(gT_h=gT, fin_g_h=fin_g, ident_h=np.eye(128, dtype=np.float32))
    out["psc_h"] = f(inputs["pool_scale"].reshape(2, 4, 128).transpose(2, 0, 1).reshape(128, 8))
    def st_major(a):
        return a.reshape(2, 16, 2, 64).transpose(2, 3, 0, 1).reshape(128, 32)
    ldt = np.broadcast_to(inputs["s5_log_dt"][:, :, None], (2, 32, 64))
    out["s5lam_h"] = f(np.stack([st_major(inputs["s5_lam_re"]), st_major(inputs["s5_lam_im"]), st_major(ldt)], axis=1))
    def st_major_b(a):
        return a.reshape(2, 16, 2, 64, 16).transpose(0, 2, 3, 1, 4).reshape(2, 128, 16, 16)
    def st_major_c(a):
        return a.reshape(2, 16, 2, 16, 64).transpose(0, 2, 4, 1, 3).reshape(2, 128, 16, 16)
    out["s5b_h"] = f(np.stack([st_major_b(inputs["s5_b_re"]), st_major_b(inputs["s5_b_im"]),
                               st_major_c(inputs["s5_c_re"]), st_major_c(inputs["s5_c_im"])], axis=1))
    out["s5d_h"] = f(inputs["s5_d"].reshape(2, 4, 8, 16).transpose(2, 3, 0, 1).reshape(128, 8))
    out["inv16_h"] = f(np.broadcast_to(1.0 / np.arange(1, 17, dtype=np.float64)[None, :], (128, 16)))
    out["triT_h"] = f(np.triu(np.ones((128, 128))))
    out["gm_ws_h"] = f(inputs["gm_w_s"].transpose(0, 3, 1, 2))
    out["gm_b_h"] = f(inputs["gm_b"].transpose(0, 2, 1))
    pe = inputs["nsa_cmp_pe"]
    peT = pe.transpose(0, 3, 1, 2)
    out["nsa_pe_h"] = f(np.concatenate([peT, peT], axis=1))
    inv = 10000.0 ** (-np.arange(32, dtype=np.float64) / 32)
    pos = (np.arange(NT)[None, :] * 128 + np.arange(128)[:, None]).astype(np.float64)
    ang = (pos[:, :, None].astype(np.float32) * inv.astype(np.float32)[None, None, :]).astype(np.float32)
    out["rope_h"] = f(np.stack([np.cos(ang), np.sin(ang)], axis=0))
    posc = (np.arange(128) * 16 + 31).astype(np.float32)
    angc = (posc[:, None] * inv.astype(np.float32)[None, :]).astype(np.float32)
    out["ropec_h"] = f(np.stack([np.cos(angc), np.sin(angc)], axis=0))
    tl = np.arange(128)[:, None, None]
    ti = np.arange(NT)[None, :, None]
    jj = np.arange(32)[None, None, :]
    cur = (ti * 128 + tl) // 64
    forced = (jj == 0) | (jj == cur) | (jj == cur - 1)
    out["fsel_h"] = f(np.stack([np.where(forced, 1e4, 0.0), np.where(jj > cur, -1.0, 0.0)], axis=0))
    eb = np.zeros((36, S), np.float32)
    eb[np.arange(S) // 64, np.arange(S)] = 1.0
    eb[32:36, :] = 1.0
    eb2 = np.zeros((100, S), np.float32)
    eb2[0:36] = eb
    eb2[64:100] = eb
    out["eb_h"] = eb2
    sk = np.arange(128)[:, None]
    tq = np.arange(128)[None, :]
    tri = np.where(sk <= tq, 0.0, NEG).astype(np.float32)
    band = np.where(sk > tq, 0.0, NEG).astype(np.float32)
    out["trib_h"] = f(np.tile(tri, (1, 4)))
    out["bandb_h"] = f(np.tile(band, (1, 4)))
    bm = np.ones((36, 4, 128), np.float32)
    bm[32:36] = np.eye(4, dtype=np.float32)[:, :, None]
    bm2 = np.zeros((100, 512), np.float32)
    bm2[0:36] = bm.reshape(36, 512)
    bm2[64:100] = bm.reshape(36, 512)
    out["bm_h"] = bm2
    fq = np.floor((np.arange(128) - 31) / 16.0)
    mp = np.zeros((9, 128), np.float32)
    for r in range(8):
        mp[r] = np.where((r - 1) <= fq, 0.0, NEG)
    mp[8] = NEG
    mp2 = np.zeros((73, 512), np.float32)
    mp2[0:9] = np.tile(mp, (1, 4))
    mp2[64:73] = np.tile(mp, (1, 4))
    out["mpat_h"] = mp2
    bw = np.zeros((9, 256), np.float32)
    for c in range(256):
        cp = c - 127
        r = cp + 1
        if 0 <= r < 8:
            bw[r, c] = 1.0
        if cp >= 7:
            bw[8, c] = 1.0
    bw2 = np.zeros((73, 256), np.float32)
    bw2[0:9] = bw
    bw2[64:73] = bw
    out["bwin_h"] = bw2
    nn = np.arange(127)[:, None] * 16
    jb = np.arange(32)[None, :] * 64
    out["ovl_h"] = f(((nn < jb + 64) & (nn + 32 > jb)).astype(np.float32))
    return out


ALL_STAGES = []
for _l in range(DEPTH):
    ALL_STAGES += [("ffn", _l, 0), ("mix", _l), ("ffn", _l, 1)]
ALL_STAGES += [("final",)]


def host_inputs(inputs):
    shared = host_consts(inputs)
    for k in ("od_w_in", "od_w_out", "pool_w", "s5_w_glu", "ev_w_in", "ev_w_out", "nsa_cmp_w1", "nsa_cmp_w2"):
        shared[k] = np.asarray(inputs[k], np.float32)
    shared.update(ffn_w_gate=np.asarray(inputs["ffn_w_gate"], np.float32),
                  ffn_w_up=np.asarray(inputs["ffn_w_up"], np.float32),
                  ffn_w_down=np.asarray(inputs["ffn_w_down"], np.float32))
    return shared


def run(inputs, stages, core_ids, xs_per_core, trace=False):
    nc, C = build(stages)
    shared = host_inputs(inputs)
    in_maps = []
    for xc in xs_per_core:
        m = dict(shared)
        m["x"] = np.ascontiguousarray(xc, dtype=np.float32)
        in_maps.append(m)
    res = run_bass_kernel_spmd(nc, in_maps, core_ids=core_ids, trace=trace)
    return [r["y"] for r in res.results], res


def kernel(**inputs):
    x = np.asarray(inputs["x"], np.float32)
    ys, _ = run(inputs, ALL_STAGES, list(range(8)), [x[b] for b in range(8)])
    return np.stack(ys, axis=0).astype(np.float32)


KB = 1024
import os as _os
DBG_STOP = int(_os.environ.get('DBG_STOP', '0'))
POOL_WINDOWS = (2, 4, 8, 16)


def _cmul(C, eng_m, eng_a, ore, oim, are, aim, bre, bim, t, rd, wr):
    sc = C.sc
    sc.op(eng_m, lambda e: e.tensor_tensor(out=t[0], in0=are, in1=bre, op=ALU.mult), reads=rd, writes=wr)
    sc.op(eng_m, lambda e: e.tensor_tensor(out=t[1], in0=aim, in1=bim, op=ALU.mult), reads=rd, writes=wr)
    sc.op(eng_m, lambda e: e.tensor_tensor(out=t[2], in0=are, in1=bim, op=ALU.mult), reads=rd, writes=wr)
    sc.op(eng_m, lambda e: e.tensor_tensor(out=t[3], in0=aim, in1=bre, op=ALU.mult), reads=rd, writes=wr)
    sc.op(eng_a, lambda e: e.tensor_tensor(out=ore, in0=t[0], in1=t[1], op=ALU.subtract), reads=rd, writes=wr)
    sc.op(eng_a, lambda e: e.tensor_tensor(out=oim, in0=t[2], in1=t[3], op=ALU.add), reads=rd, writes=wr)


def odd_mixer(C, l):
    sc = C.sc
    nc = C.nc
    i = l // 2
    gcol = C.gT[:, (l * 3 + 1) * 8:(l * 3 + 2) * 8]
    mixT = aview(C, 0, [128, 8, S], BF16)
    uT = aview(C, 32 * KB, [128, 4, S], BF16)
    ygT = aview(C, 48 * KB, [128, 4, S], BF16)
    r_mix = [[Res(f"mix{m}_{b}") for b in range(4)] for m in range(8)]
    r_u = [[Res(f"u{m}_{b}") for b in range(4)] for m in range(4)]

    hT = aview(C, 64 * KB, [128, 8, S], BF16)
    r_h = [Res(f"oh{t}") for t in range(NT)]
    W1 = aview(C, 96 * KB, [128, 8, 1024], BF16)
    r_W1 = [Res("oW1a"), Res("oW1b")]
    zb = aview(C, 112 * KB, [128, S], F32)
    pa = aview(C, 120 * KB, [128, S], F32)
    pb = aview(C, 128 * KB, [128, S], F32)
    pl = aview(C, 136 * KB, [128, S], BF16)
    r_z, r_pa, r_pb, r_pl = Res("oz"), Res("opa"), Res("opb"), Res("opl")
    C.xs = [aview(C, 48 * KB + 4096 * k, [128, D], F32) for k in range(2)]
    C.junk = aview(C, 56 * KB, [128, D], BF16)
    PW = aview(C, 58 * KB, [128, 4, 128], BF16)
    r_PW = Res("oPW")
    w_in = C.od_w_in[i].rearrange("(kc p) f -> p kc f", p=128)
    sc.dma("pool", "dw", W1[:, 0:4, :], w_in[:, 0:4, :], writes=[r_W1[0]])
    sc.dma("pool", "dw", W1[:, 4:8, :], w_in[:, 4:8, :], writes=[r_W1[1]])
    sc.dma("pool", "dw", PW, C.pool_w[i].rearrange("g c d -> c g d"), writes=[r_PW])
    for t in range(NT):
        norm_transpose(C, t, gcol, hT[:, :, t * 128:(t + 1) * 128], r_h[t])
    nb = 0
    for m in (0, 4, 1, 5, 2, 6, 3, 7):
        for tb in range(4):
            b = nb % 2
            nb += 1
            for kc in range(8):
                sc.op("pe", lambda e, kc=kc: e.matmul(C.PS[:, b, :], lhsT=W1[:, kc, m * 128:(m + 1) * 128], rhs=hT[:, kc, tb * 512:(tb + 1) * 512],
                                                    start=(kc == 0), stop=(kc == 7)),
                      reads=r_W1 + r_h[tb * 4:(tb + 1) * 4], writes=[C.pr[b]], count=(kc == 7))
            if m >= 4:
                sc.op("act", lambda e: e.activation(out=uT[:, m - 4, tb * 512:(tb + 1) * 512], in_=C.PS[:, b, :], func=AF.Copy),
                      reads=[C.pr[b]], writes=[r_u[m - 4][tb]])
            else:
                sc.op("act", lambda e: e.activation(out=zb[:, tb * 512:(tb + 1) * 512], in_=C.PS[:, b, :], func=AF.Copy),
                      reads=[C.pr[b]], writes=[r_z])
        if m < 4:
            g = m
            w = POOL_WINDOWS[g]
            src, rsrc = zb, r_z
            bufs = [(pa, r_pa), (pb, r_pb)]
            k = 1
            step = 0
            while k < w:
                dst, rdst = bufs[step % 2]
                sc.op("pool", lambda e, dst=dst, src=src, k=k: e.tensor_tensor(out=dst[:, k:], in0=src[:, k:], in1=src[:, :S - k], op=ALU.add),
                      reads=[rsrc], writes=[rdst])
                sc.op("pool", lambda e, dst=dst, src=src, k=k: e.tensor_copy(out=dst[:, :k], in_=src[:, :k]),
                      reads=[rsrc], writes=[rdst])
                src, rsrc = dst, rdst
                k *= 2
                step += 1
            sc.op("dve", lambda e, src=src: e.scalar_tensor_tensor(out=pl[:, w - 1:], in0=src[:, w - 1:], scalar=1.0 / w, in1=zb[:, w - 1:],
                                                                  op0=ALU.mult, op1=ALU.subtract),
                  reads=[rsrc, r_z], writes=[r_pl])
            tmpw = C.stat[:, 32:32 + w - 1]
            sc.op("dve", lambda e, src=src: e.tensor_tensor(out=tmpw, in0=src[:, :w - 1], in1=C.inv16[:, :w - 1], op=ALU.mult),
                  reads=[rsrc, C.r_const], writes=[r_pl])
            sc.op("dve", lambda e: e.tensor_tensor(out=pl[:, :w - 1], in0=tmpw, in1=zb[:, :w - 1], op=ALU.subtract),
                  reads=[r_pl, r_z], writes=[r_pl])
            for tb in range(4):
                b = 2 + tb % 2
                sc.op("pe", lambda e: e.matmul(C.PS[:, b, :], lhsT=PW[:, g, :], rhs=pl[:, tb * 512:(tb + 1) * 512], start=True, stop=True),
                      reads=[r_PW, r_pl], writes=[C.pr[b]])
                sc.op("act", lambda e: e.activation(out=mixT[:, g, tb * 512:(tb + 1) * 512], in_=C.PS[:, b, :], func=AF.Copy,
                                                    scale=C.psc[:, i * 4 + g:i * 4 + g + 1]),
                      reads=[C.pr[b], C.r_const], writes=[r_mix[g][tb]])
    sc.barrier()
    if DBG_STOP == 1:
        return

    LIre = aview(C, 64 * KB, [128, S], F32)
    LIim = aview(C, 72 * KB, [128, S], F32)
    TFre = aview(C, 80 * KB, [128, 16, 128], F32)
    TFim = aview(C, 88 * KB, [128, 16, 128], F32)
    LBre = aview(C, 96 * KB, [128, 16, 128], BF16)
    LBim = aview(C, 100 * KB, [128, 16, 128], BF16)
    LCre = aview(C, 104 * KB, [128, 16, 128], BF16)
    LCimn = aview(C, 108 * KB, [128, 16, 128], BF16)
    LCren = aview(C, 20 * KB, [128, 16, 128], BF16)
    bc = aview(C, 48 * KB, [128, 4, 16, 16], F32)
    Bb = aview(C, 52 * KB, [128, 2, 16, 16], F32)
    E = aview(C, 116 * KB, [128, 16, 128], F32)
    TIre = aview(C, 124 * KB, [128, 16, 128], F32)
    TIim = aview(C, 132 * KB, [128, 16, 128], F32)
    sm = aview(C, 112 * KB, [128, 48, 16], F32)
    rp = Res("oprep")
    RP = [rp]
    sc.dma("sp", "dio", bc[:, 0], C.s5b[i, 0], writes=[rp])
    for k in range(1, 4):
        sc.dma("sp", "dio", bc[:, k], C.s5b[i, k], reads=[rp], writes=[rp])
    lr = C.s5lam[:, 0, i * 16:(i + 1) * 16]
    li = C.s5lam[:, 1, i * 16:(i + 1) * 16]
    ld = C.s5lam[:, 2, i * 16:(i + 1) * 16]
    V = lambda k: sm[:, k, :]
    dt_, a_, b_, ea, eai, s_, c_, cc, ss, lbr, lbi, ibr, ibi = [V(k) for k in range(13)]
    nr, ni, den, cfr, cfi, t0, t1_, pwr, pwi, qwr, qwi = [V(k) for k in range(13, 24)]
    L128r, L128i = V(24), V(25)
    tq = [V(26 + k) for k in range(4)]

    def dv(fn, eng="dve"):
        sc.op(eng, fn, reads=RP + [C.r_const], writes=RP)

    dv(lambda e: e.activation(out=dt_, in_=ld, func=AF.Exp), "act")
    dv(lambda e: e.tensor_tensor(out=a_, in0=lr, in1=dt_, op=ALU.mult))
    dv(lambda e: e.tensor_tensor(out=b_, in0=li, in1=dt_, op=ALU.mult))
    dv(lambda e: e.activation(out=ea, in_=a_, func=AF.Exp), "act")
    dv(lambda e: e.activation(out=eai, in_=a_, func=AF.Exp, scale=-1.0), "act")
    dv(lambda e: e.activation(out=s_, in_=b_, func=AF.Sin, scale=1.0 / 16), "act")
    dv(lambda e: e.activation(out=c_, in_=b_, func=AF.Sin, scale=1.0 / 16, bias=C.halfpi[:]), "act")
    for _ in range(4):
        dv(lambda e: e.tensor_tensor(out=cc, in0=c_, in1=c_, op=ALU.mult))
        dv(lambda e: e.tensor_tensor(out=ss, in0=s_, in1=s_, op=ALU.mult))
        dv(lambda e: e.scalar_tensor_tensor(out=s_, in0=c_, scalar=2.0, in1=s_, op0=ALU.mult, op1=ALU.mult))
        dv(lambda e: e.tensor_tensor(out=c_, in0=cc, in1=ss, op=ALU.subtract))
    dv(lambda e: e.tensor_tensor(out=lbr, in0=ea, in1=c_, op=ALU.mult))
    dv(lambda e: e.tensor_tensor(out=lbi, in0=ea, in1=s_, op=ALU.mult))
    dv(lambda e: e.tensor_tensor(out=ibr, in0=eai, in1=c_, op=ALU.mult))
    dv(lambda e: e.scalar_tensor_tensor(out=ibi, in0=eai, scalar=-1.0, in1=s_, op0=ALU.mult, op1=ALU.mult))
    dv(lambda e: e.tensor_scalar(out=t0, in0=lbr, scalar1=-1.0, scalar2=None, op0=ALU.add))
    dv(lambda e: e.tensor_tensor(out=nr, in0=t0, in1=lr, op=ALU.mult))
    dv(lambda e: e.tensor_tensor(out=t1_, in0=lbi, in1=li, op=ALU.mult))
    dv(lambda e: e.tensor_tensor(out=nr, in0=nr, in1=t1_, op=ALU.add))
    dv(lambda e: e.tensor_tensor(out=ni, in0=lbi, in1=lr, op=ALU.mult))
    dv(lambda e: e.tensor_tensor(out=t1_, in0=t0, in1=li, op=ALU.mult))
    dv(lambda e: e.tensor_tensor(out=ni, in0=ni, in1=t1_, op=ALU.subtract))
    dv(lambda e: e.tensor_tensor(out=den, in0=lr, in1=lr, op=ALU.mult))
    dv(lambda e: e.tensor_tensor(out=t1_, in0=li, in1=li, op=ALU.mult))
    dv(lambda e: e.tensor_tensor(out=den, in0=den, in1=t1_, op=ALU.add))
    dv(lambda e: e.reciprocal(out=den, in_=den))
    dv(lambda e: e.tensor_tensor(out=cfr, in0=nr, in1=den, op=ALU.mult))
    dv(lambda e: e.tensor_tensor(out=cfi, in0=ni, in1=den, op=ALU.mult))

    def build_table(Tre, Tim, br, bi, keep128=False):
        dv(lambda e: e.memset(Tre[:, :, 0:1], 1.0))
        dv(lambda e: e.memset(Tim[:, :, 0:1], 0.0))
        dv(lambda e: e.tensor_copy(out=Tre[:, :, 1], in_=br))
        dv(lambda e: e.tensor_copy(out=Tim[:, :, 1], in_=bi))
        dv(lambda e: e.tensor_copy(out=pwr, in_=br))
        dv(lambda e: e.tensor_copy(out=pwi, in_=bi))
        n = 1
        while n < 128:
            dv(lambda e: e.tensor_tensor(out=qwr, in0=pwr, in1=pwr, op=ALU.mult))
            dv(lambda e: e.tensor_tensor(out=qwi, in0=pwi, in1=pwi, op=ALU.mult))
            dv(lambda e: e.scalar_tensor_tensor(out=pwi, in0=pwr, scalar=2.0, in1=pwi, op0=ALU.mult, op1=ALU.mult))
            dv(lambda e: e.tensor_tensor(out=pwr, in0=qwr, in1=qwi, op=ALU.subtract))
            n *= 2
            if n == 128:
                break
            pr_b = pwr.unsqueeze(2).to_broadcast([128, 16, n])
            pi_b = pwi.unsqueeze(2).to_broadcast([128, 16, n])
            A_re, A_im = Tre[:, :, 0:n], Tim[:, :, 0:n]
            O_re, O_im = Tre[:, :, n:2 * n], Tim[:, :, n:2 * n]
            X1 = E[:, :, 0:n]
            X2 = E[:, :, 64:64 + n]
            dv(lambda e, n=n: e.tensor_tensor(out=X1, in0=A_re, in1=pr_b, op=ALU.mult))
            dv(lambda e, n=n: e.tensor_tensor(out=X2, in0=A_im, in1=pi_b, op=ALU.mult))
            dv(lambda e, n=n: e.tensor_tensor(out=O_re, in0=X1, in1=X2, op=ALU.subtract))
            dv(lambda e, n=n: e.tensor_tensor(out=X1, in0=A_re, in1=pi_b, op=ALU.mult))
            dv(lambda e, n=n: e.tensor_tensor(out=X2, in0=A_im, in1=pr_b, op=ALU.mult))
            dv(lambda e, n=n: e.tensor_tensor(out=O_im, in0=X1, in1=X2, op=ALU.add))
        if keep128:
            dv(lambda e: e.tensor_copy(out=L128r, in_=pwr))
            dv(lambda e: e.tensor_copy(out=L128i, in_=pwi))

    build_table(TFre, TFim, lbr, lbi, keep128=True)
    build_table(TIre, TIim, ibr, ibi)
    for (TI, LI) in ((TIre, LIre), (TIim, LIim)):
        for q4 in range(4):
            b = 4 + q4 % 2
            for q in range(4):
                gp = q4 * 4 + q
                sc.op("pe", lambda e, gp=gp, q=q, TI=TI: e.transpose(out=C.PS[:, b, q * 128:(q + 1) * 128], in_=TI[:, gp, :], identity=C.identf[:]),
                      reads=RP + [C.r_const], writes=[C.pr[b]], count=(q == 3))
            sc.op("act", lambda e, LI=LI: e.activation(out=LI[:, q4 * 512:(q4 + 1) * 512], in_=C.PS[:, b, :], func=AF.Copy),
                  reads=[C.pr[b]], writes=RP)
    cr_b = cfr.unsqueeze(2).to_broadcast([128, 16, 16])
    ci_b = cfi.unsqueeze(2).to_broadcast([128, 16, 16])
    Y1, Y2 = E[:, :, 0:16], E[:, :, 64:80]
    dv(lambda e: e.tensor_tensor(out=Y1, in0=bc[:, 0], in1=cr_b, op=ALU.mult))
    dv(lambda e: e.tensor_tensor(out=Y2, in0=bc[:, 1], in1=ci_b, op=ALU.mult))
    dv(lambda e: e.tensor_tensor(out=Bb[:, 0], in0=Y1, in1=Y2, op=ALU.subtract))
    dv(lambda e: e.tensor_tensor(out=Y1, in0=bc[:, 1], in1=cr_b, op=ALU.mult))
    dv(lambda e: e.tensor_tensor(out=Y2, in0=bc[:, 0], in1=ci_b, op=ALU.mult))
    dv(lambda e: e.tensor_tensor(out=Bb[:, 1], in0=Y1, in1=Y2, op=ALU.add))

    def diag(T, h):
        base = T[h * 64:(h + 1) * 64, 0, 0:1]
        ps = base.ap[0][0]
        return bass.AP(base.tensor, base.offset + h * 16, [[ps, 64], [512, 4], [160, 4], [1, 16]])

    def src4(T3, h):
        return T3[h * 64:(h + 1) * 64].rearrange("p (cu q) c -> p cu q c", q=4)

    for part, LB in ((0, LBre), (1, LBim)):
        dv(lambda e: e.memset(E, 0.0))
        for h in range(2):
            dv(lambda e, h=h, part=part: e.tensor_copy(out=diag(E, h), in_=src4(Bb[:, part], h)))
        for q4 in range(4):
            b = 4 + q4 % 2
            for q in range(4):
                gp = q4 * 4 + q
                sc.op("pe", lambda e, gp=gp, q=q: e.transpose(out=C.PS[:, b, q * 128:(q + 1) * 128], in_=E[:, gp, :], identity=C.identf[:]),
                      reads=RP + [C.r_const], writes=[C.pr[b]], count=(q == 3))
            sc.op("act", lambda e, LB=LB: e.activation(out=LB[:, q4 * 4:(q4 + 1) * 4, :], in_=C.PS[:, b, :].rearrange("p (q s) -> p q s", s=128), func=AF.Copy),
                  reads=[C.pr[b]], writes=RP)
    dv(lambda e: e.memset(LCre, 0.0))
    dv(lambda e: e.memset(LCimn, 0.0))
    dv(lambda e: e.memset(LCren, 0.0))
    for h in range(2):
        dv(lambda e, h=h: e.tensor_copy(out=diag(LCre, h), in_=src4(bc[:, 2], h)))
        dv(lambda e, h=h: e.tensor_scalar(out=diag(LCimn, h), in0=src4(bc[:, 3], h), scalar1=-1.0, scalar2=None, op0=ALU.mult))
        dv(lambda e, h=h: e.tensor_scalar(out=diag(LCren, h), in0=src4(bc[:, 2], h), scalar1=-1.0, scalar2=None, op0=ALU.mult))
    cXr = V(30)
    cXi = V(31)
    dv(lambda e: e.memset(cXr, 0.0))
    dv(lambda e: e.memset(cXi, 0.0))
    sc.barrier()
    if DBG_STOP == 2:
        return

    T = [aview(C, 116 * KB + 2048 * k, [128, 512], F32) for k in range(4)]
    btr2 = [aview(C, 124 * KB, [128, 512], BF16), aview(C, 18 * KB, [128, 512], BF16)]
    bti2 = [aview(C, 125 * KB, [128, 512], BF16), aview(C, 19 * KB, [128, 512], BF16)]
    wr_ = aview(C, 126 * KB, [128, 4, 128], F32)
    wi_ = aview(C, 128 * KB, [128, 4, 128], F32)
    Pq = [[aview(C, 130 * KB + 1024 * (2 * k + pp), [128, 4, 128], BF16) for k in range(4)] for pp in range(2)]
    r_Pq = [[Res(f"oPq{pp}_{k}") for k in range(4)] for pp in range(2)]
    xre = [aview(C, 138 * KB + 1024 * k, [128, 4, 128], BF16) for k in range(2)]
    xim = [aview(C, 16 * KB + 1024 * k, [128, 4, 128], BF16) for k in range(2)]
    yv = [aview(C, 115 * KB + 512 * k, [128, 128], F32) for k in range(2)]
    r_T, r_w, r_P, r_P3 = Res("oT"), Res("ow"), Res("oP"), Res("oP3")
    r_bt2 = [Res("obt0"), Res("obt1")]
    r_x = [Res("ox0"), Res("ox1")]
    r_yv = [Res("oyv0"), Res("oyv1")]
    r_cx = [Res(f"ocx{cu}") for cu in range(4)]
    r_yg = [[Res(f"oyg{cu}_{k}") for k in range(NT)] for cu in range(4)]
    units = [(k, cu) for k in range(NT) for cu in range(4)]

    def stage1(n):
        k, cu = units[n]
        par = n % 2
        bA, bB = 2 * par, 2 * par + 1
        ts = slice(k * 128, (k + 1) * 128)
        g4 = slice(4 * cu, 4 * cu + 4)
        ru = [r_u[cu][k // 4]]
        sc.op("pe", lambda e: e.matmul(C.PS[:, bA, :], lhsT=uT[:, cu, ts], rhs=LBre[:, g4, :], start=True, stop=True),
              reads=ru, writes=[C.pr[bA]])
        sc.op("pe", lambda e: e.matmul(C.PS[:, bB, :], lhsT=uT[:, cu, ts], rhs=LBim[:, g4, :], start=True, stop=True),
              reads=ru, writes=[C.pr[bB]])
        cs = slice(cu * 512, (cu + 1) * 512)
        sc.op("dve", lambda e: e.tensor_tensor(out=T[0], in0=C.PS[:, bA, :], in1=LIre[:, cs], op=ALU.mult), reads=[C.pr[bA]], writes=[r_T])
        sc.op("dve", lambda e: e.tensor_tensor(out=T[1], in0=C.PS[:, bB, :], in1=LIim[:, cs], op=ALU.mult), reads=[C.pr[bB]], writes=[r_T])
        sc.op("dve", lambda e: e.tensor_tensor(out=T[2], in0=C.PS[:, bB, :], in1=LIre[:, cs], op=ALU.mult), reads=[C.pr[bB]], writes=[r_T])
        sc.op("dve", lambda e: e.tensor_tensor(out=T[3], in0=C.PS[:, bA, :], in1=LIim[:, cs], op=ALU.mult), reads=[C.pr[bA]], writes=[r_T])
        btr, bti, r_bt = btr2[n % 2], bti2[n % 2], r_bt2[n % 2]
        sc.op("dve", lambda e: e.tensor_tensor(out=btr, in0=T[0], in1=T[1], op=ALU.subtract), reads=[r_T], writes=[r_bt])
        sc.op("pool", lambda e: e.tensor_tensor(out=bti, in0=T[2], in1=T[3], op=ALU.add), reads=[r_T], writes=[r_bt])

    def stage1b(n):
        btr, bti, r_bt = btr2[n % 2], bti2[n % 2], r_bt2[n % 2]
        for (bt, bk) in ((btr, 4), (bti, 5)):
            for q in range(4):
                sc.op("pe", lambda e, bt=bt, bk=bk, q=q: e.matmul(C.PS[:, bk, q * 128:(q + 1) * 128], lhsT=bt[:, q * 128:(q + 1) * 128], rhs=C.triT[:],
                                                                start=True, stop=True),
                      reads=[r_bt, C.r_const], writes=[C.pr[bk]], count=(q == 3))

    def stage2a(n):
        k, cu = units[n]
        g4 = slice(4 * cu, 4 * cu + 4)
        cxr_b = cXr[:, g4].unsqueeze(2).to_broadcast([128, 4, 128])
        cxi_b = cXi[:, g4].unsqueeze(2).to_broadcast([128, 4, 128])
        w4r = C.PS[:, 4, :].rearrange("p (q j) -> p q j", j=128)
        w4i = C.PS[:, 5, :].rearrange("p (q j) -> p q j", j=128)
        for q in range(4):
            sc.op("act", lambda e, q=q: e.activation(out=wr_[:, q, :], in_=w4r[:, q, :], func=AF.Identity, bias=cXr[:, 4 * cu + q:4 * cu + q + 1]),
                  reads=[C.pr[4], r_cx[cu]], writes=[r_w])
            sc.op("act", lambda e, q=q: e.activation(out=wi_[:, q, :], in_=w4i[:, q, :], func=AF.Identity, bias=cXi[:, 4 * cu + q:4 * cu + q + 1]),
                  reads=[C.pr[5], r_cx[cu]], writes=[r_w])
        w127r, w127i = wr_[:, :, 127], wi_[:, :, 127]
        sc.op("pool", lambda e: e.tensor_tensor(out=tq[0][:, 0:4], in0=w127r, in1=L128r[:, g4], op=ALU.mult), reads=[r_w], writes=[r_cx[cu]])
        sc.op("pool", lambda e: e.tensor_tensor(out=tq[1][:, 0:4], in0=w127i, in1=L128i[:, g4], op=ALU.mult), reads=[r_w], writes=[r_cx[cu]])
        sc.op("pool", lambda e: e.tensor_tensor(out=tq[2][:, 0:4], in0=w127i, in1=L128r[:, g4], op=ALU.mult), reads=[r_w], writes=[r_cx[cu]])
        sc.op("pool", lambda e: e.tensor_tensor(out=tq[3][:, 0:4], in0=w127r, in1=L128i[:, g4], op=ALU.mult), reads=[r_w], writes=[r_cx[cu]])
        sc.op("pool", lambda e: e.tensor_tensor(out=cXr[:, g4], in0=tq[0][:, 0:4], in1=tq[1][:, 0:4], op=ALU.subtract), reads=[r_cx[cu]], writes=[r_cx[cu]])
        sc.op("pool", lambda e: e.tensor_tensor(out=cXi[:, g4], in0=tq[2][:, 0:4], in1=tq[3][:, 0:4], op=ALU.add), reads=[r_cx[cu]], writes=[r_cx[cu]])
        par = n % 2
        P = Pq[par]
        rP = r_Pq[par]
        sc.op("pool", lambda e: e.tensor_tensor(out=P[0], in0=TFre[:, g4, :], in1=wr_, op=ALU.mult), reads=[r_w], writes=[rP[0]])
        sc.op("pool", lambda e: e.tensor_tensor(out=P[1], in0=TFim[:, g4, :], in1=wi_, op=ALU.mult), reads=[r_w], writes=[rP[1]])
        sc.op("dve", lambda e: e.tensor_tensor(out=P[2], in0=TFre[:, g4, :], in1=wi_, op=ALU.mult), reads=[r_w], writes=[rP[2]])
        sc.op("dve", lambda e: e.tensor_tensor(out=P[3], in0=TFim[:, g4, :], in1=wr_, op=ALU.mult), reads=[r_w], writes=[rP[3]])

    def stage2b(n):
        k, cu = units[n]
        par = n % 2
        ts = slice(k * 128, (k + 1) * 128)
        ru = [r_u[cu][k // 4]]
        by = 6 + par
        P = Pq[par]
        rP = r_Pq[par]
        terms = [(LCre, 0), (LCren, 1), (LCimn, 2), (LCimn, 3)]
        nmm = 0
        for q in range(4):
            for (LT, kk) in terms:
                nmm += 1
                sc.op("pe", lambda e, q=q, LT=LT, kk=kk, nmm=nmm: e.matmul(C.PS[:, by, 0:128], lhsT=LT[:, 4 * cu + q, :], rhs=P[kk][:, q, :], start=(nmm == 1), stop=(nmm == 16)),
                      reads=[rP[kk]], writes=[C.pr[by]], count=(nmm == 16))
        sc.op("dve", lambda e: e.scalar_tensor_tensor(out=yv[par], in0=uT[:, cu, ts], scalar=C.s5d[:, i * 4 + cu:i * 4 + cu + 1], in1=C.PS[:, by, 0:128],
                                                     op0=ALU.mult, op1=ALU.add),
              reads=[C.pr[by], C.r_const] + ru, writes=[r_yv[par]])

    def stage2c(n):
        k, cu = units[n]
        par = n % 2
        ts = slice(k * 128, (k + 1) * 128)
        sc.op("act", lambda e: e.activation(out=ygT[:, cu, ts], in_=yv[par], func=AF.Gelu_apprx_tanh),
              reads=[r_yv[par]], writes=[r_yg[cu][k]])

    NU = len(units)
    stage1(0)
    stage1b(0)
    if NU > 1:
        stage1(1)
    for n in range(NU):
        stage2a(n)
        if n >= 1:
            stage2c(n - 1)
        if n + 1 < NU:
            stage1b(n + 1)
        if n + 2 < NU:
            stage1(n + 2)
        stage2b(n)
    stage2c(NU - 1)
    sc.barrier()
    if DBG_STOP == 3:
        return

    WGL = aview(C, 64 * KB, [128, 4, 1024], BF16)
    WO = aview(C, 72 * KB, [128, 8, 1024], BF16)
    sgl = [aview(C, 88 * KB + 2048 * k, [128, 512], F32) for k in range(2)]
    r_WGL, r_WO = Res("oWGL"), [Res("oWOa"), Res("oWOb")]
    r_sgl = [Res("osgl0"), Res("osgl1")]
    sc.dma("pool", "dw", WGL, C.s5_w_glu[i].rearrange("(kc p) f -> p kc f", p=128), writes=[r_WGL])
    wov = C.od_w_out[i].rearrange("(kc p) f -> p kc f", p=128)
    sc.dma("pool", "dw", WO[:, 0:4, :], wov[:, 0:4, :], writes=[r_WO[0]])
    sc.dma("pool", "dw", WO[:, 4:8, :], wov[:, 4:8, :], writes=[r_WO[1]])
    n = 0
    for m in range(4):
        for tb in range(4):
            par = n % 2
            n += 1
            bA, bG = par, 2 + par
            for (bk, col) in ((bA, m), (bG, m + 4)):
                for kc in range(4):
                    sc.op("pe", lambda e, kc=kc, bk=bk, col=col: e.matmul(C.PS[:, bk, :], lhsT=WGL[:, kc, col * 128:(col + 1) * 128],
                                                                        rhs=ygT[:, kc, tb * 512:(tb + 1) * 512], start=(kc == 0), stop=(kc == 3)),
                          reads=[r_WGL], writes=[C.pr[bk]], count=(kc == 3))
            sc.op("act", lambda e: e.activation(out=sgl[par], in_=C.PS[:, bG, :], func=AF.Sigmoid), reads=[C.pr[bG]], writes=[r_sgl[par]])
            sc.op("dve", lambda e: e.tensor_tensor(out=mixT[:, 4 + m, tb * 512:(tb + 1) * 512], in0=sgl[par], in1=C.PS[:, bA, :], op=ALU.mult),
                  reads=[r_sgl[par], C.pr[bA]], writes=[r_mix[4 + m][tb]])
    if DBG_STOP == 4:
        return
    out_proj(C, mixT, [r for rr in r_mix for r in rr], WO, r_WO)


def out_proj(C, mixT, r_mix_all, WO, r_WO):
    sc = C.sc
    n = 0
    for t in range(NT):
        for db in range(2):
            b = 4 + n % 4
            n += 1
            for kc in range(8):
                sc.op("pe", lambda e, kc=kc: e.matmul(C.PS[:, b, :], lhsT=mixT[:, kc, t * 128:(t + 1) * 128], rhs=WO[:, kc, db * 512:(db + 1) * 512],
                                                    start=(kc == 0), stop=(kc == 7)),
                      reads=r_mix_all + r_WO, writes=[C.pr[b]], count=(kc == 7))
            xv = C.X[:, t, db * 512:(db + 1) * 512]
            sc.op("dve", lambda e, xv=xv: e.scalar_tensor_tensor(out=xv, in0=C.PS[:, b, :], scalar=1.0, in1=xv, op0=ALU.mult, op1=ALU.add),
                  reads=[C.pr[b], C.xr[t]], writes=[C.xr[t]])


NEG = -30000.0
KBOUND = 12.0


class Bump:
    def __init__(self, C, start, end=ARENA):
        self.C, self.off, self.end = C, start, end

    def __call__(self, shape, dtype):
        esz = 4 if dtype == F32 else 2
        n = 1
        for d in shape[1:]:
            n *= d
        nb = (n * esz + 63) // 64 * 64
        assert self.off + nb <= self.end, ("arena overflow", self.off, nb, self.end)
        v = aview(self.C, self.off, [shape[0], nb // esz], dtype)[:, 0:n]
        self.off += nb
        if len(shape) > 2:
            names = ["a", "b", "c", "d"][:len(shape) - 1]
            pat = "p (" + " ".join(names) + ") -> p " + " ".join(names)
            v = v.rearrange(pat, **{nm: d for nm, d in zip(names[1:], shape[2:])})
        return v


def psb(C, b):
    return C.PS[:, b, :].bitcast(BF16)


def rope_ops(C, src3, dst3, cos_b, sin_b, tmp, rd, r_tmp, r_dst, npart=128):
    sc = C.sc
    x1, x2 = src3[:, :, 0:32], src3[:, :, 32:64]
    sc.op("dve", lambda e: e.tensor_tensor(out=tmp[0], in0=x1, in1=cos_b, op=ALU.mult), reads=rd, writes=[r_tmp])
    sc.op("dve", lambda e: e.tensor_tensor(out=tmp[1], in0=x2, in1=sin_b, op=ALU.mult), reads=rd, writes=[r_tmp])
    sc.op("dve", lambda e: e.tensor_tensor(out=tmp[2], in0=x2, in1=cos_b, op=ALU.mult), reads=rd, writes=[r_tmp])
    sc.op("dve", lambda e: e.tensor_tensor(out=tmp[3], in0=x1, in1=sin_b, op=ALU.mult), reads=rd, writes=[r_tmp])
    sc.op("pool", lambda e: e.tensor_tensor(out=dst3[:, :, 0:32], in0=tmp[0], in1=tmp[1], op=ALU.subtract), reads=[r_tmp], writes=[r_dst])
    sc.op("pool", lambda e: e.tensor_tensor(out=dst3[:, :, 32:64], in0=tmp[2], in1=tmp[3], op=ALU.add), reads=[r_tmp], writes=[r_dst])


def even_mixer(C, l):
    sc = C.sc
    i = l // 2
    gcol = C.gT[:, (l * 3 + 1) * 8:(l * 3 + 2) * 8]
    al = Bump(C, 0)
    mixT = al([128, 8, S], BF16)
    qT = al([128, 4, S], BF16)
    ksT = al([128, S], BF16)
    kwT = al([128, S], BF16)
    Vs = al([128, NT, 2, 65], BF16)
    Vw = al([128, NT, 2, 65], BF16)
    gates = al([128, NT, 24], F32)
    Zall = al([128, NT, 136], BF16)
    kcmpT = al([128, 128], BF16)
    Vc = al([128, 2, 97], BF16)
    kcT = al([128, S], BF16)
    vcT = al([128, S], BF16)
    p_end = al.off
    r_mix = [Res(f"emix{t}") for t in range(NT)]
    r_q = [Res(f"eq{t}") for t in range(NT)]
    r_k = [Res(f"ek{t}") for t in range(NT)]
    r_v = [Res(f"ev{t}") for t in range(NT)]
    r_gate = [Res(f"egate{t}") for t in range(NT)]
    r_Z = [Res(f"eZ{t}") for t in range(NT)]
    r_cv = [Res(f"ecv{b}") for b in range(4)]
    r_init = Res("einit")

    al = Bump(C, p_end)
    hT = al([128, 8, 1024], BF16)
    gus = al([128, 8, 512], BF16)
    cosT = al([128, NT, 32], F32)
    sinT = al([128, NT, 32], F32)
    WsT = al([128, 8, 128], BF16)
    WsR = al([128, 8, 128], F32)
    bT = al([128, 8], F32)
    C.xs = [al([128, D], F32) for _ in range(2)]
    C.junk = al([128, D], BF16)
    gv = al([128, 8, 64], F32)
    cen = al([128, 8, 64], F32)
    sq = al([128, 8, 64], F32)
    vn = al([128, 512], BF16)
    oa = al([128, 512], BF16)
    rtmp = [al([128, 8, 32], F32) for _ in range(4)]
    rq = al([128, 8, 64], F32)
    qA = al([128, 512], BF16)
    kA = al([128, 256], BF16)
    st8 = al([128, 8, 8], F32)
    Wr = [aview(C, 16 * KB + 8192 * k, [128, 8, 512], BF16) for k in range(2)]
    r_Wr = [[Res(f"eWr{k}_{j}") for j in range(4)] for k in range(2)]
    r_h = [Res(f"eh{t}") for t in range(8)]
    r_gus = [Res(f"egus{t}") for t in range(8)]
    r_rope, r_ws = Res("erope"), Res("ews")
    r_gv, r_cen, r_sq, r_vn, r_oa, r_rt, r_rq, r_qA, r_kA, r_st = [Res("e" + n) for n in ("gv", "cen", "sq", "vn", "oa", "rt", "rq", "qA", "kA", "st")]
    sc.dma("sp", "dio", cosT, C.rope_in[0], writes=[r_rope])
    sc.dma("sp", "dio", sinT, C.rope_in[1], reads=[r_rope], writes=[r_rope])
    sc.dma("sp", "dio", WsR, C.gm_ws_in[i], writes=[r_ws])
    sc.dma("sp", "dio", bT, C.gm_b_in[i], reads=[r_ws], writes=[r_ws])
    sc.op("dve", lambda e: e.tensor_tensor(out=WsT, in0=WsR, in1=C.triT[:].unsqueeze(1).to_broadcast([128, 8, 128]), op=ALU.mult),
          reads=[r_ws, C.r_const], writes=[r_ws])
    sc.op("dve", lambda e: e.memset(Zall, 0.0), writes=[r_init])
    sc.op("dve", lambda e: e.memset(Vs[:, :, :, 64:65], 1.0), reads=[r_init], writes=[r_init])
    sc.op("dve", lambda e: e.memset(Vw[:, :, :, 64:65], 1.0), reads=[r_init], writes=[r_init])
    w_in = C.ev_w_in[i].rearrange("(kc p) f -> p kc f", p=128)
    wctr = [0]

    def load_group(cols):
        k = wctr[0] % 2
        wctr[0] += 1
        o = 0
        rs = []
        for j, (c0, w) in enumerate(cols):
            sc.dma("pool", "dw", Wr[k][:, :, o:o + w], w_in[:, :, c0:c0 + w], writes=[r_Wr[k][j]])
            rs.append(r_Wr[k][j])
            o += w
        return Wr[k], rs

    pctr = [0]

    def proj_tm(W, rW, tt, ncols, col0=0):
        b = pctr[0] % 2
        pctr[0] += 1
        for kc in range(8):
            sc.op("pe", lambda e, kc=kc: e.matmul(C.PS[:, b, 0:ncols], lhsT=hT[:, kc, tt * 128:(tt + 1) * 128], rhs=W[:, kc, col0:col0 + ncols],
                                                start=(kc == 0), stop=(kc == 7)),
                  reads=rW + [r_h[tt]], writes=[C.pr[b]], count=(kc == 7))
        return b

    sq2 = WsR[:, 0:4, :].rearrange("p a (b d) -> p (a b) d", d=64)
    r_sq2 = Res("esq2")

    def chain_v(hf, tt, W, rW):
        t = hf * 8 + tt
        b = proj_tm(W, rW, tt, 512)
        sc.op("act", lambda e: e.activation(out=gv, in_=C.PS[:, b, :].rearrange("p (h d) -> p h d", d=64), func=AF.Gelu_apprx_tanh),
              reads=[C.pr[b]], writes=[r_gv])
        yield
        s1, mean, s2, rstd = st8[:, 0, :], st8[:, 1, :], st8[:, 2, :], st8[:, 3, :]
        sc.op("dve", lambda e: e.tensor_reduce(out=s1, in_=gv, axis=AX.X, op=ALU.add), reads=[r_gv], writes=[r_st])
        sc.op("dve", lambda e: e.tensor_scalar(out=mean, in0=s1, scalar1=1.0 / 64, scalar2=None, op0=ALU.mult), reads=[r_st], writes=[r_st])
        yield
        sc.op("dve", lambda e: e.tensor_tensor(out=cen, in0=gv, in1=mean.unsqueeze(2).to_broadcast([128, 8, 64]), op=ALU.subtract),
              reads=[r_gv, r_st], writes=[r_cen])
        sc.op("pool", lambda e: e.tensor_tensor(out=sq, in0=cen, in1=cen, op=ALU.mult), reads=[r_cen], writes=[r_sq])
        yield
        sc.op("dve", lambda e: e.tensor_reduce(out=s2, in_=sq, axis=AX.X, op=ALU.add), reads=[r_sq], writes=[r_st])
        sc.op("dve", lambda e: e.tensor_scalar(out=s2, in0=s2, scalar1=1.0 / 64, scalar2=1e-5, op0=ALU.mult, op1=ALU.add), reads=[r_st], writes=[r_st])
        sc.op("pool", lambda e: e.tensor_tensor(out=rstd, in0=s2, in1=C.cm05[:].to_broadcast([128, 8]), op=ALU.pow), reads=[r_st, C.r_const], writes=[r_st])
        yield
        sc.op("dve", lambda e: e.tensor_tensor(out=vn.rearrange("p (h d) -> p h d", d=64), in0=cen, in1=rstd.unsqueeze(2).to_broadcast([128, 8, 64]), op=ALU.mult),
              reads=[r_cen, r_st], writes=[r_vn])
        for h in range(8):
            sc.op("pe", lambda e, h=h: e.matmul(C.PS[:, 2, h * 64:(h + 1) * 64], lhsT=WsT[:, h, :], rhs=vn[:, h * 64:(h + 1) * 64], start=True, stop=True),
                  reads=[r_ws, r_vn], writes=[C.pr[2]], count=(h == 7))
        yield
        sc.op("dve", lambda e: e.tensor_tensor(out=cen, in0=C.PS[:, 2, :].rearrange("p (h d) -> p h d", d=64), in1=bT.unsqueeze(2).to_broadcast([128, 8, 64]), op=ALU.add),
              reads=[C.pr[2], r_ws], writes=[r_cen])
        sc.op("dve", lambda e: e.tensor_tensor(out=oa, in0=cen.rearrange("p h d -> p (h d)"), in1=gus[:, tt, :], op=ALU.mult),
              reads=[r_cen, r_gus[tt]], writes=[r_oa])
        yield
        for c in range(4):
            sc.op("pe", lambda e, c=c: e.transpose(out=psb(C, 3)[:, c * 128:(c + 1) * 128], in_=oa[:, c * 128:(c + 1) * 128], identity=C.identb[:]),
                  reads=[r_oa, C.r_const], writes=[C.pr[3]], count=(c == 3))
        sc.op("act", lambda e: e.activation(out=mixT[:, 0:4, t * 128:(t + 1) * 128], in_=psb(C, 3)[:, 0:512].rearrange("p (c t) -> p c t", t=128), func=AF.Copy),
              reads=[C.pr[3]], writes=[r_mix[t]])
        yield

    def chain_q(hf, tt, W, rW):
        t = hf * 8 + tt
        b = proj_tm(W, rW, tt, 512)
        cb_ = cosT[:, t, :].unsqueeze(1).to_broadcast([128, 8, 32])
        sb_ = sinT[:, t, :].unsqueeze(1).to_broadcast([128, 8, 32])
        rope_ops(C, C.PS[:, b, :].rearrange("p (h d) -> p h d", d=64), rq, cb_, sb_, rtmp, [C.pr[b], r_rope], r_rt, r_rq)
        yield
        sc.op("pool", lambda e: e.tensor_tensor(out=sq2, in0=rq, in1=rq, op=ALU.mult), reads=[r_rq, r_ws], writes=[r_sq2])
        ssq, nm = st8[:, 4, :], st8[:, 5, :]
        sc.op("dve", lambda e: e.tensor_reduce(out=ssq, in_=sq2, axis=AX.X, op=ALU.add), reads=[r_sq2], writes=[r_st2])
        yield
        sc.op("pool", lambda e: e.tensor_tensor(out=nm, in0=ssq, in1=C.c05[:].to_broadcast([128, 8]), op=ALU.pow), reads=[r_st2, C.r_const], writes=[r_st2])
        zb_ = Zall[:, t, 32:36]
        zout = bass.AP(zb_.tensor, zb_.offset, [[zb_.ap[0][0], 128], [100, 2], [1, 4]])
        sc.op("dve", lambda e, zout=zout: e.tensor_scalar(out=zout, in0=nm.rearrange("p (g h) -> p g h", h=4), scalar1=-0.125 * KBOUND, scalar2=None, op0=ALU.mult),
              reads=[r_st2, r_init], writes=[r_Z[t]])
        sc.op("act", lambda e: e.activation(out=qA.rearrange("p (h g d) -> p g h d", h=4, g=2), in_=rq.rearrange("p (g h) d -> p g h d", g=2), func=AF.Copy, scale=0.125),
              reads=[r_rq], writes=[r_qA])
        yield
        for h in range(4):
            sc.op("pe", lambda e, h=h: e.transpose(out=psb(C, 3)[:, h * 128:(h + 1) * 128], in_=qA[:, h * 128:(h + 1) * 128], identity=C.identb[:]),
                  reads=[r_qA, C.r_const], writes=[C.pr[3]], count=(h == 3))
        sc.op("act", lambda e: e.activation(out=qT[:, :, t * 128:(t + 1) * 128], in_=psb(C, 3)[:, 0:512].rearrange("p (h t) -> p h t", t=128), func=AF.Copy),
              reads=[C.pr[3]], writes=[r_q[t]])
        yield

    def chain_k(hf, tt, W, rW):
        t = hf * 8 + tt
        b = proj_tm(W, rW, tt, 512)
        cb_ = cosT[:, t, :].unsqueeze(1).to_broadcast([128, 4, 32])
        sb_ = sinT[:, t, :].unsqueeze(1).to_broadcast([128, 4, 32])
        sc.op("act", lambda e: e.activation(out=Vs[:, t, :, 0:64], in_=C.PS[:, b, 256:384].rearrange("p (g d) -> p g d", d=64), func=AF.Copy),
              reads=[C.pr[b], r_init], writes=[r_v[t]])
        sc.op("act", lambda e: e.activation(out=Vw[:, t, :, 0:64], in_=C.PS[:, b, 384:512].rearrange("p (g d) -> p g d", d=64), func=AF.Copy),
              reads=[C.pr[b], r_init], writes=[r_v[t]])
        rope_ops(C, C.PS[:, b, 0:256].rearrange("p (h d) -> p h d", d=64), rq[:, 0:4, :], cb_, sb_, [x[:, 0:4, :] for x in rtmp], [C.pr[b], r_rope], r_rt, r_rq)
        yield
        sc.op("act", lambda e: e.activation(out=kA, in_=rq[:, 0:4, :].rearrange("p h d -> p (h d)"), func=AF.Copy), reads=[r_rq], writes=[r_kA])
        yield
        for c in range(2):
            sc.op("pe", lambda e, c=c: e.transpose(out=psb(C, 3)[:, c * 128:(c + 1) * 128], in_=kA[:, c * 128:(c + 1) * 128], identity=C.identb[:]),
                  reads=[r_kA, C.r_const], writes=[C.pr[3]], count=(c == 1))
        sc.op("act", lambda e: e.activation(out=ksT[:, t * 128:(t + 1) * 128], in_=psb(C, 3)[:, 0:128], func=AF.Copy), reads=[C.pr[3]], writes=[r_k[t]])
        sc.op("act", lambda e: e.activation(out=kwT[:, t * 128:(t + 1) * 128], in_=psb(C, 3)[:, 128:256], func=AF.Copy), reads=[C.pr[3]], writes=[r_k[t]])
        yield

    def chain_c(hf, W, rW):
        for tb in range(2):
            for cc, dstT in ((0, kcT), (1, vcT)):
                b = pctr[0] % 2
                pctr[0] += 1
                for kc in range(8):
                    sc.op("pe", lambda e, kc=kc: e.matmul(C.PS[:, b, :], lhsT=W[:, kc, cc * 128:(cc + 1) * 128], rhs=hT[:, kc, tb * 512:(tb + 1) * 512],
                                                        start=(kc == 0), stop=(kc == 7)),
                          reads=rW + r_h[tb * 4:(tb + 1) * 4], writes=[C.pr[b]], count=(kc == 7))
                tok0 = hf * 1024 + tb * 512
                sc.op("act", lambda e: e.activation(out=dstT[:, tok0:tok0 + 512], in_=C.PS[:, b, :], func=AF.Copy), reads=[C.pr[b]], writes=[r_cv[hf * 2 + tb]])
                yield
        for tt in range(8):
            t = hf * 8 + tt
            b = proj_tm(W, rW, tt, 32, col0=256)
            sc.op("act", lambda e: e.activation(out=gates[:, t, :], in_=C.PS[:, b, 8:32], func=AF.Sigmoid), reads=[C.pr[b]], writes=[r_gate[t]])
            yield

    def seq(gens):
        for g_ in gens:
            yield from g_

    def ilv(gens):
        gens = list(gens)
        while gens:
            for g_ in list(gens):
                try:
                    next(g_)
                except StopIteration:
                    gens.remove(g_)

    r_st2 = Res("est2")
    for hf in range(2):
        for tt in range(8):
            norm_transpose(C, hf * 8 + tt, gcol, hT[:, :, tt * 128:(tt + 1) * 128], r_h[tt])
        W, rW = load_group([(0, 512)])
        for tt in range(8):
            b = proj_tm(W, rW, tt, 512)
            sc.op("act", lambda e: e.activation(out=gus[:, tt, :], in_=C.PS[:, b, :], func=AF.Gelu_apprx_tanh), reads=[C.pr[b]], writes=[r_gus[tt]])
        Wv, rWv = load_group([(512, 512)])
        Wq, rWq = load_group([(1024, 512)])
        ilv([seq(chain_v(hf, tt, Wv, rWv) for tt in range(8)), seq(chain_q(hf, tt, Wq, rWq) for tt in range(8))])
        Wk, rWk = load_group([(1792, 128), (2048, 128), (1920, 128), (2176, 128)])
        Wc, rWc = load_group([(1536, 256), (2296, 32)])
        ilv([seq(chain_k(hf, tt, Wk, rWk) for tt in range(8)), chain_c(hf, Wc, rWc)])
    sc.barrier()
    if DBG_STOP == 1:
        return
    even_attention(C, i, p_end, mixT, qT, ksT, kwT, Vs, Vw, gates, Zall, kcmpT, Vc, kcT, vcT, r_mix)


def even_attention(C, i, p_end, mixT, qT, ksT, kwT, Vs, Vw, gates, Zall, kcmpT, Vc, kcT, vcT, r_mix):
    sc = C.sc
    al = Bump(C, p_end)
    w1 = al([128, 2, 32, 64], BF16)
    w2 = al([64, 2, 64], BF16)
    peT = al([128, 2, 32], BF16)
    cbias = al([64, 2], F32)
    gh = al([64, 4, 128], BF16)
    cosc = al([128, 32], F32)
    sinc = al([128, 32], F32)
    rkc = al([128, 2, 64], F32)
    ctmp = [al([128, 2, 32], F32) for _ in range(4)]
    kcA = al([128, 128], BF16)
    Fpos = al([128, NT, 32], F32)
    Fneg = al([128, NT, 32], F32)
    EB = al([100, S], BF16)
    triB = al([128, 512], BF16)
    bandB = al([128, 512], BF16)
    BM = al([100, 512], BF16)
    Mpat = al([73, 512], BF16)
    Bwin = al([73, 256], BF16)
    NP = 4
    Pt = [al([128, 512], BF16) for _ in range(NP)]
    SB1 = [al([100, 512], BF16) for _ in range(2)]
    SB2 = [al([100, 512], BF16) for _ in range(2)]
    oacc = al([128, 4, 64], F32)
    otmp = al([128, 4, 64], F32)
    ob = al([128, 512], BF16)
    sm4 = al([128, 16, 4], F32)
    impn = al([128, 4, 32], F32)
    vsel = al([128, 32], F32)
    top8 = al([128, 8], F32)
    WO = al([128, 8, 1024], BF16)
    rc = Res("ecmp")
    r_cst = [Res(f"ecst{k}") for k in range(9)]
    r_WO = [Res("eWOa"), Res("eWOb")]
    sc.dma("sp", "dio", Fpos, C.fsel_in[0], writes=[r_cst[0]])
    sc.dma("sp", "dio", Fneg, C.fsel_in[1], writes=[r_cst[1]])
    sc.dma("pool", "dio", EB, C.eb_in, writes=[r_cst[2]])
    sc.dma("pool", "dio", triB, C.trib_in, writes=[r_cst[3]])
    sc.dma("pool", "dio", bandB, C.bandb_in, writes=[r_cst[4]])
    sc.dma("pool", "dio", BM, C.bm_in, writes=[r_cst[5]])
    sc.dma("pool", "dio", Mpat, C.mpat_in, writes=[r_cst[6]])
    sc.dma("pool", "dio", Bwin, C.bwin_in, writes=[r_cst[7]])
    sc.dma("sp", "dio", cosc, C.ropec_in[0], writes=[r_cst[8]])
    sc.dma("sp", "dio", sinc, C.ropec_in[1], reads=[r_cst[8]], writes=[r_cst[8]])
    wov = C.ev_w_out[i].rearrange("(kc p) f -> p kc f", p=128)
    sc.dma("pool", "dw", WO[:, 0:4, :], wov[:, 0:4, :], writes=[r_WO[0]])
    sc.dma("pool", "dw", WO[:, 4:8, :], wov[:, 4:8, :], writes=[r_WO[1]])
    r_w1 = [Res(f"ew1_{k}") for k in range(4)]
    for kv in range(2):
        src = C.nsa_w1[i, kv].rearrange("(l d) e -> d l e", d=64)
        for hh in range(2):
            sc.dma("pool", "dw", w1[hh * 64:(hh + 1) * 64, kv], src, writes=[r_w1[kv * 2 + hh]])
    r_w2 = Res("ew2")
    sc.dma("pool", "dw", w2, C.nsa_w2[i].rearrange("k e f -> e k f"), writes=[r_w2])
    r_pe = Res("epe")
    sc.dma("pool", "dw", peT, C.nsa_pe_in[i], writes=[r_pe])
    sc.dma("pool", "dio", Vc[0:127, 0, 65:97], C.ovl_in, writes=[rc])
    sc.dma("pool", "dio", Vc[0:127, 1, 65:97], C.ovl_in, reads=[rc], writes=[rc])
    sc.op("dve", lambda e: e.memset(Vc[:, :, 64:65], 1.0), reads=[rc], writes=[rc])
    sc.op("dve", lambda e: e.memset(kcA, 0.0), writes=[rc])

    RC = [rc]
    if DBG_STOP == 21:
        sc.barrier()
        return
    for kv in range(2):
        for l_ in range(32):
            sc.op("pe", lambda e, kv=kv, l_=l_: e.matmul(C.PS[0:64, 7, kv:kv + 1], lhsT=w1[0:64, kv, l_, :], rhs=peT[0:64, kv, l_:l_ + 1], start=(l_ == 0), stop=(l_ == 31)),
                  reads=r_w1 + [r_pe], writes=[C.pr[7]], count=(l_ == 31))
    sc.op("act", lambda e: e.activation(out=cbias, in_=C.PS[0:64, 7, 0:2], func=AF.Copy), reads=[C.pr[7]], writes=RC)
    if DBG_STOP == 22:
        sc.barrier()
        return
    for kv in range(2):
        srcT = kcT if kv == 0 else vcT
        s3 = srcT.rearrange("p (n s) -> p n s", s=16)
        for g in range(2):
            slot = kv * 2 + g
            for l_ in range(32):
                rhs = s3[64 * g:64 * g + 64, 0:127, l_] if l_ < 16 else s3[64 * g:64 * g + 64, 1:128, l_ - 16]
                sc.op("pe", lambda e, l_=l_, rhs=rhs, slot=slot, kv=kv, g=g: e.matmul(C.PS[0:64, 6 - g, slot * 128:slot * 128 + 127], lhsT=w1[64 * g:64 * g + 64, kv, l_, :], rhs=rhs,
                                                                                 start=(l_ == 0), stop=(l_ == 31)),
                      reads=r_w1, writes=[C.pr[6 - g]], count=(l_ == 31))
        for g in range(2):
            slot = kv * 2 + g
            sc.op("act", lambda e, slot=slot, kv=kv, g=g: e.activation(out=gh[:, slot, 0:127], in_=C.PS[0:64, 6 - g, slot * 128:slot * 128 + 127], func=AF.Gelu_apprx_tanh,
                                                                 bias=cbias[:, kv:kv + 1]),
                  reads=[C.pr[6 - g]] + RC, writes=RC)
    if DBG_STOP == 23:
        sc.barrier()
        return
    for kv in range(2):
        for g in range(2):
            sc.op("pe", lambda e, kv=kv, g=g: e.matmul(C.PS[0:127, 7, 128 + (kv * 2 + g) * 64:128 + (kv * 2 + g) * 64 + 64], lhsT=gh[:, kv * 2 + g, 0:127], rhs=w2[:, kv, :],
                                                     start=True, stop=True),
                  reads=RC + [r_w2], writes=[C.pr[7]], count=True)
    kc_ps = C.PS[0:127, 7, 128:256].rearrange("p (g d) -> p g d", d=64)
    vc_ps = C.PS[0:127, 7, 256:384].rearrange("p (g d) -> p g d", d=64)
    sc.op("act", lambda e: e.activation(out=Vc[0:127, :, 0:64], in_=vc_ps, func=AF.Copy), reads=[C.pr[7]] + RC, writes=RC)
    if DBG_STOP == 24:
        sc.barrier()
        return
    cb_ = cosc[0:127, :].unsqueeze(1).to_broadcast([127, 2, 32])
    sb_ = sinc[0:127, :].unsqueeze(1).to_broadcast([127, 2, 32])
    r_ct = Res("ectmp")
    rope_ops(C, kc_ps, rkc[0:127], cb_, sb_, [x[0:127] for x in ctmp], [C.pr[7], r_cst[8]], r_ct, rc)
    if DBG_STOP == 25:
        sc.barrier()
        return
    sc.op("act", lambda e: e.activation(out=kcA[0:127, :], in_=rkc[0:127].rearrange("p g d -> p (g d)"), func=AF.Copy), reads=RC, writes=RC)
    if DBG_STOP == 26:
        sc.barrier()
        return
    sc.op("pe", lambda e: e.transpose(out=psb(C, 6)[:, 0:127], in_=kcA[0:127, :], identity=C.identb[0:127, 0:127]), reads=RC + [C.r_const], writes=[C.pr[6]])
    sc.op("act", lambda e: e.activation(out=kcmpT[:, 0:127], in_=psb(C, 6)[:, 0:127], func=AF.Copy), reads=[C.pr[6]], writes=RC)
    sc.barrier()
    if DBG_STOP == 2:
        return

    r_P = [Res(f"eP{k}") for k in range(NP)]
    r_SB1 = [Res("eSB1a"), Res("eSB1b")]
    r_SB2 = [Res("eSB2a"), Res("eSB2b")]
    r_o = [Res("eo0"), Res("eo1")]
    r_ot = [Res("eot0"), Res("eot1")]
    r_ob = [Res("eob0"), Res("eob1")]
    r_sm = [Res("esm0"), Res("esm1")]
    r_imp = [Res("eimp0"), Res("eimp1")]
    r_vs = [Res("evs0"), Res("evs1")]
    oacc2 = [oacc, al([128, 4, 64], F32)]
    otmp2 = [otmp, al([128, 4, 64], F32)]
    sm42 = [sm4, al([128, 16, 4], F32)]
    impn2 = [impn, al([128, 4, 32], F32)]
    vsel2 = [vsel, al([128, 32], F32)]
    top82 = [top8, al([128, 8], F32)]
    pc = [0]
    BANK_SC = (0, 1)
    B_C, B_T = 2, 5
    B_Sg, B_Wg = (3, 6), (4, 7)
    scn = [0]

    def score_tile(main_lhsT, main_rhs, extra, npart, rds):
        b = BANK_SC[scn[0] % 2]
        scn[0] += 1
        mm = [(main_lhsT, main_rhs)] + extra
        for k_, (lt, rh) in enumerate(mm):
            sc.op("pe", lambda e, lt=lt, rh=rh, k_=k_: e.matmul(C.PS[0:npart, b, :], lhsT=lt, rhs=rh, start=(k_ == 0), stop=(k_ == len(mm) - 1)),
                  reads=rds, writes=[C.pr[b]], count=(k_ == len(mm) - 1))
        k = pc[0] % NP
        pc[0] += 1
        sc.op("act", lambda e: e.activation(out=Pt[k][0:npart, :], in_=C.PS[0:npart, b, :], func=AF.Exp), reads=[C.pr[b]], writes=[r_P[k]])
        return Pt[k], r_P[k]

    def chain(t, g):
        tsl = slice(t * 128, (t + 1) * 128)
        gp = slice(64 * g, 64 * g + 64)
        rhs_q = qT[gp, :, tsl]
        s1 = g
        B_S, B_W = B_Sg[g], B_Wg[g]
        oacc_, otmp_, sm4_, impn_, vsel_, top8_ = oacc2[g], otmp2[g], sm42[g], impn2[g], vsel2[g], top82[g]
        zc = slice(g * 256, g * 256 + 128)
        zc2 = slice(g * 256 + 128, g * 256 + 256)
        r0 = 64 * g
        zin = Zall[:, t, 0:36] if g == 0 else Zall[:, t, 36:136]
        zrows = 36 if g == 0 else 100
        RS = slice(r0, r0 + 36)
        sc.op("pe", lambda e: e.transpose(out=psb(C, B_T)[0:zrows, zc], in_=zin, identity=C.identb[:]), reads=[C.r_const], writes=[C.pr[B_T]])
        sc.op("dve", lambda e: e.tensor_tensor(out=SB1[s1][RS, :].rearrange("p (h t) -> p h t", t=128), in0=psb(C, B_T)[RS, zc].unsqueeze(1).to_broadcast([36, 4, 128]),
                                              in1=BM[RS, :].rearrange("p (h t) -> p h t", t=128), op=ALU.mult),
              reads=[C.pr[B_T], r_cst[5]], writes=[r_SB1[s1]])
        yield
        bw0 = 127 - 8 * t
        Pc, rPc = score_tile(kcmpT[gp, 0:127], rhs_q, [(EB[RS, 0:127], SB1[s1][RS, :]), (Bwin[r0:r0 + 9, bw0:bw0 + 127], Mpat[r0:r0 + 9, :])], 127,
                             [r_SB1[s1], r_cst[2], r_cst[6], r_cst[7]])
        yield
        for h in range(4):
            sc.op("pe", lambda e, h=h: e.matmul(C.PS[:, B_C, h * 97:(h + 1) * 97], lhsT=Pc[0:127, h * 128:(h + 1) * 128], rhs=Vc[0:127, g, :], start=True, stop=True),
                  reads=[rPc], writes=[C.pr[B_C]], count=(h == 3))
        accc = C.PS[:, B_C, 0:388].rearrange("p (h c) -> p h c", c=97)
        gt = gates[:, t, 12 * g:12 * g + 12].rearrange("p (h b) -> p h b", b=3)
        dc, fc = sm4_[:, 0, :], sm4_[:, 1, :]
        sc.op("dve", lambda e: e.tensor_scalar(out=dc, in0=accc[:, :, 64], scalar1=1e-30, scalar2=None, op0=ALU.add), reads=[C.pr[B_C]], writes=[r_sm[g]])
        sc.op("dve", lambda e: e.reciprocal(out=dc, in_=dc), reads=[r_sm[g]], writes=[r_sm[g]])
        sc.op("dve", lambda e: e.tensor_tensor(out=fc, in0=dc, in1=gt[:, :, 0], op=ALU.mult), reads=[r_sm[g]], writes=[r_sm[g]])
        sc.op("dve", lambda e: e.tensor_tensor(out=oacc_, in0=accc[:, :, 0:64], in1=fc.unsqueeze(2).to_broadcast([128, 4, 64]), op=ALU.mult),
              reads=[C.pr[B_C], r_sm[g]], writes=[r_o[g]])
        sc.op("dve", lambda e: e.tensor_tensor(out=impn_, in0=accc[:, :, 65:97], in1=dc.unsqueeze(2).to_broadcast([128, 4, 32]), op=ALU.mult),
              reads=[C.pr[B_C], r_sm[g]], writes=[r_imp[g]])
        yield
        sc.op("dve", lambda e: e.tensor_reduce(out=vsel_, in_=impn_.rearrange("p h j -> p j h"), axis=AX.X, op=ALU.add), reads=[r_imp[g]], writes=[r_vs[g]])
        sc.op("dve", lambda e: e.tensor_tensor(out=vsel_, in0=vsel_, in1=Fpos[:, t, :], op=ALU.max), reads=[r_vs[g], r_cst[0]], writes=[r_vs[g]])
        sc.op("dve", lambda e: e.tensor_tensor(out=vsel_, in0=vsel_, in1=Fneg[:, t, :], op=ALU.add), reads=[r_vs[g], r_cst[1]], writes=[r_vs[g]])
        sc.op("dve", lambda e: e.max(out=top8_, in_=vsel_), reads=[r_vs[g]], writes=[r_imp[g]])
        sc.op("dve", lambda e: e.tensor_scalar(out=vsel_, in0=vsel_, scalar1=top8_[:, 7:8], scalar2=1.0, op0=ALU.is_ge, op1=ALU.subtract), reads=[r_vs[g], r_imp[g]], writes=[r_vs[g]])
        zsel = Zall[:, t, 0:32] if g == 0 else Zall[:, t, 100:132]
        sc.op("dve", lambda e: e.tensor_scalar(out=zsel, in0=vsel_, scalar1=-NEG, scalar2=None, op0=ALU.mult), reads=[r_vs[g]], writes=[r_SB2[s1]])
        yield
        sc.op("pe", lambda e: e.transpose(out=psb(C, B_T)[0:zrows, zc2], in_=zin, identity=C.identb[:]), reads=[r_SB2[s1], C.r_const], writes=[C.pr[B_T]])
        sc.op("dve", lambda e: e.tensor_tensor(out=SB2[s1][RS, :].rearrange("p (h t) -> p h t", t=128), in0=psb(C, B_T)[RS, zc2].unsqueeze(1).to_broadcast([36, 4, 128]),
                                              in1=BM[RS, :].rearrange("p (h t) -> p h t", t=128), op=ALU.mult),
              reads=[C.pr[B_T], r_cst[5]], writes=[r_SB2[s1]])
        yield
        jobs = []
        kts = list(range(max(0, t - 4), t + 1))
        for n_, kt in enumerate(kts):
            ksl = slice(kt * 128, (kt + 1) * 128)
            extra = [(EB[RS, ksl], SB1[s1][RS, :])]
            if kt == t:
                extra.append((C.identb[:], triB[:, :]))
            elif kt == t - 4:
                extra.append((C.identb[:], bandB[:, :]))
            jobs.append((kwT[gp, ksl], extra, [r_SB1[s1], r_cst[2], r_cst[3], r_cst[4], C.r_const], B_W, Vw, kt, n_ == 0, n_ == len(kts) - 1))
        for kt in range(t + 1):
            ksl = slice(kt * 128, (kt + 1) * 128)
            extra = [(EB[RS, ksl], SB2[s1][RS, :])]
            if kt == t:
                extra.append((C.identb[:], triB[:, :]))
            jobs.append((ksT[gp, ksl], extra, [r_SB2[s1], r_cst[2], r_cst[3], C.r_const], B_S, Vs, kt, kt == 0, kt == t))
        prev = None
        for n_ in range(len(jobs) + 1):
            cur = None
            if n_ < len(jobs):
                lt, extra, rds, bk, Vt, kt, first, last = jobs[n_]
                P_, rP_ = score_tile(lt, rhs_q, extra, 128, rds)
                cur = (P_, rP_, bk, Vt, kt, first, last)
            if prev is not None:
                P_, rP_, bk, Vt, kt, first, last = prev
                for h in range(4):
                    sc.op("pe", lambda e, h=h, P_=P_, bk=bk, Vt=Vt, kt=kt, first=first, last=last: e.matmul(
                        C.PS[:, bk, h * 65:(h + 1) * 65], lhsT=P_[:, h * 128:(h + 1) * 128], rhs=Vt[:, kt, g, :], start=(first and h == 0), stop=(last and h == 3)),
                          reads=[rP_], writes=[C.pr[bk]], count=(h == 3))
            prev = cur
            yield
        for (bk, gi_, dd, ff) in ((B_W, 2, sm4_[:, 2, :], sm4_[:, 3, :]), (B_S, 1, sm4_[:, 4, :], sm4_[:, 5, :])):
            acc = C.PS[:, bk, 0:260].rearrange("p (h c) -> p h c", c=65)
            sc.op("dve", lambda e, acc=acc, dd=dd: e.reciprocal(out=dd, in_=acc[:, :, 64]), reads=[C.pr[bk]], writes=[r_sm[g]])
            sc.op("dve", lambda e, dd=dd, ff=ff, gi_=gi_: e.tensor_tensor(out=ff, in0=dd, in1=gt[:, :, gi_], op=ALU.mult), reads=[r_sm[g]], writes=[r_sm[g]])
            sc.op("dve", lambda e, acc=acc, ff=ff: e.tensor_tensor(out=otmp_, in0=acc[:, :, 0:64], in1=ff.unsqueeze(2).to_broadcast([128, 4, 64]), op=ALU.mult),
                  reads=[C.pr[bk], r_sm[g]], writes=[r_ot[g]])
            sc.op("pool", lambda e: e.tensor_tensor(out=oacc_, in0=oacc_, in1=otmp_, op=ALU.add), reads=[r_ot[g], r_o[g]], writes=[r_o[g]])
            yield
        sc.op("act", lambda e: e.activation(out=ob[:, 256 * g:256 * g + 256], in_=oacc_.rearrange("p h d -> p (h d)"), func=AF.Copy), reads=[r_o[g]], writes=[r_ob[g]])

    def finish_tile(t):
        tsl = slice(t * 128, (t + 1) * 128)
        for c in range(4):
            sc.op("pe", lambda e, c=c: e.transpose(out=psb(C, B_T)[:, 512 + c * 128:512 + (c + 1) * 128], in_=ob[:, c * 128:(c + 1) * 128], identity=C.identb[:]),
                  reads=r_ob + [C.r_const], writes=[C.pr[B_T]], count=(c == 3))
        sc.op("act", lambda e: e.activation(out=mixT[:, 4:8, tsl], in_=psb(C, B_T)[:, 512:1024].rearrange("p (c t) -> p c t", t=128), func=AF.Copy),
              reads=[C.pr[B_T]], writes=[r_mix[t]])

    active = {0: (0, chain(0, 0)), 1: (0, chain(0, 1))}
    done = [0] * NT
    for _ in range(9):
        next(active[0][1])
    while active:
        for g in (0, 1):
            if g not in active:
                continue
            t, gen = active[g]
            try:
                next(gen)
            except StopIteration:
                done[t] += 1
                if done[t] == 2:
                    finish_tile(t)
                if t + 1 < NT:
                    active[g] = (t + 1, chain(t + 1, g))
                else:
                    del active[g]
    if DBG_STOP == 3:
        return
    out_proj(C, mixT, r_mix, WO, r_WO)
```

```python
import numpy as np
import concourse.bass as bass
import concourse.mybir as mybir
from concourse.bass_utils import run_bass_kernel_spmd

F32 = mybir.dt.float32
BF16 = mybir.dt.bfloat16
AF = mybir.ActivationFunctionType
ALU = mybir.AluOpType
AX = mybir.AxisListType

D = 1024
S = 2048
NT = S // 128
FF = 2816
NFC = FF // 128
DEPTH = 4
RMS_EPS = 1e-6


class Res:
    __slots__ = ("name", "w", "r", "excl")

    def __init__(self, name="", excl=False):
        self.name = name
        self.w = None
        self.r = {}
        self.excl = excl


class Sched:
    ENG = ("pe", "act", "dve", "pool", "sp")

    def __init__(self, nc, dma_keys=()):
        self.nc = nc
        self.dkeys = []
        self.e = dict(pe=nc.tensor, act=nc.scalar, dve=nc.vector, pool=nc.gpsimd, sp=nc.sync)
        self.sem = {}
        self.cnt = {}
        for k in self.ENG + tuple(dma_keys):
            self.sem[k] = nc.alloc_semaphore("s_" + k)
            self.cnt[k] = 0
        self.seen = {k: {} for k in self.ENG}
        self.nins = 0

    def _wait(self, eng, key, val):
        if val <= self.seen[eng].get(key, 0):
            return
        self.e[eng].wait_ge(self.sem[key], val)
        self.seen[eng][key] = val
        self.nins += 1

    def _deps(self, eng, reads, writes, async_=False):
        need = {}

        def add(t):
            if t is None:
                return
            k, v = t
            if need.get(k, 0) < v:
                need[k] = v

        for r in reads:
            add(r.w)
            if r.excl:
                for k, v in r.r.items():
                    if k != eng:
                        add((k, v))
        for w in writes:
            if async_ or (w.w is not None and w.w[0] != eng):
                add(w.w)
            for k, v in w.r.items():
                if async_ or k != eng:
                    add((k, v))
        for k, v in need.items():
            if k == eng and eng == "pe":
                continue
            self._wait(eng, k, v)

    def op(self, eng, fn, reads=(), writes=(), count=True):
        self._deps(eng, reads, writes)
        ins = fn(self.e[eng])
        self.nins += 1
        if count:
            self.cnt[eng] += 1
            ins.then_inc(self.sem[eng], 1)
            v = self.cnt[eng]
        else:
            v = self.cnt[eng] + 1
        for r in reads:
            if r.r.get(eng, 0) < v:
                r.r[eng] = v
        for w in writes:
            w.w = (eng, v)
            w.r = {}
        return ins

    def dma(self, q, dkey, out, in_, reads=(), writes=()):
        res0 = writes[0] if len(writes) > 0 else reads[0]
        dkey = "d_" + res0.name
        if dkey not in self.sem:
            self.sem[dkey] = self.nc.alloc_semaphore("s_" + dkey)
            self.cnt[dkey] = 0
            self.dkeys.append(dkey)
        self._deps(q, reads, writes, async_=True)
        ins = self.e[q].dma_start(out=out, in_=in_)
        self.nins += 1
        self.cnt[dkey] += 16
        ins.then_inc(self.sem[dkey], 16)
        v = self.cnt[dkey]
        for r in reads:
            if r.r.get(dkey, 0) < v:
                r.r[dkey] = v
        for w in writes:
            w.w = (dkey, v)
            w.r = {}
        return ins

    def barrier(self):
        snap = dict(self.cnt)
        for eng in self.ENG:
            for k, v in snap.items():
                if v > 0 and k != eng:
                    self._wait(eng, k, v)
            if eng != "pe" and snap[eng] > 0:
                self._wait(eng, eng, snap[eng])

    def finish(self, keys=None):
        for k in self.dkeys:
            if self.cnt[k] > 0:
                self._wait("sp", k, self.cnt[k])


class Ctx:
    pass


ARENA = 140 * 1024


def aview(C, off, shape, dtype):
    esz = 4 if dtype == F32 else 2
    n = 1
    for d in shape[1:]:
        n *= d
    nb = n * esz
    assert off % 4 == 0 and nb % 4 == 0 and off + nb <= ARENA, (off, shape)
    v = C.AR[0:shape[0], off // 4:(off + nb) // 4]
    if dtype != F32:
        v = v.bitcast(dtype)
    if len(shape) == 2:
        return v
    names = ["a", "b", "c", "d"][:len(shape) - 1]
    pat = "p (" + " ".join(names) + ") -> p " + " ".join(names)
    return v.rearrange(pat, **{nm: d for nm, d in zip(names[1:], shape[2:])})


def build(stages, n_layers_inputs=True):
    nc = bass.Bass("TRN2", target_bir_lowering=False)
    C = Ctx()
    C.nc = nc
    sc = Sched(nc)
    C.sc = sc

    def din(name, shape):
        return nc.dram_tensor(name, list(shape), F32, kind="ExternalInput").ap()

    C.x_in = din("x", [S, D])
    C.gT_in = din("gT_h", [128, 13 * 8])
    C.fin_g_in = din("fin_g_h", [128, D])
    C.ident_in = din("ident_h", [128, 128])
    C.wg = din("ffn_w_gate", [DEPTH, 2, D, FF])
    C.wu = din("ffn_w_up", [DEPTH, 2, D, FF])
    C.wd = din("ffn_w_down", [DEPTH, 2, FF, D])
    C.y_out = nc.dram_tensor("y", [S, D], F32, kind="ExternalOutput").ap()
    C.ev_w_in = din("ev_w_in", [2, D, 2328])
    C.ev_w_out = din("ev_w_out", [2, D, D])
    C.nsa_w1 = din("nsa_cmp_w1", [2, 2, 2048, 64])
    C.nsa_w2 = din("nsa_cmp_w2", [2, 2, 64, 64])
    C.nsa_pe_in = din("nsa_pe_h", [2, 128, 2, 32])
    C.rope_in = din("rope_h", [2, 128, NT, 32])
    C.ropec_in = din("ropec_h", [2, 128, 32])
    C.gm_ws_in = din("gm_ws_h", [2, 128, 8, 128])
    C.gm_b_in = din("gm_b_h", [2, 128, 8])
    C.fsel_in = din("fsel_h", [2, 128, NT, 32])
    C.eb_in = din("eb_h", [100, S])
    C.trib_in = din("trib_h", [128, 512])
    C.bandb_in = din("bandb_h", [128, 512])
    C.bm_in = din("bm_h", [100, 512])
    C.mpat_in = din("mpat_h", [73, 512])
    C.bwin_in = din("bwin_h", [73, 256])
    C.ovl_in = din("ovl_h", [127, 32])
    C.od_w_in = din("od_w_in", [2, D, D])
    C.od_w_out = din("od_w_out", [2, D, D])
    C.pool_w = din("pool_w", [2, 4, 128, 128])
    C.s5_w_glu = din("s5_w_glu", [2, 512, 1024])
    C.psc_in = din("psc_h", [128, 8])
    C.s5lam_in = din("s5lam_h", [128, 3, 32])
    C.s5d_in = din("s5d_h", [128, 8])
    C.inv16_in = din("inv16_h", [128, 16])
    C.triT_in = din("triT_h", [128, 128])
    C.s5b = din("s5b_h", [2, 4, 128, 16, 16])

    C.X = nc.alloc_sbuf_tensor("X", [128, NT, D], F32)
    C.xr = [Res(f"x{t}") for t in range(NT)]
    C.identf = nc.alloc_sbuf_tensor("identf", [128, 128], F32)
    C.identb = nc.alloc_sbuf_tensor("identb", [128, 128], BF16)
    C.gT = nc.alloc_sbuf_tensor("gT", [128, 13 * 8], F32)
    C.cm05 = nc.alloc_sbuf_tensor("cm05", [128, 1], F32)
    C.stat = nc.alloc_sbuf_tensor("stat", [128, 64], F32)
    C.r_const = Res("const")
    C.PS = nc.alloc_psum_tensor("PS", [128, 8, 512], F32)
    C.pr = [Res(f"ps{b}", excl=True) for b in range(8)]

    C.AR = nc.alloc_sbuf_tensor("AR", [128, ARENA // 4], F32)
    C.hT = aview(C, 0, [128, 8, 1024], BF16)
    C.r_hT = [Res(f"hT{t}") for t in range(8)]
    C.hT2 = aview(C, 116736, [128, 8, 1024], BF16)
    C.r_hT2 = [Res(f"hTb{t}") for t in range(8)]
    C.aT = aview(C, 16384, [128, NFC, 1024], BF16)
    C.r_aT = [[Res(f"aT{f}_{b}") for b in range(2)] for f in range(NFC)]
    C.NWS = 2
    C.wgb = [aview(C, 61440 + 8192 * i, [128, 8, 512], BF16) for i in range(C.NWS)]
    C.wub = [aview(C, 77824 + 8192 * i, [128, 8, 512], BF16) for i in range(C.NWS)]
    C.r_wg = [Res(f"wg{i}") for i in range(C.NWS)]
    C.r_wu = [Res(f"wu{i}") for i in range(C.NWS)]
    C.NWD = 8
    C.wdb = [aview(C, 94208 + 2048 * i, [128, 2, 512], BF16) for i in range(4)] + \
            [aview(C, 133120 + 2048 * i, [128, 2, 512], BF16) for i in range(4)]
    C.r_wd = [Res(f"wd{i}") for i in range(C.NWD)]
    C.xs = [aview(C, 102400 + 4096 * i, [128, D], F32) for i in range(2)]
    C.r_xs = [Res(f"xs{i}") for i in range(2)]
    C.sg = [aview(C, 110592 + 2048 * i, [128, 512], F32) for i in range(2)]
    C.r_sg = [Res(f"sg{i}") for i in range(2)]
    C.junk = aview(C, 114688, [128, D], BF16)
    C.r_junk = Res("junk")
    C.ffn_xs, C.ffn_junk = C.xs, C.junk
    C.psc = nc.alloc_sbuf_tensor("psc", [128, 8], F32)
    C.s5lam = nc.alloc_sbuf_tensor("s5lam", [128, 3, 32], F32)
    C.s5d = nc.alloc_sbuf_tensor("s5d", [128, 8], F32)
    C.inv16 = nc.alloc_sbuf_tensor("inv16", [128, 16], F32)
    C.triT = nc.alloc_sbuf_tensor("triT", [128, 128], BF16)
    C.halfpi = nc.alloc_sbuf_tensor("halfpi", [128, 1], F32)
    C.c05 = nc.alloc_sbuf_tensor("c05", [128, 1], F32)
    C.r_stat = [Res(f"stat{i}") for i in range(4)]
    C.ctr = dict(wgu=0, wd=0, xs=0, sg=0, st=0, tp=0, gu=0)

    r_c1, r_c2 = Res("c_ident"), Res("c_gT")
    sc.dma("sp", "dio", C.identf[:], C.ident_in, writes=[r_c1])
    sc.dma("sp", "dio", C.gT[:], C.gT_in, writes=[r_c2])
    r_cs = [Res(f"c_s{k}") for k in range(5)]
    sc.dma("sp", "dio", C.psc[:], C.psc_in, writes=[r_cs[0]])
    sc.dma("sp", "dio", C.s5lam[:], C.s5lam_in, writes=[r_cs[1]])
    sc.dma("sp", "dio", C.s5d[:], C.s5d_in, writes=[r_cs[2]])
    sc.dma("sp", "dio", C.inv16[:], C.inv16_in, writes=[r_cs[3]])
    sc.dma("pool", "dio", C.triT[:], C.triT_in, writes=[r_cs[4]])
    sc.op("dve", lambda e: e.memset(C.halfpi[:], float(np.pi / 2)), reads=r_cs, writes=[C.r_const])
    sc.op("dve", lambda e: e.memset(C.c05[:], 0.5), writes=[C.r_const])
    sc.op("dve", lambda e: e.memset(C.cm05[:], -0.5), reads=[r_c1, r_c2], writes=[C.r_const])
    sc.op("dve", lambda e: e.tensor_copy(out=C.identb[:], in_=C.identf[:]), reads=[C.r_const], writes=[C.r_const])
    for t in range(NT):
        q = "sp" if t % 2 == 0 else "act"
        sc.dma(q, "dx", C.X[:, t, :], C.x_in[t * 128:(t + 1) * 128, :], writes=[C.xr[t]])

    prev_kind = None
    for st in stages:
        if not (prev_kind == "ffn" and st[0] in ("ffn", "final")):
            sc.barrier()
        prev_kind = st[0]
        C.xs, C.junk = C.ffn_xs, C.ffn_junk
        if st[0] == "ffn":
            ffn_block(C, st[1], st[2])
        elif st[0] == "mix" and st[1] % 2 == 1:
            odd_mixer(C, st[1])
        elif st[0] == "mix":
            even_mixer(C, st[1])
        elif st[0] == "final":
            final_norm(C)
        else:
            raise ValueError(st)

    for t in range(NT):
        q = "sp" if t % 2 == 0 else "act"
        sc.dma(q, "dx", C.y_out[t * 128:(t + 1) * 128, :], C.X[:, t, :], reads=[C.xr[t]])
    sc.finish()
    return nc, C


def rstd_of_tile(C, t):
    sc = C.sc
    i = C.ctr["st"] % 4
    C.ctr["st"] += 1
    r = C.r_stat[i]
    ss = C.stat[:, 2 * i:2 * i + 1]
    rs = C.stat[:, 2 * i + 1:2 * i + 2]
    sc.op("act", lambda e: e.activation(out=C.junk[:], in_=C.X[:, t, :], func=AF.Square, accum_out=ss),
          reads=[C.xr[t], C.r_junk], writes=[C.r_junk, r])
    sc.op("dve", lambda e: e.tensor_scalar(out=ss, in0=ss, scalar1=1.0 / D, scalar2=RMS_EPS, op0=ALU.mult, op1=ALU.add),
          reads=[r], writes=[r])
    sc.op("pool", lambda e: e.tensor_tensor(out=rs, in0=ss, in1=C.cm05[:], op=ALU.pow),
          reads=[r, C.r_const], writes=[r])
    return rs, r


def norm_transpose(C, t, gcol, dst, r_dst):
    st = norm_part1(C, t)
    norm_part2(C, st, gcol, dst, r_dst)


def norm_part1(C, t):
    sc = C.sc
    rs, r = rstd_of_tile(C, t)
    i = C.ctr["xs"] % 2
    C.ctr["xs"] += 1
    xs, rxs = C.xs[i], C.r_xs[i]
    sc.op("act", lambda e: e.activation(out=xs, in_=C.X[:, t, :], func=AF.Copy, scale=rs),
          reads=[C.xr[t], r], writes=[rxs])
    return xs, rxs


def norm_part2(C, st, gcol, dst, r_dst):
    sc = C.sc
    xs, rxs = st
    j = C.ctr["tp"] % 2
    C.ctr["tp"] += 1
    b0 = 4 + 2 * j
    pst = C.PS[:, b0:b0 + 2, :].rearrange("p b (k t) -> p (b k) t", t=128)
    for kc in range(8):
        sc.op("pe", lambda e, kc=kc: e.transpose(out=pst[:, kc, :], in_=xs[:, kc * 128:(kc + 1) * 128], identity=C.identf[:]),
              reads=[rxs, C.r_const], writes=[C.pr[b0], C.pr[b0 + 1]], count=(kc == 7))
    sc.op("dve", lambda e: e.tensor_tensor(out=dst, in0=pst, in1=gcol.unsqueeze(2).to_broadcast([128, 8, 128]), op=ALU.mult),
          reads=[C.pr[b0], C.pr[b0 + 1], C.r_const], writes=[r_dst])


def ffn_block(C, l, j):
    sc = C.sc
    nc = C.nc
    ni = l * 3 + (0 if j == 0 else 2)
    gcol = C.gT[:, ni * 8:(ni + 1) * 8]
    wgv = C.wg[l, j].rearrange("(kc p) f -> p kc f", p=128)
    wuv = C.wu[l, j].rearrange("(kc p) f -> p kc f", p=128)
    wdv = C.wd[l, j].rearrange("(fc p) d -> p fc d", p=128)
    groups = [(g * 512, 512) for g in range(5)] + [(2560, 256)]
    hTs = [C.hT, C.hT2]
    r_hTs = [C.r_hT, C.r_hT2]

    def phase_a(tg):
        hT_, r_hT_ = hTs[tg % 2], r_hTs[tg % 2]
        for tt in range(8):
            st = norm_part1(C, tg * 8 + tt)
            yield
            norm_part2(C, st, gcol, hT_[:, :, tt * 128:(tt + 1) * 128], r_hT_[tt])
            yield

    def pass1(tg):
        hT_, r_hT_ = hTs[tg % 2], r_hTs[tg % 2]
        for (f0, fw) in groups:
            s = C.ctr["wgu"] % C.NWS
            C.ctr["wgu"] += 1
            sc.dma("pool", "dw", C.wgb[s][:, :, 0:fw], wgv[:, :, f0:f0 + fw], writes=[C.r_wg[s]])
            sc.dma("pool", "dw", C.wub[s][:, :, 0:fw], wuv[:, :, f0:f0 + fw], writes=[C.r_wu[s]])
            for tb in range(2):
                for fci in range(fw // 128):
                    fc = f0 // 128 + fci
                    gi = C.ctr["gu"] % 2
                    C.ctr["gu"] += 1
                    bG, bU = gi, 2 + gi
                    hr = r_hT_[tb * 4:(tb + 1) * 4]
                    for kc in range(8):
                        sc.op("pe", lambda e, kc=kc: e.matmul(C.PS[:, bG, :], lhsT=C.wgb[s][:, kc, fci * 128:(fci + 1) * 128],
                                                            rhs=hT_[:, kc, tb * 512:(tb + 1) * 512], start=(kc == 0), stop=(kc == 7)),
                              reads=[C.r_wg[s]] + hr, writes=[C.pr[bG]], count=(kc == 7))
                    for kc in range(8):
                        sc.op("pe", lambda e, kc=kc: e.matmul(C.PS[:, bU, :], lhsT=C.wub[s][:, kc, fci * 128:(fci + 1) * 128],
                                                            rhs=hT_[:, kc, tb * 512:(tb + 1) * 512], start=(kc == 0), stop=(kc == 7)),
                              reads=[C.r_wu[s]] + hr, writes=[C.pr[bU]], count=(kc == 7))
                    si = C.ctr["sg"] % 2
                    C.ctr["sg"] += 1
                    sc.op("act", lambda e: e.activation(out=C.sg[si], in_=C.PS[:, bG, :], func=AF.Silu),
                          reads=[C.pr[bG]], writes=[C.r_sg[si]])
                    sc.op("dve", lambda e: e.tensor_tensor(out=C.aT[:, fc, tb * 512:(tb + 1) * 512], in0=C.sg[si], in1=C.PS[:, bU, :], op=ALU.mult),
                          reads=[C.r_sg[si], C.pr[bU]], writes=[C.r_aT[fc][tb]])
                    yield

    side0 = phase_a(0)
    for _ in range(8):
        next(side0)
    for tg in range(2):
        main = pass1(tg)
        side = phase_a(tg + 1) if tg + 1 < 2 else iter(())
        nmain = 0
        for _ in main:
            nmain += 1
            if tg == 0 and nmain <= 4:
                next(side0, None)
                next(side0, None)
            elif nmain % 5 in (0, 3):
                next(side, None)
        for _ in side:
            pass
        for db in range(2):
            for fp in range(NFC // 2):
                s = C.ctr["wd"] % C.NWD
                C.ctr["wd"] += 1
                sc.dma("pool", "dw", C.wdb[s], wdv[:, 2 * fp:2 * fp + 2, db * 512:(db + 1) * 512], writes=[C.r_wd[s]])
                for fi in range(2):
                    fc = 2 * fp + fi
                    for tt in range(8):
                        sc.op("pe", lambda e, fi=fi, tt=tt, fc=fc: e.matmul(C.PS[:, tt, :], lhsT=C.aT[:, fc, tt * 128:(tt + 1) * 128], rhs=C.wdb[s][:, fi, :],
                                                                          start=(fc == 0), stop=(fc == NFC - 1)),
                              reads=[C.r_wd[s], C.r_aT[fc][tt // 4]], writes=[C.pr[tt]], count=(fc == NFC - 1 or (fi == 1 and tt == 7)))
            for tt in range(8):
                t = tg * 8 + tt
                xv = C.X[:, t, db * 512:(db + 1) * 512]
                sc.op("dve", lambda e, tt=tt, xv=xv: e.scalar_tensor_tensor(out=xv, in0=C.PS[:, tt, :], scalar=0.5, in1=xv, op0=ALU.mult, op1=ALU.add),
                      reads=[C.pr[tt], C.xr[t]], writes=[C.xr[t]])


def final_norm(C):
    sc = C.sc
    fg = C.xs[0]
    sc.dma("sp", "dio", fg, C.fin_g_in, writes=[C.r_xs[0]])
    for t in range(NT):
        rs, r = rstd_of_tile(C, t)
        sc.op("dve", lambda e, t=t, rs=rs: e.scalar_tensor_tensor(out=C.X[:, t, :], in0=C.X[:, t, :], scalar=rs, in1=fg, op0=ALU.mult, op1=ALU.mult),
              reads=[C.xr[t], r, C.r_xs[0]], writes=[C.xr[t]])


def host_consts(inputs):
    nw = np.concatenate([inputs["norm_w"].reshape(12, D), inputs["final_norm_w"].reshape(1, D)], axis=0)
    gT = np.ascontiguousarray(nw.reshape(13, 8, 128).transpose(2, 0, 1).reshape(128, 13 * 8)).astype(np.float32)
    fin_g = np.ascontiguousarray(np.broadcast_to(inputs["final_norm_w"].reshape(1, D), (128, D))).astype(np.float32)
    f = lambda a: np.ascontiguousarray(a, dtype=np.float32)
    out = dict(gT_h=gT, fin_g_h=fin_g, ident_h=np.eye(128, dtype=np.float32))
    out["psc_h"] = f(inputs["pool_scale"].reshape(2, 4, 128).transpose(2, 0, 1).reshape(128, 8))
    def st_major(a):
        return a.reshape(2, 16, 2, 64).transpose(2, 3, 0, 1).reshape(128, 32)
    ldt = np.broadcast_to(inputs["s5_log_dt"][:, :, None], (2, 32, 64))
    out["s5lam_h"] = f(np.stack([st_major(inputs["s5_lam_re"]), st_major(inputs["s5_lam_im"]), st_major(ldt)], axis=1))
    def st_major_b(a):
        return a.reshape(2, 16, 2, 64, 16).transpose(0, 2, 3, 1, 4).reshape(2, 128, 16, 16)
    def st_major_c(a):
        return a.reshape(2, 16, 2, 16, 64).transpose(0, 2, 4, 1, 3).reshape(2, 128, 16, 16)
    out["s5b_h"] = f(np.stack([st_major_b(inputs["s5_b_re"]), st_major_b(inputs["s5_b_im"]),
                               st_major_c(inputs["s5_c_re"]), st_major_c(inputs["s5_c_im"])], axis=1))
    out["s5d_h"] = f(inputs["s5_d"].reshape(2, 4, 8, 16).transpose(2, 3, 0, 1).reshape(128, 8))
    out["inv16_h"] = f(np.broadcast_to(1.0 / np.arange(1, 17, dtype=np.float64)[None, :], (128, 16)))
    out["triT_h"] = f(np.triu(np.ones((128, 128))))
    out["gm_ws_h"] = f(inputs["gm_w_s"].transpose(0, 3, 1, 2))
    out["gm_b_h"] = f(inputs["gm_b"].transpose(0, 2, 1))
    pe = inputs["nsa_cmp_pe"]
    peT = pe.transpose(0, 3, 1, 2)
    out["nsa_pe_h"] = f(np.concatenate([peT, peT], axis=1))
    inv = 10000.0 ** (-np.arange(32, dtype=np.float64) / 32)
    pos = (np.arange(NT)[None, :] * 128 + np.arange(128)[:, None]).astype(np.float64)
    ang = (pos[:, :, None].astype(np.float32) * inv.astype(np.float32)[None, None, :]).astype(np.float32)
    out["rope_h"] = f(np.stack([np.cos(ang), np.sin(ang)], axis=0))
    posc = (np.arange(128) * 16 + 31).astype(np.float32)
    angc = (posc[:, None] * inv.astype(np.float32)[None, :]).astype(np.float32)
    out["ropec_h"] = f(np.stack([np.cos(angc), np.sin(angc)], axis=0))
    tl = np.arange(128)[:, None, None]
    ti = np.arange(NT)[None, :, None]
    jj = np.arange(32)[None, None, :]
    cur = (ti * 128 + tl) // 64
    forced = (jj == 0) | (jj == cur) | (jj == cur - 1)
    out["fsel_h"] = f(np.stack([np.where(forced, 1e4, 0.0), np.where(jj > cur, -1.0, 0.0)], axis=0))
    eb = np.zeros((36, S), np.float32)
    eb[np.arange(S) // 64, np.arange(S)] = 1.0
    eb[32:36, :] = 1.0
    eb2 = np.zeros((100, S), np.float32)
    eb2[0:36] = eb
    eb2[64:100] = eb
    out["eb_h"] = eb2
    sk = np.arange(128)[:, None]
    tq = np.arange(128)[None, :]
    tri = np.where(sk <= tq, 0.0, NEG).astype(np.float32)
    band = np.where(sk > tq, 0.0, NEG).astype(np.float32)
    out["trib_h"] = f(np.tile(tri, (1, 4)))
    out["bandb_h"] = f(np.tile(band, (1, 4)))
    bm = np.ones((36, 4, 128), np.float32)
    bm[32:36] = np.eye(4, dtype=np.float32)[:, :, None]
    bm2 = np.zeros((100, 512), np.float32)
    bm2[0:36] = bm.reshape(36, 512)
    bm2[64:100] = bm.reshape(36, 512)
    out["bm_h"] = bm2
    fq = np.floor((np.arange(128) - 31) / 16.0)
    mp = np.zeros((9, 128), np.float32)
    for r in range(8):
        mp[r] = np.where((r - 1) <= fq, 0.0, NEG)
    mp[8] = NEG
    mp2 = np.zeros((73, 512), np.float32)
    mp2[0:9] = np.tile(mp, (1, 4))
    mp2[64:73] = np.tile(mp, (1, 4))
    out["mpat_h"] = mp2
    bw = np.zeros((9, 256), np.float32)
    for c in range(256):
        cp = c - 127
        r = cp + 1
        if 0 <= r < 8:
            bw[r, c] = 1.0
        if cp >= 7:
            bw[8, c] = 1.0
    bw2 = np.zeros((73, 256), np.float32)
    bw2[0:9] = bw
    bw2[64:73] = bw
    out["bwin_h"] = bw2
    nn = np.arange(127)[:, None] * 16
    jb = np.arange(32)[None, :] * 64
    out["ovl_h"] = f(((nn < jb + 64) & (nn + 32 > jb)).astype(np.float32))
    return out


ALL_STAGES = []
for _l in range(DEPTH):
    ALL_STAGES += [("ffn", _l, 0), ("mix", _l), ("ffn", _l, 1)]
ALL_STAGES += [("final",)]


def host_inputs(inputs):
    shared = host_consts(inputs)
    for k in ("od_w_in", "od_w_out", "pool_w", "s5_w_glu", "ev_w_in", "ev_w_out", "nsa_cmp_w1", "nsa_cmp_w2"):
        shared[k] = np.asarray(inputs[k], np.float32)
    shared.update(ffn_w_gate=np.asarray(inputs["ffn_w_gate"], np.float32),
                  ffn_w_up=np.asarray(inputs["ffn_w_up"], np.float32),
                  ffn_w_down=np.asarray(inputs["ffn_w_down"], np.float32))
    return shared


def run(inputs, stages, core_ids, xs_per_core, trace=False):
    nc, C = build(stages)
    shared = host_inputs(inputs)
    in_maps = []
    for xc in xs_per_core:
        m = dict(shared)
        m["x"] = np.ascontiguousarray(xc, dtype=np.float32)
        in_maps.append(m)
    res = run_bass_kernel_spmd(nc, in_maps, core_ids=core_ids, trace=trace)
    return [r["y"] for r in res.results], res


def kernel(**inputs):
    x = np.asarray(inputs["x"], np.float32)
    ys, _ = run(inputs, ALL_STAGES, list(range(8)), [x[b] for b in range(8)])
    return np.stack(ys, axis=0).astype(np.float32)


KB = 1024
import os as _os
DBG_STOP = int(_os.environ.get('DBG_STOP', '0'))
POOL_WINDOWS = (2, 4, 8, 16)


def _cmul(C, eng_m, eng_a, ore, oim, are, aim, bre, bim, t, rd, wr):
    sc = C.sc
    sc.op(eng_m, lambda e: e.tensor_tensor(out=t[0], in0=are, in1=bre, op=ALU.mult), reads=rd, writes=wr)
    sc.op(eng_m, lambda e: e.tensor_tensor(out=t[1], in0=aim, in1=bim, op=ALU.mult), reads=rd, writes=wr)
    sc.op(eng_m, lambda e: e.tensor_tensor(out=t[2], in0=are, in1=bim, op=ALU.mult), reads=rd, writes=wr)
    sc.op(eng_m, lambda e: e.tensor_tensor(out=t[3], in0=aim, in1=bre, op=ALU.mult), reads=rd, writes=wr)
    sc.op(eng_a, lambda e: e.tensor_tensor(out=ore, in0=t[0], in1=t[1], op=ALU.subtract), reads=rd, writes=wr)
    sc.op(eng_a, lambda e: e.tensor_tensor(out=oim, in0=t[2], in1=t[3], op=ALU.add), reads=rd, writes=wr)


def odd_mixer(C, l):
    sc = C.sc
    nc = C.nc
    i = l // 2
    gcol = C.gT[:, (l * 3 + 1) * 8:(l * 3 + 2) * 8]
    mixT = aview(C, 0, [128, 8, S], BF16)
    uT = aview(C, 32 * KB, [128, 4, S], BF16)
    ygT = aview(C, 48 * KB, [128, 4, S], BF16)
    r_mix = [[Res(f"mix{m}_{b}") for b in range(4)] for m in range(8)]
    r_u = [[Res(f"u{m}_{b}") for b in range(4)] for m in range(4)]

    hT = aview(C, 64 * KB, [128, 8, S], BF16)
    r_h = [Res(f"oh{t}") for t in range(NT)]
    W1 = aview(C, 96 * KB, [128, 8, 1024], BF16)
    r_W1 = [Res("oW1a"), Res("oW1b")]
    zb = aview(C, 112 * KB, [128, S], F32)
    pa = aview(C, 120 * KB, [128, S], F32)
    pb = aview(C, 128 * KB, [128, S], F32)
    pl = aview(C, 136 * KB, [128, S], BF16)
    r_z, r_pa, r_pb, r_pl = Res("oz"), Res("opa"), Res("opb"), Res("opl")
    C.xs = [aview(C, 48 * KB + 4096 * k, [128, D], F32) for k in range(2)]
    C.junk = aview(C, 56 * KB, [128, D], BF16)
    PW = aview(C, 58 * KB, [128, 4, 128], BF16)
    r_PW = Res("oPW")
    w_in = C.od_w_in[i].rearrange("(kc p) f -> p kc f", p=128)
    sc.dma("pool", "dw", W1[:, 0:4, :], w_in[:, 0:4, :], writes=[r_W1[0]])
    sc.dma("pool", "dw", W1[:, 4:8, :], w_in[:, 4:8, :], writes=[r_W1[1]])
    sc.dma("pool", "dw", PW, C.pool_w[i].rearrange("g c d -> c g d"), writes=[r_PW])
    for t in range(NT):
        norm_transpose(C, t, gcol, hT[:, :, t * 128:(t + 1) * 128], r_h[t])
    nb = 0
    for m in (0, 4, 1, 5, 2, 6, 3, 7):
        for tb in range(4):
            b = nb % 2
            nb += 1
            for kc in range(8):
                sc.op("pe", lambda e, kc=kc: e.matmul(C.PS[:, b, :], lhsT=W1[:, kc, m * 128:(m + 1) * 128], rhs=hT[:, kc, tb * 512:(tb + 1) * 512],
                                                    start=(kc == 0), stop=(kc == 7)),
                      reads=r_W1 + r_h[tb * 4:(tb + 1) * 4], writes=[C.pr[b]], count=(kc == 7))
            if m >= 4:
                sc.op("act", lambda e: e.activation(out=uT[:, m - 4, tb * 512:(tb + 1) * 512], in_=C.PS[:, b, :], func=AF.Copy),
                      reads=[C.pr[b]], writes=[r_u[m - 4][tb]])
            else:
                sc.op("act", lambda e: e.activation(out=zb[:, tb * 512:(tb + 1) * 512], in_=C.PS[:, b, :], func=AF.Copy),
                      reads=[C.pr[b]], writes=[r_z])
        if m < 4:
            g = m
            w = POOL_WINDOWS[g]
            src, rsrc = zb, r_z
            bufs = [(pa, r_pa), (pb, r_pb)]
            k = 1
            step = 0
            while k < w:
                dst, rdst = bufs[step % 2]
                sc.op("pool", lambda e, dst=dst, src=src, k=k: e.tensor_tensor(out=dst[:, k:], in0=src[:, k:], in1=src[:, :S - k], op=ALU.add),
                      reads=[rsrc], writes=[rdst])
                sc.op("pool", lambda e, dst=dst, src=src, k=k: e.tensor_copy(out=dst[:, :k], in_=src[:, :k]),
                      reads=[rsrc], writes=[rdst])
                src, rsrc = dst, rdst
                k *= 2
                step += 1
            sc.op("dve", lambda e, src=src: e.scalar_tensor_tensor(out=pl[:, w - 1:], in0=src[:, w - 1:], scalar=1.0 / w, in1=zb[:, w - 1:],
                                                                  op0=ALU.mult, op1=ALU.subtract),
                  reads=[rsrc, r_z], writes=[r_pl])
            tmpw = C.stat[:, 32:32 + w - 1]
            sc.op("dve", lambda e, src=src: e.tensor_tensor(out=tmpw, in0=src[:, :w - 1], in1=C.inv16[:, :w - 1], op=ALU.mult),
                  reads=[rsrc, C.r_const], writes=[r_pl])
            sc.op("dve", lambda e: e.tensor_tensor(out=pl[:, :w - 1], in0=tmpw, in1=zb[:, :w - 1], op=ALU.subtract),
                  reads=[r_pl, r_z], writes=[r_pl])
            for tb in range(4):
                b = 2 + tb % 2
                sc.op("pe", lambda e: e.matmul(C.PS[:, b, :], lhsT=PW[:, g, :], rhs=pl[:, tb * 512:(tb + 1) * 512], start=True, stop=True),
                      reads=[r_PW, r_pl], writes=[C.pr[b]])
                sc.op("act", lambda e: e.activation(out=mixT[:, g, tb * 512:(tb + 1) * 512], in_=C.PS[:, b, :], func=AF.Copy,
                                                    scale=C.psc[:, i * 4 + g:i * 4 + g + 1]),
                      reads=[C.pr[b], C.r_const], writes=[r_mix[g][tb]])
    sc.barrier()
    if DBG_STOP == 1:
        return

    LIre = aview(C, 64 * KB, [128, S], F32)
    LIim = aview(C, 72 * KB, [128, S], F32)
    TFre = aview(C, 80 * KB, [128, 16, 128], F32)
    TFim = aview(C, 88 * KB, [128, 16, 128], F32)
    LBre = aview(C, 96 * KB, [128, 16, 128], BF16)
    LBim = aview(C, 100 * KB, [128, 16, 128], BF16)
    LCre = aview(C, 104 * KB, [128, 16, 128], BF16)
    LCimn = aview(C, 108 * KB, [128, 16, 128], BF16)
    LCren = aview(C, 20 * KB, [128, 16, 128], BF16)
    bc = aview(C, 48 * KB, [128, 4, 16, 16], F32)
    Bb = aview(C, 52 * KB, [128, 2, 16, 16], F32)
    E = aview(C, 116 * KB, [128, 16, 128], F32)
    TIre = aview(C, 124 * KB, [128, 16, 128], F32)
    TIim = aview(C, 132 * KB, [128, 16, 128], F32)
    sm = aview(C, 112 * KB, [128, 48, 16], F32)
    rp = Res("oprep")
    RP = [rp]
    sc.dma("sp", "dio", bc[:, 0], C.s5b[i, 0], writes=[rp])
    for k in range(1, 4):
        sc.dma("sp", "dio", bc[:, k], C.s5b[i, k], reads=[rp], writes=[rp])
    lr = C.s5lam[:, 0, i * 16:(i + 1) * 16]
    li = C.s5lam[:, 1, i * 16:(i + 1) * 16]
    ld = C.s5lam[:, 2, i * 16:(i + 1) * 16]
    V = lambda k: sm[:, k, :]
    dt_, a_, b_, ea, eai, s_, c_, cc, ss, lbr, lbi, ibr, ibi = [V(k) for k in range(13)]
    nr, ni, den, cfr, cfi, t0, t1_, pwr, pwi, qwr, qwi = [V(k) for k in range(13, 24)]
    L128r, L128i = V(24), V(25)
    tq = [V(26 + k) for k in range(4)]

    def dv(fn, eng="dve"):
        sc.op(eng, fn, reads=RP + [C.r_const], writes=RP)

    dv(lambda e: e.activation(out=dt_, in_=ld, func=AF.Exp), "act")
    dv(lambda e: e.tensor_tensor(out=a_, in0=lr, in1=dt_, op=ALU.mult))
    dv(lambda e: e.tensor_tensor(out=b_, in0=li, in1=dt_, op=ALU.mult))
    dv(lambda e: e.activation(out=ea, in_=a_, func=AF.Exp), "act")
    dv(lambda e: e.activation(out=eai, in_=a_, func=AF.Exp, scale=-1.0), "act")
    dv(lambda e: e.activation(out=s_, in_=b_, func=AF.Sin, scale=1.0 / 16), "act")
    dv(lambda e: e.activation(out=c_, in_=b_, func=AF.Sin, scale=1.0 / 16, bias=C.halfpi[:]), "act")
    for _ in range(4):
        dv(lambda e: e.tensor_tensor(out=cc, in0=c_, in1=c_, op=ALU.mult))
        dv(lambda e: e.tensor_tensor(out=ss, in0=s_, in1=s_, op=ALU.mult))
        dv(lambda e: e.scalar_tensor_tensor(out=s_, in0=c_, scalar=2.0, in1=s_, op0=ALU.mult, op1=ALU.mult))
        dv(lambda e: e.tensor_tensor(out=c_, in0=cc, in1=ss, op=ALU.subtract))
    dv(lambda e: e.tensor_tensor(out=lbr, in0=ea, in1=c_, op=ALU.mult))
    dv(lambda e: e.tensor_tensor(out=lbi, in0=ea, in1=s_, op=ALU.mult))
    dv(lambda e: e.tensor_tensor(out=ibr, in0=eai, in1=c_, op=ALU.mult))
    dv(lambda e: e.scalar_tensor_tensor(out=ibi, in0=eai, scalar=-1.0, in1=s_, op0=ALU.mult, op1=ALU.mult))
    dv(lambda e: e.tensor_scalar(out=t0, in0=lbr, scalar1=-1.0, scalar2=None, op0=ALU.add))
    dv(lambda e: e.tensor_tensor(out=nr, in0=t0, in1=lr, op=ALU.mult))
    dv(lambda e: e.tensor_tensor(out=t1_, in0=lbi, in1=li, op=ALU.mult))
    dv(lambda e: e.tensor_tensor(out=nr, in0=nr, in1=t1_, op=ALU.add))
    dv(lambda e: e.tensor_tensor(out=ni, in0=lbi, in1=lr, op=ALU.mult))
    dv(lambda e: e.tensor_tensor(out=t1_, in0=t0, in1=li, op=ALU.mult))
    dv(lambda e: e.tensor_tensor(out=ni, in0=ni, in1=t1_, op=ALU.subtract))
    dv(lambda e: e.tensor_tensor(out=den, in0=lr, in1=lr, op=ALU.mult))
    dv(lambda e: e.tensor_tensor(out=t1_, in0=li, in1=li, op=ALU.mult))
    dv(lambda e: e.tensor_tensor(out=den, in0=den, in1=t1_, op=ALU.add))
    dv(lambda e: e.reciprocal(out=den, in_=den))
    dv(lambda e: e.tensor_tensor(out=cfr, in0=nr, in1=den, op=ALU.mult))
    dv(lambda e: e.tensor_tensor(out=cfi, in0=ni, in1=den, op=ALU.mult))

    def build_table(Tre, Tim, br, bi, keep128=False):
        dv(lambda e: e.memset(Tre[:, :, 0:1], 1.0))
        dv(lambda e: e.memset(Tim[:, :, 0:1], 0.0))
        dv(lambda e: e.tensor_copy(out=Tre[:, :, 1], in_=br))
        dv(lambda e: e.tensor_copy(out=Tim[:, :, 1], in_=bi))
        dv(lambda e: e.tensor_copy(out=pwr, in_=br))
        dv(lambda e: e.tensor_copy(out=pwi, in_=bi))
        n = 1
        while n < 128:
            dv(lambda e: e.tensor_tensor(out=qwr, in0=pwr, in1=pwr, op=ALU.mult))
            dv(lambda e: e.tensor_tensor(out=qwi, in0=pwi, in1=pwi, op=ALU.mult))
            dv(lambda e: e.scalar_tensor_tensor(out=pwi, in0=pwr, scalar=2.0, in1=pwi, op0=ALU.mult, op1=ALU.mult))
            dv(lambda e: e.tensor_tensor(out=pwr, in0=qwr, in1=qwi, op=ALU.subtract))
            n *= 2
            if n == 128:
                break
            pr_b = pwr.unsqueeze(2).to_broadcast([128, 16, n])
            pi_b = pwi.unsqueeze(2).to_broadcast([128, 16, n])
            A_re, A_im = Tre[:, :, 0:n], Tim[:, :, 0:n]
            O_re, O_im = Tre[:, :, n:2 * n], Tim[:, :, n:2 * n]
            X1 = E[:, :, 0:n]
            X2 = E[:, :, 64:64 + n]
            dv(lambda e, n=n: e.tensor_tensor(out=X1, in0=A_re, in1=pr_b, op=ALU.mult))
            dv(lambda e, n=n: e.tensor_tensor(out=X2, in0=A_im, in1=pi_b, op=ALU.mult))
            dv(lambda e, n=n: e.tensor_tensor(out=O_re, in0=X1, in1=X2, op=ALU.subtract))
            dv(lambda e, n=n: e.tensor_tensor(out=X1, in0=A_re, in1=pi_b, op=ALU.mult))
            dv(lambda e, n=n: e.tensor_tensor(out=X2, in0=A_im, in1=pr_b, op=ALU.mult))
            dv(lambda e, n=n: e.tensor_tensor(out=O_im, in0=X1, in1=X2, op=ALU.add))
        if keep128:
            dv(lambda e: e.tensor_copy(out=L128r, in_=pwr))
            dv(lambda e: e.tensor_copy(out=L128i, in_=pwi))

    build_table(TFre, TFim, lbr, lbi, keep128=True)
    build_table(TIre, TIim, ibr, ibi)
    for (TI, LI) in ((TIre, LIre), (TIim, LIim)):
        for q4 in range(4):
            b = 4 + q4 % 2
            for q in range(4):
                gp = q4 * 4 + q
                sc.op("pe", lambda e, gp=gp, q=q, TI=TI: e.transpose(out=C.PS[:, b, q * 128:(q + 1) * 128], in_=TI[:, gp, :], identity=C.identf[:]),
                      reads=RP + [C.r_const], writes=[C.pr[b]], count=(q == 3))
            sc.op("act", lambda e, LI=LI: e.activation(out=LI[:, q4 * 512:(q4 + 1) * 512], in_=C.PS[:, b, :], func=AF.Copy),
                  reads=[C.pr[b]], writes=RP)
    cr_b = cfr.unsqueeze(2).to_broadcast([128, 16, 16])
    ci_b = cfi.unsqueeze(2).to_broadcast([128, 16, 16])
    Y1, Y2 = E[:, :, 0:16], E[:, :, 64:80]
    dv(lambda e: e.tensor_tensor(out=Y1, in0=bc[:, 0], in1=cr_b, op=ALU.mult))
    dv(lambda e: e.tensor_tensor(out=Y2, in0=bc[:, 1], in1=ci_b, op=ALU.mult))
    dv(lambda e: e.tensor_tensor(out=Bb[:, 0], in0=Y1, in1=Y2, op=ALU.subtract))
    dv(lambda e: e.tensor_tensor(out=Y1, in0=bc[:, 1], in1=cr_b, op=ALU.mult))
    dv(lambda e: e.tensor_tensor(out=Y2, in0=bc[:, 0], in1=ci_b, op=ALU.mult))
    dv(lambda e: e.tensor_tensor(out=Bb[:, 1], in0=Y1, in1=Y2, op=ALU.add))

    def diag(T, h):
        base = T[h * 64:(h + 1) * 64, 0, 0:1]
        ps = base.ap[0][0]
        return bass.AP(base.tensor, base.offset + h * 16, [[ps, 64], [512, 4], [160, 4], [1, 16]])

    def src4(T3, h):
        return T3[h * 64:(h + 1) * 64].rearrange("p (cu q) c -> p cu q c", q=4)

    for part, LB in ((0, LBre), (1, LBim)):
        dv(lambda e: e.memset(E, 0.0))
        for h in range(2):
            dv(lambda e, h=h, part=part: e.tensor_copy(out=diag(E, h), in_=src4(Bb[:, part], h)))
        for q4 in range(4):
            b = 4 + q4 % 2
            for q in range(4):
                gp = q4 * 4 + q
                sc.op("pe", lambda e, gp=gp, q=q: e.transpose(out=C.PS[:, b, q * 128:(q + 1) * 128], in_=E[:, gp, :], identity=C.identf[:]),
                      reads=RP + [C.r_const], writes=[C.pr[b]], count=(q == 3))
            sc.op("act", lambda e, LB=LB: e.activation(out=LB[:, q4 * 4:(q4 + 1) * 4, :], in_=C.PS[:, b, :].rearrange("p (q s) -> p q s", s=128), func=AF.Copy),
                  reads=[C.pr[b]], writes=RP)
    dv(lambda e: e.memset(LCre, 0.0))
    dv(lambda e: e.memset(LCimn, 0.0))
    dv(lambda e: e.memset(LCren, 0.0))
    for h in range(2):
        dv(lambda e, h=h: e.tensor_copy(out=diag(LCre, h), in_=src4(bc[:, 2], h)))
        dv(lambda e, h=h: e.tensor_scalar(out=diag(LCimn, h), in0=src4(bc[:, 3], h), scalar1=-1.0, scalar2=None, op0=ALU.mult))
        dv(lambda e, h=h: e.tensor_scalar(out=diag(LCren, h), in0=src4(bc[:, 2], h), scalar1=-1.0, scalar2=None, op0=ALU.mult))
    cXr = V(30)
    cXi = V(31)
    dv(lambda e: e.memset(cXr, 0.0))
    dv(lambda e: e.memset(cXi, 0.0))
    sc.barrier()
    if DBG_STOP == 2:
        return

    T = [aview(C, 116 * KB + 2048 * k, [128, 512], F32) for k in range(4)]
    btr2 = [aview(C, 124 * KB, [128, 512], BF16), aview(C, 18 * KB, [128, 512], BF16)]
    bti2 = [aview(C, 125 * KB, [128, 512], BF16), aview(C, 19 * KB, [128, 512], BF16)]
    wr_ = aview(C, 126 * KB, [128, 4, 128], F32)
    wi_ = aview(C, 128 * KB, [128, 4, 128], F32)
    Pq = [[aview(C, 130 * KB + 1024 * (2 * k + pp), [128, 4, 128], BF16) for k in range(4)] for pp in range(2)]
    r_Pq = [[Res(f"oPq{pp}_{k}") for k in range(4)] for pp in range(2)]
    xre = [aview(C, 138 * KB + 1024 * k, [128, 4, 128], BF16) for k in range(2)]
    xim = [aview(C, 16 * KB + 1024 * k, [128, 4, 128], BF16) for k in range(2)]
    yv = [aview(C, 115 * KB + 512 * k, [128, 128], F32) for k in range(2)]
    r_T, r_w, r_P, r_P3 = Res("oT"), Res("ow"), Res("oP"), Res("oP3")
    r_bt2 = [Res("obt0"), Res("obt1")]
    r_x = [Res("ox0"), Res("ox1")]
    r_yv = [Res("oyv0"), Res("oyv1")]
    r_cx = [Res(f"ocx{cu}") for cu in range(4)]
    r_yg = [[Res(f"oyg{cu}_{k}") for k in range(NT)] for cu in range(4)]
    units = [(k, cu) for k in range(NT) for cu in range(4)]

    def stage1(n):
        k, cu = units[n]
        par = n % 2
        bA, bB = 2 * par, 2 * par + 1
        ts = slice(k * 128, (k + 1) * 128)
        g4 = slice(4 * cu, 4 * cu + 4)
        ru = [r_u[cu][k // 4]]
        sc.op("pe", lambda e: e.matmul(C.PS[:, bA, :], lhsT=uT[:, cu, ts], rhs=LBre[:, g4, :], start=True, stop=True),
              reads=ru, writes=[C.pr[bA]])
        sc.op("pe", lambda e: e.matmul(C.PS[:, bB, :], lhsT=uT[:, cu, ts], rhs=LBim[:, g4, :], start=True, stop=True),
              reads=ru, writes=[C.pr[bB]])
        cs = slice(cu * 512, (cu + 1) * 512)
        sc.op("dve", lambda e: e.tensor_tensor(out=T[0], in0=C.PS[:, bA, :], in1=LIre[:, cs], op=ALU.mult), reads=[C.pr[bA]], writes=[r_T])
        sc.op("dve", lambda e: e.tensor_tensor(out=T[1], in0=C.PS[:, bB, :], in1=LIim[:, cs], op=ALU.mult), reads=[C.pr[bB]], writes=[r_T])
        sc.op("dve", lambda e: e.tensor_tensor(out=T[2], in0=C.PS[:, bB, :], in1=LIre[:, cs], op=ALU.mult), reads=[C.pr[bB]], writes=[r_T])
        sc.op("dve", lambda e: e.tensor_tensor(out=T[3], in0=C.PS[:, bA, :], in1=LIim[:, cs], op=ALU.mult), reads=[C.pr[bA]], writes=[r_T])
        btr, bti, r_bt = btr2[n % 2], bti2[n % 2], r_bt2[n % 2]
        sc.op("dve", lambda e: e.tensor_tensor(out=btr, in0=T[0], in1=T[1], op=ALU.subtract), reads=[r_T], writes=[r_bt])
        sc.op("pool", lambda e: e.tensor_tensor(out=bti, in0=T[2], in1=T[3], op=ALU.add), reads=[r_T], writes=[r_bt])

    def stage1b(n):
        btr, bti, r_bt = btr2[n % 2], bti2[n % 2], r_bt2[n % 2]
        for (bt, bk) in ((btr, 4), (bti, 5)):
            for q in range(4):
                sc.op("pe", lambda e, bt=bt, bk=bk, q=q: e.matmul(C.PS[:, bk, q * 128:(q + 1) * 128], lhsT=bt[:, q * 128:(q + 1) * 128], rhs=C.triT[:],
                                                                start=True, stop=True),
                      reads=[r_bt, C.r_const], writes=[C.pr[bk]], count=(q == 3))

    def stage2a(n):
        k, cu = units[n]
        g4 = slice(4 * cu, 4 * cu + 4)
        cxr_b = cXr[:, g4].unsqueeze(2).to_broadcast([128, 4, 128])
        cxi_b = cXi[:, g4].unsqueeze(2).to_broadcast([128, 4, 128])
        w4r = C.PS[:, 4, :].rearrange("p (q j) -> p q j", j=128)
        w4i = C.PS[:, 5, :].rearrange("p (q j) -> p q j", j=128)
        for q in range(4):
            sc.op("act", lambda e, q=q: e.activation(out=wr_[:, q, :], in_=w4r[:, q, :], func=AF.Identity, bias=cXr[:, 4 * cu + q:4 * cu + q + 1]),
                  reads=[C.pr[4], r_cx[cu]], writes=[r_w])
            sc.op("act", lambda e, q=q: e.activation(out=wi_[:, q, :], in_=w4i[:, q, :], func=AF.Identity, bias=cXi[:, 4 * cu + q:4 * cu + q + 1]),
                  reads=[C.pr[5], r_cx[cu]], writes=[r_w])
        w127r, w127i = wr_[:, :, 127], wi_[:, :, 127]
        sc.op("pool", lambda e: e.tensor_tensor(out=tq[0][:, 0:4], in0=w127r, in1=L128r[:, g4], op=ALU.mult), reads=[r_w], writes=[r_cx[cu]])
        sc.op("pool", lambda e: e.tensor_tensor(out=tq[1][:, 0:4], in0=w127i, in1=L128i[:, g4], op=ALU.mult), reads=[r_w], writes=[r_cx[cu]])
        sc.op("pool", lambda e: e.tensor_tensor(out=tq[2][:, 0:4], in0=w127i, in1=L128r[:, g4], op=ALU.mult), reads=[r_w], writes=[r_cx[cu]])
        sc.op("pool", lambda e: e.tensor_tensor(out=tq[3][:, 0:4], in0=w127r, in1=L128i[:, g4], op=ALU.mult), reads=[r_w], writes=[r_cx[cu]])
        sc.op("pool", lambda e: e.tensor_tensor(out=cXr[:, g4], in0=tq[0][:, 0:4], in1=tq[1][:, 0:4], op=ALU.subtract), reads=[r_cx[cu]], writes=[r_cx[cu]])
        sc.op("pool", lambda e: e.tensor_tensor(out=cXi[:, g4], in0=tq[2][:, 0:4], in1=tq[3][:, 0:4], op=ALU.add), reads=[r_cx[cu]], writes=[r_cx[cu]])
        par = n % 2
        P = Pq[par]
        rP = r_Pq[par]
        sc.op("pool", lambda e: e.tensor_tensor(out=P[0], in0=TFre[:, g4, :], in1=wr_, op=ALU.mult), reads=[r_w], writes=[rP[0]])
        sc.op("pool", lambda e: e.tensor_tensor(out=P[1], in0=TFim[:, g4, :], in1=wi_, op=ALU.mult), reads=[r_w], writes=[rP[1]])
        sc.op("dve", lambda e: e.tensor_tensor(out=P[2], in0=TFre[:, g4, :], in1=wi_, op=ALU.mult), reads=[r_w], writes=[rP[2]])
        sc.op("dve", lambda e: e.tensor_tensor(out=P[3], in0=TFim[:, g4, :], in1=wr_, op=ALU.mult), reads=[r_w], writes=[rP[3]])

    def stage2b(n):
        k, cu = units[n]
        par = n % 2
        ts = slice(k * 128, (k + 1) * 128)
        ru = [r_u[cu][k // 4]]
        by = 6 + par
        P = Pq[par]
        rP = r_Pq[par]
        terms = [(LCre, 0), (LCren, 1), (LCimn, 2), (LCimn, 3)]
        nmm = 0
        for q in range(4):
            for (LT, kk) in terms:
                nmm += 1
                sc.op("pe", lambda e, q=q, LT=LT, kk=kk, nmm=nmm: e.matmul(C.PS[:, by, 0:128], lhsT=LT[:, 4 * cu + q, :], rhs=P[kk][:, q, :], start=(nmm == 1), stop=(nmm == 16)),
                      reads=[rP[kk]], writes=[C.pr[by]], count=(nmm == 16))
        sc.op("dve", lambda e: e.scalar_tensor_tensor(out=yv[par], in0=uT[:, cu, ts], scalar=C.s5d[:, i * 4 + cu:i * 4 + cu + 1], in1=C.PS[:, by, 0:128],
                                                     op0=ALU.mult, op1=ALU.add),
              reads=[C.pr[by], C.r_const] + ru, writes=[r_yv[par]])

    def stage2c(n):
        k, cu = units[n]
        par = n % 2
        ts = slice(k * 128, (k + 1) * 128)
        sc.op("act", lambda e: e.activation(out=ygT[:, cu, ts], in_=yv[par], func=AF.Gelu_apprx_tanh),
              reads=[r_yv[par]], writes=[r_yg[cu][k]])

    NU = len(units)
    stage1(0)
    stage1b(0)
    if NU > 1:
        stage1(1)
    for n in range(NU):
        stage2a(n)
        if n >= 1:
            stage2c(n - 1)
        if n + 1 < NU:
            stage1b(n + 1)
        if n + 2 < NU:
            stage1(n + 2)
        stage2b(n)
    stage2c(NU - 1)
    sc.barrier()
    if DBG_STOP == 3:
        return

    WGL = aview(C, 64 * KB, [128, 4, 1024], BF16)
    WO = aview(C, 72 * KB, [128, 8, 1024], BF16)
    sgl = [aview(C, 88 * KB + 2048 * k, [128, 512], F32) for k in range(2)]
    r_WGL, r_WO = Res("oWGL"), [Res("oWOa"), Res("oWOb")]
    r_sgl = [Res("osgl0"), Res("osgl1")]
    sc.dma("pool", "dw", WGL, C.s5_w_glu[i].rearrange("(kc p) f -> p kc f", p=128), writes=[r_WGL])
    wov = C.od_w_out[i].rearrange("(kc p) f -> p kc f", p=128)
    sc.dma("pool", "dw", WO[:, 0:4, :], wov[:, 0:4, :], writes=[r_WO[0]])
    sc.dma("pool", "dw", WO[:, 4:8, :], wov[:, 4:8, :], writes=[r_WO[1]])
    n = 0
    for m in range(4):
        for tb in range(4):
            par = n % 2
            n += 1
            bA, bG = par, 2 + par
            for (bk, col) in ((bA, m), (bG, m + 4)):
                for kc in range(4):
                    sc.op("pe", lambda e, kc=kc, bk=bk, col=col: e.matmul(C.PS[:, bk, :], lhsT=WGL[:, kc, col * 128:(col + 1) * 128],
                                                                        rhs=ygT[:, kc, tb * 512:(tb + 1) * 512], start=(kc == 0), stop=(kc == 3)),
                          reads=[r_WGL], writes=[C.pr[bk]], count=(kc == 3))
            sc.op("act", lambda e: e.activation(out=sgl[par], in_=C.PS[:, bG, :], func=AF.Sigmoid), reads=[C.pr[bG]], writes=[r_sgl[par]])
            sc.op("dve", lambda e: e.tensor_tensor(out=mixT[:, 4 + m, tb * 512:(tb + 1) * 512], in0=sgl[par], in1=C.PS[:, bA, :], op=ALU.mult),
                  reads=[r_sgl[par], C.pr[bA]], writes=[r_mix[4 + m][tb]])
    if DBG_STOP == 4:
        return
    out_proj(C, mixT, [r for rr in r_mix for r in rr], WO, r_WO)


def out_proj(C, mixT, r_mix_all, WO, r_WO):
    sc = C.sc
    n = 0
    for t in range(NT):
        for db in range(2):
            b = 4 + n % 4
            n += 1
            for kc in range(8):
                sc.op("pe", lambda e, kc=kc: e.matmul(C.PS[:, b, :], lhsT=mixT[:, kc, t * 128:(t + 1) * 128], rhs=WO[:, kc, db * 512:(db + 1) * 512],
                                                    start=(kc == 0), stop=(kc == 7)),
                      reads=r_mix_all + r_WO, writes=[C.pr[b]], count=(kc == 7))
            xv = C.X[:, t, db * 512:(db + 1) * 512]
            sc.op("dve", lambda e, xv=xv: e.scalar_tensor_tensor(out=xv, in0=C.PS[:, b, :], scalar=1.0, in1=xv, op0=ALU.mult, op1=ALU.add),
                  reads=[C.pr[b], C.xr[t]], writes=[C.xr[t]])


NEG = -30000.0
KBOUND = 12.0


class Bump:
    def __init__(self, C, start, end=ARENA):
        self.C, self.off, self.end = C, start, end

    def __call__(self, shape, dtype):
        esz = 4 if dtype == F32 else 2
        n = 1
        for d in shape[1:]:
            n *= d
        nb = (n * esz + 63) // 64 * 64
        assert self.off + nb <= self.end, ("arena overflow", self.off, nb, self.end)
        v = aview(self.C, self.off, [shape[0], nb // esz], dtype)[:, 0:n]
        self.off += nb
        if len(shape) > 2:
            names = ["a", "b", "c", "d"][:len(shape) - 1]
            pat = "p (" + " ".join(names) + ") -> p " + " ".join(names)
            v = v.rearrange(pat, **{nm: d for nm, d in zip(names[1:], shape[2:])})
        return v


def psb(C, b):
    return C.PS[:, b, :].bitcast(BF16)


def rope_ops(C, src3, dst3, cos_b, sin_b, tmp, rd, r_tmp, r_dst, npart=128):
    sc = C.sc
    x1, x2 = src3[:, :, 0:32], src3[:, :, 32:64]
    sc.op("dve", lambda e: e.tensor_tensor(out=tmp[0], in0=x1, in1=cos_b, op=ALU.mult), reads=rd, writes=[r_tmp])
    sc.op("dve", lambda e: e.tensor_tensor(out=tmp[1], in0=x2, in1=sin_b, op=ALU.mult), reads=rd, writes=[r_tmp])
    sc.op("dve", lambda e: e.tensor_tensor(out=tmp[2], in0=x2, in1=cos_b, op=ALU.mult), reads=rd, writes=[r_tmp])
    sc.op("dve", lambda e: e.tensor_tensor(out=tmp[3], in0=x1, in1=sin_b, op=ALU.mult), reads=rd, writes=[r_tmp])
    sc.op("pool", lambda e: e.tensor_tensor(out=dst3[:, :, 0:32], in0=tmp[0], in1=tmp[1], op=ALU.subtract), reads=[r_tmp], writes=[r_dst])
    sc.op("pool", lambda e: e.tensor_tensor(out=dst3[:, :, 32:64], in0=tmp[2], in1=tmp[3], op=ALU.add), reads=[r_tmp], writes=[r_dst])


def even_mixer(C, l):
    sc = C.sc
    i = l // 2
    gcol = C.gT[:, (l * 3 + 1) * 8:(l * 3 + 2) * 8]
    al = Bump(C, 0)
    mixT = al([128, 8, S], BF16)
    qT = al([128, 4, S], BF16)
    ksT = al([128, S], BF16)
    kwT = al([128, S], BF16)
    Vs = al([128, NT, 2, 65], BF16)
    Vw = al([128, NT, 2, 65], BF16)
    gates = al([128, NT, 24], F32)
    Zall = al([128, NT, 136], BF16)
    kcmpT = al([128, 128], BF16)
    Vc = al([128, 2, 97], BF16)
    kcT = al([128, S], BF16)
    vcT = al([128, S], BF16)
    p_end = al.off
    r_mix = [Res(f"emix{t}") for t in range(NT)]
    r_q = [Res(f"eq{t}") for t in range(NT)]
    r_k = [Res(f"ek{t}") for t in range(NT)]
    r_v = [Res(f"ev{t}") for t in range(NT)]
    r_gate = [Res(f"egate{t}") for t in range(NT)]
    r_Z = [Res(f"eZ{t}") for t in range(NT)]
    r_cv = [Res(f"ecv{b}") for b in range(4)]
    r_init = Res("einit")

    al = Bump(C, p_end)
    hT = al([128, 8, 1024], BF16)
    gus = al([128, 8, 512], BF16)
    cosT = al([128, NT, 32], F32)
    sinT = al([128, NT, 32], F32)
    WsT = al([128, 8, 128], BF16)
    WsR = al([128, 8, 128], F32)
    bT = al([128, 8], F32)
    C.xs = [al([128, D], F32) for _ in range(2)]
    C.junk = al([128, D], BF16)
    gv = al([128, 8, 64], F32)
    cen = al([128, 8, 64], F32)
    sq = al([128, 8, 64], F32)
    vn = al([128, 512], BF16)
    oa = al([128, 512], BF16)
    rtmp = [al([128, 8, 32], F32) for _ in range(4)]
    rq = al([128, 8, 64], F32)
    qA = al([128, 512], BF16)
    kA = al([128, 256], BF16)
    st8 = al([128, 8, 8], F32)
    Wr = [aview(C, 16 * KB + 8192 * k, [128, 8, 512], BF16) for k in range(2)]
    r_Wr = [[Res(f"eWr{k}_{j}") for j in range(4)] for k in range(2)]
    r_h = [Res(f"eh{t}") for t in range(8)]
    r_gus = [Res(f"egus{t}") for t in range(8)]
    r_rope, r_ws = Res("erope"), Res("ews")
    r_gv, r_cen, r_sq, r_vn, r_oa, r_rt, r_rq, r_qA, r_kA, r_st = [Res("e" + n) for n in ("gv", "cen", "sq", "vn", "oa", "rt", "rq", "qA", "kA", "st")]
    sc.dma("sp", "dio", cosT, C.rope_in[0], writes=[r_rope])
    sc.dma("sp", "dio", sinT, C.rope_in[1], reads=[r_rope], writes=[r_rope])
    sc.dma("sp", "dio", WsR, C.gm_ws_in[i], writes=[r_ws])
    sc.dma("sp", "dio", bT, C.gm_b_in[i], reads=[r_ws], writes=[r_ws])
    sc.op("dve", lambda e: e.tensor_tensor(out=WsT, in0=WsR, in1=C.triT[:].unsqueeze(1).to_broadcast([128, 8, 128]), op=ALU.mult),
          reads=[r_ws, C.r_const], writes=[r_ws])
    sc.op("dve", lambda e: e.memset(Zall, 0.0), writes=[r_init])
    sc.op("dve", lambda e: e.memset(Vs[:, :, :, 64:65], 1.0), reads=[r_init], writes=[r_init])
    sc.op("dve", lambda e: e.memset(Vw[:, :, :, 64:65], 1.0), reads=[r_init], writes=[r_init])
    w_in = C.ev_w_in[i].rearrange("(kc p) f -> p kc f", p=128)
    wctr = [0]

    def load_group(cols):
        k = wctr[0] % 2
        wctr[0] += 1
        o = 0
        rs = []
        for j, (c0, w) in enumerate(cols):
            sc.dma("pool", "dw", Wr[k][:, :, o:o + w], w_in[:, :, c0:c0 + w], writes=[r_Wr[k][j]])
            rs.append(r_Wr[k][j])
            o += w
        return Wr[k], rs

    pctr = [0]

    def proj_tm(W, rW, tt, ncols, col0=0):
        b = pctr[0] % 2
        pctr[0] += 1
        for kc in range(8):
            sc.op("pe", lambda e, kc=kc: e.matmul(C.PS[:, b, 0:ncols], lhsT=hT[:, kc, tt * 128:(tt + 1) * 128], rhs=W[:, kc, col0:col0 + ncols],
                                                start=(kc == 0), stop=(kc == 7)),
                  reads=rW + [r_h[tt]], writes=[C.pr[b]], count=(kc == 7))
        return b

    sq2 = WsR[:, 0:4, :].rearrange("p a (b d) -> p (a b) d", d=64)
    r_sq2 = Res("esq2")

    def chain_v(hf, tt, W, rW):
        t = hf * 8 + tt
        b = proj_tm(W, rW, tt, 512)
        sc.op("act", lambda e: e.activation(out=gv, in_=C.PS[:, b, :].rearrange("p (h d) -> p h d", d=64), func=AF.Gelu_apprx_tanh),
              reads=[C.pr[b]], writes=[r_gv])
        yield
        s1, mean, s2, rstd = st8[:, 0, :], st8[:, 1, :], st8[:, 2, :], st8[:, 3, :]
        sc.op("dve", lambda e: e.tensor_reduce(out=s1, in_=gv, axis=AX.X, op=ALU.add), reads=[r_gv], writes=[r_st])
        sc.op("dve", lambda e: e.tensor_scalar(out=mean, in0=s1, scalar1=1.0 / 64, scalar2=None, op0=ALU.mult), reads=[r_st], writes=[r_st])
        yield
        sc.op("dve", lambda e: e.tensor_tensor(out=cen, in0=gv, in1=mean.unsqueeze(2).to_broadcast([128, 8, 64]), op=ALU.subtract),
              reads=[r_gv, r_st], writes=[r_cen])
        sc.op("pool", lambda e: e.tensor_tensor(out=sq, in0=cen, in1=cen, op=ALU.mult), reads=[r_cen], writes=[r_sq])
        yield
        sc.op("dve", lambda e: e.tensor_reduce(out=s2, in_=sq, axis=AX.X, op=ALU.add), reads=[r_sq], writes=[r_st])
        sc.op("dve", lambda e: e.tensor_scalar(out=s2, in0=s2, scalar1=1.0 / 64, scalar2=1e-5, op0=ALU.mult, op1=ALU.add), reads=[r_st], writes=[r_st])
        sc.op("pool", lambda e: e.tensor_tensor(out=rstd, in0=s2, in1=C.cm05[:].to_broadcast([128, 8]), op=ALU.pow), reads=[r_st, C.r_const], writes=[r_st])
        yield
        sc.op("dve", lambda e: e.tensor_tensor(out=vn.rearrange("p (h d) -> p h d", d=64), in0=cen, in1=rstd.unsqueeze(2).to_broadcast([128, 8, 64]), op=ALU.mult),
              reads=[r_cen, r_st], writes=[r_vn])
        for h in range(8):
            sc.op("pe", lambda e, h=h: e.matmul(C.PS[:, 2, h * 64:(h + 1) * 64], lhsT=WsT[:, h, :], rhs=vn[:, h * 64:(h + 1) * 64], start=True, stop=True),
                  reads=[r_ws, r_vn], writes=[C.pr[2]], count=(h == 7))
        yield
        sc.op("dve", lambda e: e.tensor_tensor(out=cen, in0=C.PS[:, 2, :].rearrange("p (h d) -> p h d", d=64), in1=bT.unsqueeze(2).to_broadcast([128, 8, 64]), op=ALU.add),
              reads=[C.pr[2], r_ws], writes=[r_cen])
        sc.op("dve", lambda e: e.tensor_tensor(out=oa, in0=cen.rearrange("p h d -> p (h d)"), in1=gus[:, tt, :], op=ALU.mult),
              reads=[r_cen, r_gus[tt]], writes=[r_oa])
        yield
        for c in range(4):
            sc.op("pe", lambda e, c=c: e.transpose(out=psb(C, 3)[:, c * 128:(c + 1) * 128], in_=oa[:, c * 128:(c + 1) * 128], identity=C.identb[:]),
                  reads=[r_oa, C.r_const], writes=[C.pr[3]], count=(c == 3))
        sc.op("act", lambda e: e.activation(out=mixT[:, 0:4, t * 128:(t + 1) * 128], in_=psb(C, 3)[:, 0:512].rearrange("p (c t) -> p c t", t=128), func=AF.Copy),
              reads=[C.pr[3]], writes=[r_mix[t]])
        yield

    def chain_q(hf, tt, W, rW):
        t = hf * 8 + tt
        b = proj_tm(W, rW, tt, 512)
        cb_ = cosT[:, t, :].unsqueeze(1).to_broadcast([128, 8, 32])
        sb_ = sinT[:, t, :].unsqueeze(1).to_broadcast([128, 8, 32])
        rope_ops(C, C.PS[:, b, :].rearrange("p (h d) -> p h d", d=64), rq, cb_, sb_, rtmp, [C.pr[b], r_rope], r_rt, r_rq)
        yield
        sc.op("pool", lambda e: e.tensor_tensor(out=sq2, in0=rq, in1=rq, op=ALU.mult), reads=[r_rq, r_ws], writes=[r_sq2])
        ssq, nm = st8[:, 4, :], st8[:, 5, :]
        sc.op("dve", lambda e: e.tensor_reduce(out=ssq, in_=sq2, axis=AX.X, op=ALU.add), reads=[r_sq2], writes=[r_st2])
        yield
        sc.op("pool", lambda e: e.tensor_tensor(out=nm, in0=ssq, in1=C.c05[:].to_broadcast([128, 8]), op=ALU.pow), reads=[r_st2, C.r_const], writes=[r_st2])
        zb_ = Zall[:, t, 32:36]
        zout = bass.AP(zb_.tensor, zb_.offset, [[zb_.ap[0][0], 128], [100, 2], [1, 4]])
        sc.op("dve", lambda e, zout=zout: e.tensor_scalar(out=zout, in0=nm.rearrange("p (g h) -> p g h", h=4), scalar1=-0.125 * KBOUND, scalar2=None, op0=ALU.mult),
              reads=[r_st2, r_init], writes=[r_Z[t]])
        sc.op("act", lambda e: e.activation(out=qA.rearrange("p (h g d) -> p g h d", h=4, g=2), in_=rq.rearrange("p (g h) d -> p g h d", g=2), func=AF.Copy, scale=0.125),
              reads=[r_rq], writes=[r_qA])
        yield
        for h in range(4):
            sc.op("pe", lambda e, h=h: e.transpose(out=psb(C, 3)[:, h * 128:(h + 1) * 128], in_=qA[:, h * 128:(h + 1) * 128], identity=C.identb[:]),
                  reads=[r_qA, C.r_const], writes=[C.pr[3]], count=(h == 3))
        sc.op("act", lambda e: e.activation(out=qT[:, :, t * 128:(t + 1) * 128], in_=psb(C, 3)[:, 0:512].rearrange("p (h t) -> p h t", t=128), func=AF.Copy),
              reads=[C.pr[3]], writes=[r_q[t]])
        yield

    def chain_k(hf, tt, W, rW):
        t = hf * 8 + tt
        b = proj_tm(W, rW, tt, 512)
        cb_ = cosT[:, t, :].unsqueeze(1).to_broadcast([128, 4, 32])
        sb_ = sinT[:, t, :].unsqueeze(1).to_broadcast([128, 4, 32])
        sc.op("act", lambda e: e.activation(out=Vs[:, t, :, 0:64], in_=C.PS[:, b, 256:384].rearrange("p (g d) -> p g d", d=64), func=AF.Copy),
              reads=[C.pr[b], r_init], writes=[r_v[t]])
        sc.op("act", lambda e: e.activation(out=Vw[:, t, :, 0:64], in_=C.PS[:, b, 384:512].rearrange("p (g d) -> p g d", d=64), func=AF.Copy),
              reads=[C.pr[b], r_init], writes=[r_v[t]])
        rope_ops(C, C.PS[:, b, 0:256].rearrange("p (h d) -> p h d", d=64), rq[:, 0:4, :], cb_, sb_, [x[:, 0:4, :] for x in rtmp], [C.pr[b], r_rope], r_rt, r_rq)
        yield
        sc.op("act", lambda e: e.activation(out=kA, in_=rq[:, 0:4, :].rearrange("p h d -> p (h d)"), func=AF.Copy), reads=[r_rq], writes=[r_kA])
        yield
        for c in range(2):
            sc.op("pe", lambda e, c=c: e.transpose(out=psb(C, 3)[:, c * 128:(c + 1) * 128], in_=kA[:, c * 128:(c + 1) * 128], identity=C.identb[:]),
                  reads=[r_kA, C.r_const], writes=[C.pr[3]], count=(c == 1))
        sc.op("act", lambda e: e.activation(out=ksT[:, t * 128:(t + 1) * 128], in_=psb(C, 3)[:, 0:128], func=AF.Copy), reads=[C.pr[3]], writes=[r_k[t]])
        sc.op("act", lambda e: e.activation(out=kwT[:, t * 128:(t + 1) * 128], in_=psb(C, 3)[:, 128:256], func=AF.Copy), reads=[C.pr[3]], writes=[r_k[t]])
        yield

    def chain_c(hf, W, rW):
        for tb in range(2):
            for cc, dstT in ((0, kcT), (1, vcT)):
                b = pctr[0] % 2
                pctr[0] += 1
                for kc in range(8):
                    sc.op("pe", lambda e, kc=kc: e.matmul(C.PS[:, b, :], lhsT=W[:, kc, cc * 128:(cc + 1) * 128], rhs=hT[:, kc, tb * 512:(tb + 1) * 512],
                                                        start=(kc == 0), stop=(kc == 7)),
                          reads=rW + r_h[tb * 4:(tb + 1) * 4], writes=[C.pr[b]], count=(kc == 7))
                tok0 = hf * 1024 + tb * 512
                sc.op("act", lambda e: e.activation(out=dstT[:, tok0:tok0 + 512], in_=C.PS[:, b, :], func=AF.Copy), reads=[C.pr[b]], writes=[r_cv[hf * 2 + tb]])
                yield
        for tt in range(8):
            t = hf * 8 + tt
            b = proj_tm(W, rW, tt, 32, col0=256)
            sc.op("act", lambda e: e.activation(out=gates[:, t, :], in_=C.PS[:, b, 8:32], func=AF.Sigmoid), reads=[C.pr[b]], writes=[r_gate[t]])
            yield

    def seq(gens):
        for g_ in gens:
            yield from g_

    def ilv(gens):
        gens = list(gens)
        while gens:
            for g_ in list(gens):
                try:
                    next(g_)
                except StopIteration:
                    gens.remove(g_)

    r_st2 = Res("est2")
    for hf in range(2):
        for tt in range(8):
            norm_transpose(C, hf * 8 + tt, gcol, hT[:, :, tt * 128:(tt + 1) * 128], r_h[tt])
        W, rW = load_group([(0, 512)])
        for tt in range(8):
            b = proj_tm(W, rW, tt, 512)
            sc.op("act", lambda e: e.activation(out=gus[:, tt, :], in_=C.PS[:, b, :], func=AF.Gelu_apprx_tanh), reads=[C.pr[b]], writes=[r_gus[tt]])
        Wv, rWv = load_group([(512, 512)])
        Wq, rWq = load_group([(1024, 512)])
        ilv([seq(chain_v(hf, tt, Wv, rWv) for tt in range(8)), seq(chain_q(hf, tt, Wq, rWq) for tt in range(8))])
        Wk, rWk = load_group([(1792, 128), (2048, 128), (1920, 128), (2176, 128)])
        Wc, rWc = load_group([(1536, 256), (2296, 32)])
        ilv([seq(chain_k(hf, tt, Wk, rWk) for tt in range(8)), chain_c(hf, Wc, rWc)])
    sc.barrier()
    if DBG_STOP == 1:
        return
    even_attention(C, i, p_end, mixT, qT, ksT, kwT, Vs, Vw, gates, Zall, kcmpT, Vc, kcT, vcT, r_mix)


def even_attention(C, i, p_end, mixT, qT, ksT, kwT, Vs, Vw, gates, Zall, kcmpT, Vc, kcT, vcT, r_mix):
    sc = C.sc
    al = Bump(C, p_end)
    w1 = al([128, 2, 32, 64], BF16)
    w2 = al([64, 2, 64], BF16)
    peT = al([128, 2, 32], BF16)
    cbias = al([64, 2], F32)
    gh = al([64, 4, 128], BF16)
    cosc = al([128, 32], F32)
    sinc = al([128, 32], F32)
    rkc = al([128, 2, 64], F32)
    ctmp = [al([128, 2, 32], F32) for _ in range(4)]
    kcA = al([128, 128], BF16)
    Fpos = al([128, NT, 32], F32)
    Fneg = al([128, NT, 32], F32)
    EB = al([100, S], BF16)
    triB = al([128, 512], BF16)
    bandB = al([128, 512], BF16)
    BM = al([100, 512], BF16)
    Mpat = al([73, 512], BF16)
    Bwin = al([73, 256], BF16)
    NP = 6
    Pt = [al([128, 512], BF16) for _ in range(NP)]
    SB1 = [al([100, 512], BF16) for _ in range(2)]
    SB2 = [al([100, 512], BF16) for _ in range(2)]
    oacc = al([128, 4, 64], F32)
    otmp = al([128, 4, 64], F32)
    ob = al([128, 512], BF16)
    sm4 = al([128, 16, 4], F32)
    impn = al([128, 4, 32], F32)
    vsel = al([128, 32], F32)
    top8 = al([128, 8], F32)
    WO = al([128, 8, 1024], BF16)
    rc = Res("ecmp")
    r_cst = [Res(f"ecst{k}") for k in range(9)]
    r_WO = [Res("eWOa"), Res("eWOb")]
    sc.dma("sp", "dio", Fpos, C.fsel_in[0], writes=[r_cst[0]])
    sc.dma("sp", "dio", Fneg, C.fsel_in[1], writes=[r_cst[1]])
    sc.dma("pool", "dio", EB, C.eb_in, writes=[r_cst[2]])
    sc.dma("pool", "dio", triB, C.trib_in, writes=[r_cst[3]])
    sc.dma("pool", "dio", bandB, C.bandb_in, writes=[r_cst[4]])
    sc.dma("pool", "dio", BM, C.bm_in, writes=[r_cst[5]])
    sc.dma("pool", "dio", Mpat, C.mpat_in, writes=[r_cst[6]])
    sc.dma("pool", "dio", Bwin, C.bwin_in, writes=[r_cst[7]])
    sc.dma("sp", "dio", cosc, C.ropec_in[0], writes=[r_cst[8]])
    sc.dma("sp", "dio", sinc, C.ropec_in[1], reads=[r_cst[8]], writes=[r_cst[8]])
    wov = C.ev_w_out[i].rearrange("(kc p) f -> p kc f", p=128)
    sc.dma("pool", "dw", WO[:, 0:4, :], wov[:, 0:4, :], writes=[r_WO[0]])
    sc.dma("pool", "dw", WO[:, 4:8, :], wov[:, 4:8, :], writes=[r_WO[1]])
    r_w1 = [Res(f"ew1_{k}") for k in range(4)]
    for kv in range(2):
        src = C.nsa_w1[i, kv].rearrange("(l d) e -> d l e", d=64)
        for hh in range(2):
            sc.dma("pool", "dw", w1[hh * 64:(hh + 1) * 64, kv], src, writes=[r_w1[kv * 2 + hh]])
    r_w2 = Res("ew2")
    sc.dma("pool", "dw", w2, C.nsa_w2[i].rearrange("k e f -> e k f"), writes=[r_w2])
    r_pe = Res("epe")
    sc.dma("pool", "dw", peT, C.nsa_pe_in[i], writes=[r_pe])
    sc.dma("pool", "dio", Vc[0:127, 0, 65:97], C.ovl_in, writes=[rc])
    sc.dma("pool", "dio", Vc[0:127, 1, 65:97], C.ovl_in, reads=[rc], writes=[rc])
    sc.op("dve", lambda e: e.memset(Vc[:, :, 64:65], 1.0), reads=[rc], writes=[rc])
    sc.op("dve", lambda e: e.memset(kcA, 0.0), writes=[rc])

    RC = [rc]
    if DBG_STOP == 21:
        sc.barrier()
        return
    for kv in range(2):
        for l_ in range(32):
            sc.op("pe", lambda e, kv=kv, l_=l_: e.matmul(C.PS[0:64, 7, kv:kv + 1], lhsT=w1[0:64, kv, l_, :], rhs=peT[0:64, kv, l_:l_ + 1], start=(l_ == 0), stop=(l_ == 31)),
                  reads=r_w1 + [r_pe], writes=[C.pr[7]], count=(l_ == 31))
    sc.op("act", lambda e: e.activation(out=cbias, in_=C.PS[0:64, 7, 0:2], func=AF.Copy), reads=[C.pr[7]], writes=RC)
    if DBG_STOP == 22:
        sc.barrier()
        return
    for kv in range(2):
        srcT = kcT if kv == 0 else vcT
        s3 = srcT.rearrange("p (n s) -> p n s", s=16)
        for g in range(2):
            slot = kv * 2 + g
            for l_ in range(32):
                rhs = s3[64 * g:64 * g + 64, 0:127, l_] if l_ < 16 else s3[64 * g:64 * g + 64, 1:128, l_ - 16]
                sc.op("pe", lambda e, l_=l_, rhs=rhs, slot=slot, kv=kv, g=g: e.matmul(C.PS[0:64, 6 - g, slot * 128:slot * 128 + 127], lhsT=w1[64 * g:64 * g + 64, kv, l_, :], rhs=rhs,
                                                                                 start=(l_ == 0), stop=(l_ == 31)),
                      reads=r_w1, writes=[C.pr[6 - g]], count=(l_ == 31))
        for g in range(2):
            slot = kv * 2 + g
            sc.op("act", lambda e, slot=slot, kv=kv, g=g: e.activation(out=gh[:, slot, 0:127], in_=C.PS[0:64, 6 - g, slot * 128:slot * 128 + 127], func=AF.Gelu_apprx_tanh,
                                                                 bias=cbias[:, kv:kv + 1]),
                  reads=[C.pr[6 - g]] + RC, writes=RC)
    if DBG_STOP == 23:
        sc.barrier()
        return
    for kv in range(2):
        for g in range(2):
            sc.op("pe", lambda e, kv=kv, g=g: e.matmul(C.PS[0:127, 7, 128 + (kv * 2 + g) * 64:128 + (kv * 2 + g) * 64 + 64], lhsT=gh[:, kv * 2 + g, 0:127], rhs=w2[:, kv, :],
                                                     start=True, stop=True),
                  reads=RC + [r_w2], writes=[C.pr[7]], count=True)
    kc_ps = C.PS[0:127, 7, 128:256].rearrange("p (g d) -> p g d", d=64)
    vc_ps = C.PS[0:127, 7, 256:384].rearrange("p (g d) -> p g d", d=64)
    sc.op("act", lambda e: e.activation(out=Vc[0:127, :, 0:64], in_=vc_ps, func=AF.Copy), reads=[C.pr[7]] + RC, writes=RC)
    if DBG_STOP == 24:
        sc.barrier()
        return
    cb_ = cosc[0:127, :].unsqueeze(1).to_broadcast([127, 2, 32])
    sb_ = sinc[0:127, :].unsqueeze(1).to_broadcast([127, 2, 32])
    r_ct = Res("ectmp")
    rope_ops(C, kc_ps, rkc[0:127], cb_, sb_, [x[0:127] for x in ctmp], [C.pr[7], r_cst[8]], r_ct, rc)
    if DBG_STOP == 25:
        sc.barrier()
        return
    sc.op("act", lambda e: e.activation(out=kcA[0:127, :], in_=rkc[0:127].rearrange("p g d -> p (g d)"), func=AF.Copy), reads=RC, writes=RC)
    if DBG_STOP == 26:
        sc.barrier()
        return
    sc.op("pe", lambda e: e.transpose(out=psb(C, 6)[:, 0:127], in_=kcA[0:127, :], identity=C.identb[0:127, 0:127]), reads=RC + [C.r_const], writes=[C.pr[6]])
    sc.op("act", lambda e: e.activation(out=kcmpT[:, 0:127], in_=psb(C, 6)[:, 0:127], func=AF.Copy), reads=[C.pr[6]], writes=RC)
    sc.barrier()
    if DBG_STOP == 2:
        return

    r_P = [Res(f"eP{k}") for k in range(NP)]
    r_SB1 = [Res("eSB1a"), Res("eSB1b")]
    r_SB2 = [Res("eSB2a"), Res("eSB2b")]
    r_o = [Res("eo0"), Res("eo1")]
    r_ot = [Res("eot0"), Res("eot1")]
    r_ob = [Res("eob0"), Res("eob1")]
    r_sm = [Res("esm0"), Res("esm1")]
    r_imp = [Res("eimp0"), Res("eimp1")]
    r_vs = [Res("evs0"), Res("evs1")]
    oacc2 = [oacc, al([128, 4, 64], F32)]
    otmp2 = [otmp, al([128, 4, 64], F32)]
    sm42 = [sm4, al([128, 16, 4], F32)]
    impn2 = [impn, al([128, 4, 32], F32)]
    vsel2 = [vsel, al([128, 32], F32)]
    top82 = [top8, al([128, 8], F32)]
    pc = [0]
    BANK_SC = (0, 1)
    B_C, B_T = 2, 5
    B_Sg, B_Wg = (3, 6), (4, 7)
    scn = [0]

    def score_tile(main_lhsT, main_rhs, extra, npart, rds):
        b = BANK_SC[scn[0] % 2]
        scn[0] += 1
        mm = [(main_lhsT, main_rhs)] + extra
        for k_, (lt, rh) in enumerate(mm):
            sc.op("pe", lambda e, lt=lt, rh=rh, k_=k_: e.matmul(C.PS[0:npart, b, :], lhsT=lt, rhs=rh, start=(k_ == 0), stop=(k_ == len(mm) - 1)),
                  reads=rds, writes=[C.pr[b]], count=(k_ == len(mm) - 1))
        k = pc[0] % NP
        pc[0] += 1
        sc.op("act", lambda e: e.activation(out=Pt[k][0:npart, :], in_=C.PS[0:npart, b, :], func=AF.Exp), reads=[C.pr[b]], writes=[r_P[k]])
        return Pt[k], r_P[k]

    def chain(t, g):
        tsl = slice(t * 128, (t + 1) * 128)
        gp = slice(64 * g, 64 * g + 64)
        rhs_q = qT[gp, :, tsl]
        s1 = g
        B_S, B_W = B_Sg[g], B_Wg[g]
        oacc_, otmp_, sm4_, impn_, vsel_, top8_ = oacc2[g], otmp2[g], sm42[g], impn2[g], vsel2[g], top82[g]
        zc = slice(g * 256, g * 256 + 128)
        zc2 = slice(g * 256 + 128, g * 256 + 256)
        r0 = 64 * g
        zin = Zall[:, t, 0:36] if g == 0 else Zall[:, t, 36:136]
        zrows = 36 if g == 0 else 100
        RS = slice(r0, r0 + 36)
        sc.op("pe", lambda e: e.transpose(out=psb(C, B_T)[0:zrows, zc], in_=zin, identity=C.identb[:]), reads=[C.r_const], writes=[C.pr[B_T]])
        sc.op("dve", lambda e: e.tensor_tensor(out=SB1[s1][RS, :].rearrange("p (h t) -> p h t", t=128), in0=psb(C, B_T)[RS, zc].unsqueeze(1).to_broadcast([36, 4, 128]),
                                              in1=BM[RS, :].rearrange("p (h t) -> p h t", t=128), op=ALU.mult),
              reads=[C.pr[B_T], r_cst[5]], writes=[r_SB1[s1]])
        yield
        bw0 = 127 - 8 * t
        Pc, rPc = score_tile(kcmpT[gp, 0:127], rhs_q, [(EB[RS, 0:127], SB1[s1][RS, :]), (Bwin[r0:r0 + 9, bw0:bw0 + 127], Mpat[r0:r0 + 9, :])], 127,
                             [r_SB1[s1], r_cst[2], r_cst[6], r_cst[7]])
        yield
        for h in range(4):
            sc.op("pe", lambda e, h=h: e.matmul(C.PS[:, B_C, h * 97:(h + 1) * 97], lhsT=Pc[0:127, h * 128:(h + 1) * 128], rhs=Vc[0:127, g, :], start=True, stop=True),
                  reads=[rPc], writes=[C.pr[B_C]], count=(h == 3))
        accc = C.PS[:, B_C, 0:388].rearrange("p (h c) -> p h c", c=97)
        gt = gates[:, t, 12 * g:12 * g + 12].rearrange("p (h b) -> p h b", b=3)
        dc, fc = sm4_[:, 0, :], sm4_[:, 1, :]
        sc.op("dve", lambda e: e.tensor_scalar(out=dc, in0=accc[:, :, 64], scalar1=1e-30, scalar2=None, op0=ALU.add), reads=[C.pr[B_C]], writes=[r_sm[g]])
        sc.op("dve", lambda e: e.reciprocal(out=dc, in_=dc), reads=[r_sm[g]], writes=[r_sm[g]])
        sc.op("dve", lambda e: e.tensor_tensor(out=fc, in0=dc, in1=gt[:, :, 0], op=ALU.mult), reads=[r_sm[g]], writes=[r_sm[g]])
        sc.op("dve", lambda e: e.tensor_tensor(out=oacc_, in0=accc[:, :, 0:64], in1=fc.unsqueeze(2).to_broadcast([128, 4, 64]), op=ALU.mult),
              reads=[C.pr[B_C], r_sm[g]], writes=[r_o[g]])
        sc.op("dve", lambda e: e.tensor_tensor(out=impn_, in0=accc[:, :, 65:97], in1=dc.unsqueeze(2).to_broadcast([128, 4, 32]), op=ALU.mult),
              reads=[C.pr[B_C], r_sm[g]], writes=[r_imp[g]])
        yield
        sc.op("dve", lambda e: e.tensor_reduce(out=vsel_, in_=impn_.rearrange("p h j -> p j h"), axis=AX.X, op=ALU.add), reads=[r_imp[g]], writes=[r_vs[g]])
        sc.op("dve", lambda e: e.tensor_tensor(out=vsel_, in0=vsel_, in1=Fpos[:, t, :], op=ALU.max), reads=[r_vs[g], r_cst[0]], writes=[r_vs[g]])
        sc.op("dve", lambda e: e.tensor_tensor(out=vsel_, in0=vsel_, in1=Fneg[:, t, :], op=ALU.add), reads=[r_vs[g], r_cst[1]], writes=[r_vs[g]])
        sc.op("dve", lambda e: e.max(out=top8_, in_=vsel_), reads=[r_vs[g]], writes=[r_imp[g]])
        sc.op("dve", lambda e: e.tensor_scalar(out=vsel_, in0=vsel_, scalar1=top8_[:, 7:8], scalar2=1.0, op0=ALU.is_ge, op1=ALU.subtract), reads=[r_vs[g], r_imp[g]], writes=[r_vs[g]])
        zsel = Zall[:, t, 0:32] if g == 0 else Zall[:, t, 100:132]
        sc.op("dve", lambda e: e.tensor_scalar(out=zsel, in0=vsel_, scalar1=-NEG, scalar2=None, op0=ALU.mult), reads=[r_vs[g]], writes=[r_SB2[s1]])
        yield
        sc.op("pe", lambda e: e.transpose(out=psb(C, B_T)[0:zrows, zc2], in_=zin, identity=C.identb[:]), reads=[r_SB2[s1], C.r_const], writes=[C.pr[B_T]])
        sc.op("dve", lambda e: e.tensor_tensor(out=SB2[s1][RS, :].rearrange("p (h t) -> p h t", t=128), in0=psb(C, B_T)[RS, zc2].unsqueeze(1).to_broadcast([36, 4, 128]),
                                              in1=BM[RS, :].rearrange("p (h t) -> p h t", t=128), op=ALU.mult),
              reads=[C.pr[B_T], r_cst[5]], writes=[r_SB2[s1]])
        yield
        jobs = []
        kts = list(range(max(0, t - 4), t + 1))
        for n_, kt in enumerate(kts):
            ksl = slice(kt * 128, (kt + 1) * 128)
            extra = [(EB[RS, ksl], SB1[s1][RS, :])]
            if kt == t:
                extra.append((C.identb[:], triB[:, :]))
            elif kt == t - 4:
                extra.append((C.identb[:], bandB[:, :]))
            jobs.append((kwT[gp, ksl], extra, [r_SB1[s1], r_cst[2], r_cst[3], r_cst[4], C.r_const], B_W, Vw, kt, n_ == 0, n_ == len(kts) - 1))
        for kt in range(t + 1):
            ksl = slice(kt * 128, (kt + 1) * 128)
            extra = [(EB[RS, ksl], SB2[s1][RS, :])]
            if kt == t:
                extra.append((C.identb[:], triB[:, :]))
            jobs.append((ksT[gp, ksl], extra, [r_SB2[s1], r_cst[2], r_cst[3], C.r_const], B_S, Vs, kt, kt == 0, kt == t))
        prev = None
        for n_ in range(len(jobs) + 1):
            cur = None
            if n_ < len(jobs):
                lt, extra, rds, bk, Vt, kt, first, last = jobs[n_]
                P_, rP_ = score_tile(lt, rhs_q, extra, 128, rds)
                cur = (P_, rP_, bk, Vt, kt, first, last)
            if prev is not None:
                P_, rP_, bk, Vt, kt, first, last = prev
                for h in range(4):
                    sc.op("pe", lambda e, h=h, P_=P_, bk=bk, Vt=Vt, kt=kt, first=first, last=last: e.matmul(
                        C.PS[:, bk, h * 65:(h + 1) * 65], lhsT=P_[:, h * 128:(h + 1) * 128], rhs=Vt[:, kt, g, :], start=(first and h == 0), stop=(last and h == 3)),
                          reads=[rP_], writes=[C.pr[bk]], count=(h == 3))
            prev = cur
            yield
        for (bk, gi_, dd, ff) in ((B_W, 2, sm4_[:, 2, :], sm4_[:, 3, :]), (B_S, 1, sm4_[:, 4, :], sm4_[:, 5, :])):
            acc = C.PS[:, bk, 0:260].rearrange("p (h c) -> p h c", c=65)
            sc.op("dve", lambda e, acc=acc, dd=dd: e.reciprocal(out=dd, in_=acc[:, :, 64]), reads=[C.pr[bk]], writes=[r_sm[g]])
            sc.op("dve", lambda e, dd=dd, ff=ff, gi_=gi_: e.tensor_tensor(out=ff, in0=dd, in1=gt[:, :, gi_], op=ALU.mult), reads=[r_sm[g]], writes=[r_sm[g]])
            sc.op("dve", lambda e, acc=acc, ff=ff: e.tensor_tensor(out=otmp_, in0=acc[:, :, 0:64], in1=ff.unsqueeze(2).to_broadcast([128, 4, 64]), op=ALU.mult),
                  reads=[C.pr[bk], r_sm[g]], writes=[r_ot[g]])
            sc.op("pool", lambda e: e.tensor_tensor(out=oacc_, in0=oacc_, in1=otmp_, op=ALU.add), reads=[r_ot[g], r_o[g]], writes=[r_o[g]])
            yield
        sc.op("act", lambda e: e.activation(out=ob[:, 256 * g:256 * g + 256], in_=oacc_.rearrange("p h d -> p (h d)"), func=AF.Copy), reads=[r_o[g]], writes=[r_ob[g]])

    def finish_tile(t):
        tsl = slice(t * 128, (t + 1) * 128)
        for c in range(4):
            sc.op("pe", lambda e, c=c: e.transpose(out=psb(C, B_T)[:, 512 + c * 128:512 + (c + 1) * 128], in_=ob[:, c * 128:(c + 1) * 128], identity=C.identb[:]),
                  reads=r_ob + [C.r_const], writes=[C.pr[B_T]], count=(c == 3))
        sc.op("act", lambda e: e.activation(out=mixT[:, 4:8, tsl], in_=psb(C, B_T)[:, 512:1024].rearrange("p (c t) -> p c t", t=128), func=AF.Copy),
              reads=[C.pr[B_T]], writes=[r_mix[t]])

    active = {0: (0, chain(0, 0)), 1: (0, chain(0, 1))}
    done = [0] * NT
    for _ in range(9):
        next(active[0][1])
    while active:
        for g in (0, 1):
            if g not in active:
                continue
            t, gen = active[g]
            try:
                next(gen)
            except StopIteration:
                done[t] += 1
                if done[t] == 2:
                    finish_tile(t)
                if t + 1 < NT:
                    active[g] = (t + 1, chain(t + 1, g))
                else:
                    del active[g]
    if DBG_STOP == 3:
        return
    out_proj(C, mixT, r_mix, WO, r_WO)
```

```python
import numpy as np
import concourse.bass as bass
import concourse.mybir as mybir
from concourse.bass_utils import run_bass_kernel_spmd

F32 = mybir.dt.float32
BF16 = mybir.dt.bfloat16
AF = mybir.ActivationFunctionType
ALU = mybir.AluOpType
AX = mybir.AxisListType

D = 1024
S = 2048
NT = S // 128
FF = 2816
NFC = FF // 128
DEPTH = 4
RMS_EPS = 1e-6


class Res:
    __slots__ = ("name", "w", "r", "excl")

    def __init__(self, name="", excl=False):
        self.name = name
        self.w = None
        self.r = {}
        self.excl = excl


class Sched:
    ENG = ("pe", "act", "dve", "pool", "sp")

    def __init__(self, nc, dma_keys=()):
        self.nc = nc
        self.dkeys = []
        self.e = dict(pe=nc.tensor, act=nc.scalar, dve=nc.vector, pool=nc.gpsimd, sp=nc.sync)
        self.sem = {}
        self.cnt = {}
        for k in self.ENG + tuple(dma_keys):
            self.sem[k] = nc.alloc_semaphore("s_" + k)
            self.cnt[k] = 0
        self.seen = {k: {} for k in self.ENG}
        self.nins = 0

    def _wait(self, eng, key, val):
        if val <= self.seen[eng].get(key, 0):
            return
        self.e[eng].wait_ge(self.sem[key], val)
        self.seen[eng][key] = val
        self.nins += 1

    def _deps(self, eng, reads, writes, async_=False):
        need = {}

        def add(t):
            if t is None:
                return
            k, v = t
            if need.get(k, 0) < v:
                need[k] = v

        for r in reads:
            add(r.w)
            if r.excl:
                for k, v in r.r.items():
                    if k != eng:
                        add((k, v))
        for w in writes:
            if async_ or (w.w is not None and w.w[0] != eng):
                add(w.w)
            for k, v in w.r.items():
                if async_ or k != eng:
                    add((k, v))
        for k, v in need.items():
            if k == eng and eng == "pe":
                continue
            self._wait(eng, k, v)

    def op(self, eng, fn, reads=(), writes=(), count=True):
        self._deps(eng, reads, writes)
        ins = fn(self.e[eng])
        self.nins += 1
        if count:
            self.cnt[eng] += 1
            ins.then_inc(self.sem[eng], 1)
            v = self.cnt[eng]
        else:
            v = self.cnt[eng] + 1
        for r in reads:
            if r.r.get(eng, 0) < v:
                r.r[eng] = v
        for w in writes:
            w.w = (eng, v)
            w.r = {}
        return ins

    def dma(self, q, dkey, out, in_, reads=(), writes=()):
        res0 = writes[0] if len(writes) > 0 else reads[0]
        dkey = "d_" + res0.name
        if dkey not in self.sem:
            self.sem[dkey] = self.nc.alloc_semaphore("s_" + dkey)
            self.cnt[dkey] = 0
            self.dkeys.append(dkey)
        self._deps(q, reads, writes, async_=True)
        ins = self.e[q].dma_start(out=out, in_=in_)
        self.nins += 1
        self.cnt[dkey] += 16
        ins.then_inc(self.sem[dkey], 16)
        v = self.cnt[dkey]
        for r in reads:
            if r.r.get(dkey, 0) < v:
                r.r[dkey] = v
        for w in writes:
            w.w = (dkey, v)
            w.r = {}
        return ins

    def barrier(self):
        snap = dict(self.cnt)
        for eng in self.ENG:
            for k, v in snap.items():
                if v > 0 and k != eng:
                    self._wait(eng, k, v)
            if eng != "pe" and snap[eng] > 0:
                self._wait(eng, eng, snap[eng])

    def finish(self, keys=None):
        for k in self.dkeys:
            if self.cnt[k] > 0:
                self._wait("sp", k, self.cnt[k])


class Ctx:
    pass


ARENA = 140 * 1024


def aview(C, off, shape, dtype):
    esz = 4 if dtype == F32 else 2
    n = 1
    for d in shape[1:]:
        n *= d
    nb = n * esz
    assert off % 4 == 0 and nb % 4 == 0 and off + nb <= ARENA, (off, shape)
    v = C.AR[0:shape[0], off // 4:(off + nb) // 4]
    if dtype != F32:
        v = v.bitcast(dtype)
    if len(shape) == 2:
        return v
    names = ["a", "b", "c", "d"][:len(shape) - 1]
    pat = "p (" + " ".join(names) + ") -> p " + " ".join(names)
    return v.rearrange(pat, **{nm: d for nm, d in zip(names[1:], shape[2:])})


def build(stages, n_layers_inputs=True):
    nc = bass.Bass("TRN2", target_bir_lowering=False)
    C = Ctx()
    C.nc = nc
    sc = Sched(nc)
    C.sc = sc

    def din(name, shape):
        return nc.dram_tensor(name, list(shape), F32, kind="ExternalInput").ap()

    C.x_in = din("x", [S, D])
    C.gT_in = din("gT_h", [128, 13 * 8])
    C.fin_g_in = din("fin_g_h", [128, D])
    C.ident_in = din("ident_h", [128, 128])
    C.wg = din("ffn_w_gate", [DEPTH, 2, D, FF])
    C.wu = din("ffn_w_up", [DEPTH, 2, D, FF])
    C.wd = din("ffn_w_down", [DEPTH, 2, FF, D])
    C.y_out = nc.dram_tensor("y", [S, D], F32, kind="ExternalOutput").ap()
    C.ev_w_in = din("ev_w_in", [2, D, 2328])
    C.ev_w_out = din("ev_w_out", [2, D, D])
    C.nsa_w1 = din("nsa_cmp_w1", [2, 2, 2048, 64])
    C.nsa_w2 = din("nsa_cmp_w2", [2, 2, 64, 64])
    C.nsa_pe_in = din("nsa_pe_h", [2, 128, 2, 32])
    C.rope_in = din("rope_h", [2, 128, NT, 32])
    C.ropec_in = din("ropec_h", [2, 128, 32])
    C.gm_ws_in = din("gm_ws_h", [2, 128, 8, 128])
    C.gm_b_in = din("gm_b_h", [2, 128, 8])
    C.fsel_in = din("fsel_h", [2, 128, NT, 32])
    C.eb_in = din("eb_h", [100, S])
    C.trib_in = din("trib_h", [128, 512])
    C.bandb_in = din("bandb_h", [128, 512])
    C.bm_in = din("bm_h", [100, 512])
    C.mpat_in = din("mpat_h", [73, 512])
    C.bwin_in = din("bwin_h", [73, 256])
    C.ovl_in = din("ovl_h", [127, 32])
    C.od_w_in = din("od_w_in", [2, D, D])
    C.od_w_out = din("od_w_out", [2, D, D])
    C.pool_w = din("pool_w", [2, 4, 128, 128])
    C.s5_w_glu = din("s5_w_glu", [2, 512, 1024])
    C.psc_in = din("psc_h", [128, 8])
    C.s5lam_in = din("s5lam_h", [128, 3, 32])
    C.s5d_in = din("s5d_h", [128, 8])
    C.inv16_in = din("inv16_h", [128, 16])
    C.triT_in = din("triT_h", [128, 128])
    C.s5b = din("s5b_h", [2, 4, 128, 16, 16])

    C.X = nc.alloc_sbuf_tensor("X", [128, NT, D], F32)
    C.xr = [Res(f"x{t}") for t in range(NT)]
    C.identf = nc.alloc_sbuf_tensor("identf", [128, 128], F32)
    C.identb = nc.alloc_sbuf_tensor("identb", [128, 128], BF16)
    C.gT = nc.alloc_sbuf_tensor("gT", [128, 13 * 8], F32)
    C.cm05 = nc.alloc_sbuf_tensor("cm05", [128, 1], F32)
    C.stat = nc.alloc_sbuf_tensor("stat", [128, 64], F32)
    C.r_const = Res("const")
    C.PS = nc.alloc_psum_tensor("PS", [128, 8, 512], F32)
    C.pr = [Res(f"ps{b}", excl=True) for b in range(8)]

    C.AR = nc.alloc_sbuf_tensor("AR", [128, ARENA // 4], F32)
    C.hT = aview(C, 0, [128, 8, 1024], BF16)
    C.r_hT = [Res(f"hT{t}") for t in range(8)]
    C.hT2 = aview(C, 116736, [128, 8, 1024], BF16)
    C.r_hT2 = [Res(f"hTb{t}") for t in range(8)]
    C.aT = aview(C, 16384, [128, NFC, 1024], BF16)
    C.r_aT = [[Res(f"aT{f}_{b}") for b in range(2)] for f in range(NFC)]
    C.NWS = 2
    C.wgb = [aview(C, 61440 + 8192 * i, [128, 8, 512], BF16) for i in range(C.NWS)]
    C.wub = [aview(C, 77824 + 8192 * i, [128, 8, 512], BF16) for i in range(C.NWS)]
    C.r_wg = [Res(f"wg{i}") for i in range(C.NWS)]
    C.r_wu = [Res(f"wu{i}") for i in range(C.NWS)]
    C.NWD = 8
    C.wdb = [aview(C, 94208 + 2048 * i, [128, 2, 512], BF16) for i in range(4)] + \
            [aview(C, 133120 + 2048 * i, [128, 2, 512], BF16) for i in range(4)]
    C.r_wd = [Res(f"wd{i}") for i in range(C.NWD)]
    C.xs = [aview(C, 102400 + 4096 * i, [128, D], F32) for i in range(2)]
    C.r_xs = [Res(f"xs{i}") for i in range(2)]
    C.sg = [aview(C, 110592 + 2048 * i, [128, 512], F32) for i in range(2)]
    C.r_sg = [Res(f"sg{i}") for i in range(2)]
    C.junk = aview(C, 114688, [128, D], BF16)
    C.r_junk = Res("junk")
    C.ffn_xs, C.ffn_junk = C.xs, C.junk
    C.psc = nc.alloc_sbuf_tensor("psc", [128, 8], F32)
    C.s5lam = nc.alloc_sbuf_tensor("s5lam", [128, 3, 32], F32)
    C.s5d = nc.alloc_sbuf_tensor("s5d", [128, 8], F32)
    C.inv16 = nc.alloc_sbuf_tensor("inv16", [128, 16], F32)
    C.triT = nc.alloc_sbuf_tensor("triT", [128, 128], BF16)
    C.halfpi = nc.alloc_sbuf_tensor("halfpi", [128, 1], F32)
    C.c05 = nc.alloc_sbuf_tensor("c05", [128, 1], F32)
    C.r_stat = [Res(f"stat{i}") for i in range(4)]
    C.ctr = dict(wgu=0, wd=0, xs=0, sg=0, st=0, tp=0, gu=0)

    r_c1, r_c2 = Res("c_ident"), Res("c_gT")
    sc.dma("sp", "dio", C.identf[:], C.ident_in, writes=[r_c1])
    sc.dma("sp", "dio", C.gT[:], C.gT_in, writes=[r_c2])
    r_cs = [Res(f"c_s{k}") for k in range(5)]
    sc.dma("sp", "dio", C.psc[:], C.psc_in, writes=[r_cs[0]])
    sc.dma("sp", "dio", C.s5lam[:], C.s5lam_in, writes=[r_cs[1]])
    sc.dma("sp", "dio", C.s5d[:], C.s5d_in, writes=[r_cs[2]])
    sc.dma("sp", "dio", C.inv16[:], C.inv16_in, writes=[r_cs[3]])
    sc.dma("pool", "dio", C.triT[:], C.triT_in, writes=[r_cs[4]])
    sc.op("dve", lambda e: e.memset(C.halfpi[:], float(np.pi / 2)), reads=r_cs, writes=[C.r_const])
    sc.op("dve", lambda e: e.memset(C.c05[:], 0.5), writes=[C.r_const])
    sc.op("dve", lambda e: e.memset(C.cm05[:], -0.5), reads=[r_c1, r_c2], writes=[C.r_const])
    sc.op("dve", lambda e: e.tensor_copy(out=C.identb[:], in_=C.identf[:]), reads=[C.r_const], writes=[C.r_const])
    for t in range(NT):
        q = "sp" if t % 2 == 0 else "act"
        sc.dma(q, "dx", C.X[:, t, :], C.x_in[t * 128:(t + 1) * 128, :], writes=[C.xr[t]])

    prev_kind = None
    for st in stages:
        if not (prev_kind == "ffn" and st[0] in ("ffn", "final")):
            sc.barrier()
        prev_kind = st[0]
        C.xs, C.junk = C.ffn_xs, C.ffn_junk
        if st[0] == "ffn":
            ffn_block(C, st[1], st[2])
        elif st[0] == "mix" and st[1] % 2 == 1:
            odd_mixer(C, st[1])
        elif st[0] == "mix":
            even_mixer(C, st[1])
        elif st[0] == "final":
            final_norm(C)
        else:
            raise ValueError(st)

    for t in range(NT):
        q = "sp" if t % 2 == 0 else "act"
        sc.dma(q, "dx", C.y_out[t * 128:(t + 1) * 128, :], C.X[:, t, :], reads=[C.xr[t]])
    sc.finish()
    return nc, C


def rstd_of_tile(C, t):
    sc = C.sc
    i = C.ctr["st"] % 4
    C.ctr["st"] += 1
    r = C.r_stat[i]
    ss = C.stat[:, 2 * i:2 * i + 1]
    rs = C.stat[:, 2 * i + 1:2 * i + 2]
    sc.op("act", lambda e: e.activation(out=C.junk[:], in_=C.X[:, t, :], func=AF.Square, accum_out=ss),
          reads=[C.xr[t], C.r_junk], writes=[C.r_junk, r])
    sc.op("dve", lambda e: e.tensor_scalar(out=ss, in0=ss, scalar1=1.0 / D, scalar2=RMS_EPS, op0=ALU.mult, op1=ALU.add),
          reads=[r], writes=[r])
    sc.op("pool", lambda e: e.tensor_tensor(out=rs, in0=ss, in1=C.cm05[:], op=ALU.pow),
          reads=[r, C.r_const], writes=[r])
    return rs, r


def norm_transpose(C, t, gcol, dst, r_dst):
    st = norm_part1(C, t)
    norm_part2(C, st, gcol, dst, r_dst)


def norm_part1(C, t):
    sc = C.sc
    rs, r = rstd_of_tile(C, t)
    i = C.ctr["xs"] % 2
    C.ctr["xs"] += 1
    xs, rxs = C.xs[i], C.r_xs[i]
    sc.op("act", lambda e: e.activation(out=xs, in_=C.X[:, t, :], func=AF.Copy, scale=rs),
          reads=[C.xr[t], r], writes=[rxs])
    return xs, rxs


def norm_part2(C, st, gcol, dst, r_dst):
    sc = C.sc
    xs, rxs = st
    j = C.ctr["tp"] % 2
    C.ctr["tp"] += 1
    b0 = 4 + 2 * j
    pst = C.PS[:, b0:b0 + 2, :].rearrange("p b (k t) -> p (b k) t", t=128)
    for kc in range(8):
        sc.op("pe", lambda e, kc=kc: e.transpose(out=pst[:, kc, :], in_=xs[:, kc * 128:(kc + 1) * 128], identity=C.identf[:]),
              reads=[rxs, C.r_const], writes=[C.pr[b0], C.pr[b0 + 1]], count=(kc == 7))
    sc.op("dve", lambda e: e.tensor_tensor(out=dst, in0=pst, in1=gcol.unsqueeze(2).to_broadcast([128, 8, 128]), op=ALU.mult),
          reads=[C.pr[b0], C.pr[b0 + 1], C.r_const], writes=[r_dst])


def ffn_block(C, l, j):
    sc = C.sc
    nc = C.nc
    ni = l * 3 + (0 if j == 0 else 2)
    gcol = C.gT[:, ni * 8:(ni + 1) * 8]
    wgv = C.wg[l, j].rearrange("(kc p) f -> p kc f", p=128)
    wuv = C.wu[l, j].rearrange("(kc p) f -> p kc f", p=128)
    wdv = C.wd[l, j].rearrange("(fc p) d -> p fc d", p=128)
    groups = [(g * 512, 512) for g in range(5)] + [(2560, 256)]
    hTs = [C.hT, C.hT2]
    r_hTs = [C.r_hT, C.r_hT2]

    def phase_a(tg):
        hT_, r_hT_ = hTs[tg % 2], r_hTs[tg % 2]
        for tt in range(8):
            st = norm_part1(C, tg * 8 + tt)
            yield
            norm_part2(C, st, gcol, hT_[:, :, tt * 128:(tt + 1) * 128], r_hT_[tt])
            yield

    def pass1(tg):
        hT_, r_hT_ = hTs[tg % 2], r_hTs[tg % 2]
        for (f0, fw) in groups:
            s = C.ctr["wgu"] % C.NWS
            C.ctr["wgu"] += 1
            sc.dma("pool", "dw", C.wgb[s][:, :, 0:fw], wgv[:, :, f0:f0 + fw], writes=[C.r_wg[s]])
            sc.dma("pool", "dw", C.wub[s][:, :, 0:fw], wuv[:, :, f0:f0 + fw], writes=[C.r_wu[s]])
            for tb in range(2):
                for fci in range(fw // 128):
                    fc = f0 // 128 + fci
                    gi = C.ctr["gu"] % 2
                    C.ctr["gu"] += 1
                    bG, bU = gi, 2 + gi
                    hr = r_hT_[tb * 4:(tb + 1) * 4]
                    for kc in range(8):
                        sc.op("pe", lambda e, kc=kc: e.matmul(C.PS[:, bG, :], lhsT=C.wgb[s][:, kc, fci * 128:(fci + 1) * 128],
                                                            rhs=hT_[:, kc, tb * 512:(tb + 1) * 512], start=(kc == 0), stop=(kc == 7)),
                              reads=[C.r_wg[s]] + hr, writes=[C.pr[bG]], count=(kc == 7))
                    for kc in range(8):
                        sc.op("pe", lambda e, kc=kc: e.matmul(C.PS[:, bU, :], lhsT=C.wub[s][:, kc, fci * 128:(fci + 1) * 128],
                                                            rhs=hT_[:, kc, tb * 512:(tb + 1) * 512], start=(kc == 0), stop=(kc == 7)),
                              reads=[C.r_wu[s]] + hr, writes=[C.pr[bU]], count=(kc == 7))
                    si = C.ctr["sg"] % 2
                    C.ctr["sg"] += 1
                    sc.op("act", lambda e: e.activation(out=C.sg[si], in_=C.PS[:, bG, :], func=AF.Silu),
                          reads=[C.pr[bG]], writes=[C.r_sg[si]])
                    sc.op("dve", lambda e: e.tensor_tensor(out=C.aT[:, fc, tb * 512:(tb + 1) * 512], in0=C.sg[si], in1=C.PS[:, bU, :], op=ALU.mult),
                          reads=[C.r_sg[si], C.pr[bU]], writes=[C.r_aT[fc][tb]])
                    yield

    side0 = phase_a(0)
    for _ in range(8):
        next(side0)
    for tg in range(2):
        main = pass1(tg)
        side = phase_a(tg + 1) if tg + 1 < 2 else iter(())
        nmain = 0
        for _ in main:
            nmain += 1
            if tg == 0 and nmain <= 4:
                next(side0, None)
                next(side0, None)
            elif nmain % 5 in (0, 3):
                next(side, None)
        for _ in side:
            pass
        for db in range(2):
            for fp in range(NFC // 2):
                s = C.ctr["wd"] % C.NWD
                C.ctr["wd"] += 1
                sc.dma("pool", "dw", C.wdb[s], wdv[:, 2 * fp:2 * fp + 2, db * 512:(db + 1) * 512], writes=[C.r_wd[s]])
                for fi in range(2):
                    fc = 2 * fp + fi
                    for tt in range(8):
                        sc.op("pe", lambda e, fi=fi, tt=tt, fc=fc: e.matmul(C.PS[:, tt, :], lhsT=C.aT[:, fc, tt * 128:(tt + 1) * 128], rhs=C.wdb[s][:, fi, :],
                                                                          start=(fc == 0), stop=(fc == NFC - 1)),
                              reads=[C.r_wd[s], C.r_aT[fc][tt // 4]], writes=[C.pr[tt]], count=(fc == NFC - 1 or (fi == 1 and tt == 7)))
            for tt in range(8):
                t = tg * 8 + tt
                xv = C.X[:, t, db * 512:(db + 1) * 512]
                sc.op("dve", lambda e, tt=tt, xv=xv: e.scalar_tensor_tensor(out=xv, in0=C.PS[:, tt, :], scalar=0.5, in1=xv, op0=ALU.mult, op1=ALU.add),
                      reads=[C.pr[tt], C.xr[t]], writes=[C.xr[t]])


def final_norm(C):
    sc = C.sc
    fg = C.xs[0]
    sc.dma("sp", "dio", fg, C.fin_g_in, writes=[C.r_xs[0]])
    for t in range(NT):
        rs, r = rstd_of_tile(C, t)
        sc.op("dve", lambda e, t=t, rs=rs: e.scalar_tensor_tensor(out=C.X[:, t, :], in0=C.X[:, t, :], scalar=rs, in1=fg, op0=ALU.mult, op1=ALU.mult),
              reads=[C.xr[t], r, C.r_xs[0]], writes=[C.xr[t]])


def host_consts(inputs):
    nw = np.concatenate([inputs["norm_w"].reshape(12, D), inputs["final_norm_w"].reshape(1, D)], axis=0)
    gT = np.ascontiguousarray(nw.reshape(13, 8, 128).transpose(2, 0, 1).reshape(128, 13 * 8)).astype(np.float32)
    fin_g = np.ascontiguousarray(np.broadcast_to(inputs["final_norm_w"].reshape(1, D), (128, D))).astype(np.float32)
    f = lambda a: np.ascontiguousarray(a, dtype=np.float32)
    out = dict(gT_h=gT, fin_g_h=fin_g, ident_h=np.eye(128, dtype=np.float32))
    out["psc_h"] = f(inputs["pool_scale"].reshape(2, 4, 128).transpose(2, 0, 1).reshape(128, 8))
    def st_major(a):
        return a.reshape(2, 16, 2, 64).transpose(2, 3, 0, 1).reshape(128, 32)
    ldt = np.broadcast_to(inputs["s5_log_dt"][:, :, None], (2, 32, 64))
    out["s5lam_h"] = f(np.stack([st_major(inputs["s5_lam_re"]), st_major(inputs["s5_lam_im"]), st_major(ldt)], axis=1))
    def st_major_b(a):
        return a.reshape(2, 16, 2, 64, 16).transpose(0, 2, 3, 1, 4).reshape(2, 128, 16, 16)
    def st_major_c(a):
        return a.reshape(2, 16, 2, 16, 64).transpose(0, 2, 4, 1, 3).reshape(2, 128, 16, 16)
    out["s5b_h"] = f(np.stack([st_major_b(inputs["s5_b_re"]), st_major_b(inputs["s5_b_im"]),
                               st_major_c(inputs["s5_c_re"]), st_major_c(inputs["s5_c_im"])], axis=1))
    out["s5d_h"] = f(inputs["s5_d"].reshape(2, 4, 8, 16).transpose(2, 3, 0, 1).reshape(128, 8))
    out["inv16_h"] = f(np.broadcast_to(1.0 / np.arange(1, 17, dtype=np.float64)[None, :], (128, 16)))
    out["triT_h"] = f(np.triu(np.ones((128, 128))))
    out["gm_ws_h"] = f(inputs["gm_w_s"].transpose(0, 3, 1, 2))
    out["gm_b_h"] = f(inputs["gm_b"].transpose(0, 2, 1))
    pe = inputs["nsa_cmp_pe"]
    peT = pe.transpose(0, 3, 1, 2)
    out["nsa_pe_h"] = f(np.concatenate([peT, peT], axis=1))
    inv = 10000.0 ** (-np.arange(32, dtype=np.float64) / 32)
    pos = (np.arange(NT)[None, :] * 128 + np.arange(128)[:, None]).astype(np.float64)
    ang = (pos[:, :, None].astype(np.float32) * inv.astype(np.float32)[None, None, :]).astype(np.float32)
    out["rope_h"] = f(np.stack([np.cos(ang), np.sin(ang)], axis=0))
    posc = (np.arange(128) * 16 + 31).astype(np.float32)
    angc = (posc[:, None] * inv.astype(np.float32)[None, :]).astype(np.float32)
    out["ropec_h"] = f(np.stack([np.cos(angc), np.sin(angc)], axis=0))
    tl = np.arange(128)[:, None, None]
    ti = np.arange(NT)[None, :, None]
    jj = np.arange(32)[None, None, :]
    cur = (ti * 128 + tl) // 64
    forced = (jj == 0) | (jj == cur) | (jj == cur - 1)
    out["fsel_h"] = f(np.stack([np.where(forced, 1e4, 0.0), np.where(jj > cur, -1.0, 0.0)], axis=0))
    eb = np.zeros((36, S), np.float32)
    eb[np.arange(S) // 64, np.arange(S)] = 1.0
    eb[32:36, :] = 1.0
    eb2 = np.zeros((100, S), np.float32)
    eb2[0:36] = eb
    eb2[64:100] = eb
    out["eb_h"] = eb2
    sk = np.arange(128)[:, None]
    tq = np.arange(128)[None, :]
    tri = np.where(sk <= tq, 0.0, NEG).astype(np.float32)
    band = np.where(sk > tq, 0.0, NEG).astype(np.float32)
    out["trib_h"] = f(np.tile(tri, (1, 4)))
    out["bandb_h"] = f(np.tile(band, (1, 4)))
    bm = np.ones((36, 4, 128), np.float32)
    bm[32:36] = np.eye(4, dtype=np.float32)[:, :, None]
    bm2 = np.zeros((100, 512), np.float32)
    bm2[0:36] = bm.reshape(36, 512)
    bm2[64:100] = bm.reshape(36, 512)
    out["bm_h"] = bm2
    fq = np.floor((np.arange(128) - 31) / 16.0)
    mp = np.zeros((9, 128), np.float32)
    for r in range(8):
        mp[r] = np.where((r - 1) <= fq, 0.0, NEG)
    mp[8] = NEG
    mp2 = np.zeros((73, 512), np.float32)
    mp2[0:9] = np.tile(mp, (1, 4))
    mp2[64:73] = np.tile(mp, (1, 4))
    out["mpat_h"] = mp2
    bw = np.zeros((9, 256), np.float32)
    for c in range(256):
        cp = c - 127
        r = cp + 1
        if 0 <= r < 8:
            bw[r, c] = 1.0
        if cp >= 7:
            bw[8, c] = 1.0
    bw2 = np.zeros((73, 256), np.float32)
    bw2[0:9] = bw
    bw2[64:73] = bw
    out["bwin_h"] = bw2
    nn = np.arange(127)[:, None] * 16
    jb = np.arange(32)[None, :] * 64
    out["ovl_h"] = f(((nn < jb + 64) & (nn + 32 > jb)).astype(np.float32))
    return out


ALL_STAGES = []
for _l in range(DEPTH):
    ALL_STAGES += [("ffn", _l, 0), ("mix", _l), ("ffn", _l, 1)]
ALL_STAGES += [("final",)]


def host_inputs(inputs):
    shared = host_consts(inputs)
    for k in ("od_w_in", "od_w_out", "pool_w", "s5_w_glu", "ev_w_in", "ev_w_out", "nsa_cmp_w1", "nsa_cmp_w2"):
        shared[k] = np.asarray(inputs[k], np.float32)
    shared.update(ffn_w_gate=np.asarray(inputs["ffn_w_gate"], np.float32),
                  ffn_w_up=np.asarray(inputs["ffn_w_up"], np.float32),
                  ffn_w_down=np.asarray(inputs["ffn_w_down"], np.float32))
    return shared


def run(inputs, stages, core_ids, xs_per_core, trace=False):
    nc, C = build(stages)
    shared = host_inputs(inputs)
    in_maps = []
    for xc in xs_per_core:
        m = dict(shared)
        m["x"] = np.ascontiguousarray(xc, dtype=np.float32)
        in_maps.append(m)
    res = run_bass_kernel_spmd(nc, in_maps, core_ids=core_ids, trace=trace)
    return [r["y"] for r in res.results], res


def kernel(**inputs):
    x = np.asarray(inputs["x"], np.float32)
    ys, _ = run(inputs, ALL_STAGES, list(range(8)), [x[b] for b in range(8)])
    return np.stack(ys, axis=0).astype(np.float32)


KB = 1024
import os as _os
DBG_STOP = int(_os.environ.get('DBG_STOP', '0'))
POOL_WINDOWS = (2, 4, 8, 16)


def _cmul(C, eng_m, eng_a, ore, oim, are, aim, bre, bim, t, rd, wr):
    sc = C.sc
    sc.op(eng_m, lambda e: e.tensor_tensor(out=t[0], in0=are, in1=bre, op=ALU.mult), reads=rd, writes=wr)
    sc.op(eng_m, lambda e: e.tensor_tensor(out=t[1], in0=aim, in1=bim, op=ALU.mult), reads=rd, writes=wr)
    sc.op(eng_m, lambda e: e.tensor_tensor(out=t[2], in0=are, in1=bim, op=ALU.mult), reads=rd, writes=wr)
    sc.op(eng_m, lambda e: e.tensor_tensor(out=t[3], in0=aim, in1=bre, op=ALU.mult), reads=rd, writes=wr)
    sc.op(eng_a, lambda e: e.tensor_tensor(out=ore, in0=t[0], in1=t[1], op=ALU.subtract), reads=rd, writes=wr)
    sc.op(eng_a, lambda e: e.tensor_tensor(out=oim, in0=t[2], in1=t[3], op=ALU.add), reads=rd, writes=wr)


def odd_mixer(C, l):
    sc = C.sc
    nc = C.nc
    i = l // 2
    gcol = C.gT[:, (l * 3 + 1) * 8:(l * 3 + 2) * 8]
    mixT = aview(C, 0, [128, 8, S], BF16)
    uT = aview(C, 32 * KB, [128, 4, S], BF16)
    ygT = aview(C, 48 * KB, [128, 4, S], BF16)
    r_mix = [[Res(f"mix{m}_{b}") for b in range(4)] for m in range(8)]
    r_u = [[Res(f"u{m}_{b}") for b in range(4)] for m in range(4)]

    hT = aview(C, 64 * KB, [128, 8, S], BF16)
    r_h = [Res(f"oh{t}") for t in range(NT)]
    W1 = aview(C, 96 * KB, [128, 8, 1024], BF16)
    r_W1 = [Res("oW1a"), Res("oW1b")]
    zb = aview(C, 112 * KB, [128, S], F32)
    pa = aview(C, 120 * KB, [128, S], F32)
    pb = aview(C, 128 * KB, [128, S], F32)
    pl = aview(C, 136 * KB, [128, S], BF16)
    r_z, r_pa, r_pb, r_pl = Res("oz"), Res("opa"), Res("opb"), Res("opl")
    C.xs = [aview(C, 48 * KB + 4096 * k, [128, D], F32) for k in range(2)]
    C.junk = aview(C, 56 * KB, [128, D], BF16)
    PW = aview(C, 58 * KB, [128, 4, 128], BF16)
    r_PW = Res("oPW")
    w_in = C.od_w_in[i].rearrange("(kc p) f -> p kc f", p=128)
    sc.dma("pool", "dw", W1[:, 0:4, :], w_in[:, 0:4, :], writes=[r_W1[0]])
    sc.dma("pool", "dw", W1[:, 4:8, :], w_in[:, 4:8, :], writes=[r_W1[1]])
    sc.dma("pool", "dw", PW, C.pool_w[i].rearrange("g c d -> c g d"), writes=[r_PW])
    for t in range(NT):
        norm_transpose(C, t, gcol, hT[:, :, t * 128:(t + 1) * 128], r_h[t])
    nb = 0
    for m in (0, 4, 1, 5, 2, 6, 3, 7):
        for tb in range(4):
            b = nb % 2
            nb += 1
            for kc in range(8):
                sc.op("pe", lambda e, kc=kc: e.matmul(C.PS[:, b, :], lhsT=W1[:, kc, m * 128:(m + 1) * 128], rhs=hT[:, kc, tb * 512:(tb + 1) * 512],
                                                    start=(kc == 0), stop=(kc == 7)),
                      reads=r_W1 + r_h[tb * 4:(tb + 1) * 4], writes=[C.pr[b]], count=(kc == 7))
            if m >= 4:
                sc.op("act", lambda e: e.activation(out=uT[:, m - 4, tb * 512:(tb + 1) * 512], in_=C.PS[:, b, :], func=AF.Copy),
                      reads=[C.pr[b]], writes=[r_u[m - 4][tb]])
            else:
                sc.op("act", lambda e: e.activation(out=zb[:, tb * 512:(tb + 1) * 512], in_=C.PS[:, b, :], func=AF.Copy),
                      reads=[C.pr[b]], writes=[r_z])
        if m < 4:
            g = m
            w = POOL_WINDOWS[g]
            src, rsrc = zb, r_z
            bufs = [(pa, r_pa), (pb, r_pb)]
            k = 1
            step = 0
            while k < w:
                dst, rdst = bufs[step % 2]
                sc.op("pool", lambda e, dst=dst, src=src, k=k: e.tensor_tensor(out=dst[:, k:], in0=src[:, k:], in1=src[:, :S - k], op=ALU.add),
                      reads=[rsrc], writes=[rdst])
                sc.op("pool", lambda e, dst=dst, src=src, k=k: e.tensor_copy(out=dst[:, :k], in_=src[:, :k]),
                      reads=[rsrc], writes=[rdst])
                src, rsrc = dst, rdst
                k *= 2
                step += 1
            sc.op("dve", lambda e, src=src: e.scalar_tensor_tensor(out=pl[:, w - 1:], in0=src[:, w - 1:], scalar=1.0 / w, in1=zb[:, w - 1:],
                                                                  op0=ALU.mult, op1=ALU.subtract),
                  reads=[rsrc, r_z], writes=[r_pl])
            tmpw = C.stat[:, 32:32 + w - 1]
            sc.op("dve", lambda e, src=src: e.tensor_tensor(out=tmpw, in0=src[:, :w - 1], in1=C.inv16[:, :w - 1], op=ALU.mult),
                  reads=[rsrc, C.r_const], writes=[r_pl])
            sc.op("dve", lambda e: e.tensor_tensor(out=pl[:, :w - 1], in0=tmpw, in1=zb[:, :w - 1], op=ALU.subtract),
                  reads=[r_pl, r_z], writes=[r_pl])
            for tb in range(4):
                b = 2 + tb % 2
                sc.op("pe", lambda e: e.matmul(C.PS[:, b, :], lhsT=PW[:, g, :], rhs=pl[:, tb * 512:(tb + 1) * 512], start=True, stop=True),
                      reads=[r_PW, r_pl], writes=[C.pr[b]])
                sc.op("act", lambda e: e.activation(out=mixT[:, g, tb * 512:(tb + 1) * 512], in_=C.PS[:, b, :], func=AF.Copy,
                                                    scale=C.psc[:, i * 4 + g:i * 4 + g + 1]),
                      reads=[C.pr[b], C.r_const], writes=[r_mix[g][tb]])
    sc.barrier()
    if DBG_STOP == 1:
        return

    LIre = aview(C, 64 * KB, [128, S], F32)
    LIim = aview(C, 72 * KB, [128, S], F32)
    TFre = aview(C, 80 * KB, [128, 16, 128], F32)
    TFim = aview(C, 88 * KB, [128, 16, 128], F32)
    LBre = aview(C, 96 * KB, [128, 16, 128], BF16)
    LBim = aview(C, 100 * KB, [128, 16, 128], BF16)
    LCre = aview(C, 104 * KB, [128, 16, 128], BF16)
    LCimn = aview(C, 108 * KB, [128, 16, 128], BF16)
    LCren = aview(C, 20 * KB, [128, 16, 128], BF16)
    bc = aview(C, 48 * KB, [128, 4, 16, 16], F32)
    Bb = aview(C, 52 * KB, [128, 2, 16, 16], F32)
    E = aview(C, 116 * KB, [128, 16, 128], F32)
    TIre = aview(C, 124 * KB, [128, 16, 128], F32)
    TIim = aview(C, 132 * KB, [128, 16, 128], F32)
    sm = aview(C, 112 * KB, [128, 48, 16], F32)
    rp = Res("oprep")
    RP = [rp]
    sc.dma("sp", "dio", bc[:, 0], C.s5b[i, 0], writes=[rp])
    for k in range(1, 4):
        sc.dma("sp", "dio", bc[:, k], C.s5b[i, k], reads=[rp], writes=[rp])
    lr = C.s5lam[:, 0, i * 16:(i + 1) * 16]
    li = C.s5lam[:, 1, i * 16:(i + 1) * 16]
    ld = C.s5lam[:, 2, i * 16:(i + 1) * 16]
    V = lambda k: sm[:, k, :]
    dt_, a_, b_, ea, eai, s_, c_, cc, ss, lbr, lbi, ibr, ibi = [V(k) for k in range(13)]
    nr, ni, den, cfr, cfi, t0, t1_, pwr, pwi, qwr, qwi = [V(k) for k in range(13, 24)]
    L128r, L128i = V(24), V(25)
    tq = [V(26 + k) for k in range(4)]

    def dv(fn, eng="dve"):
        sc.op(eng, fn, reads=RP + [C.r_const], writes=RP)

    dv(lambda e: e.activation(out=dt_, in_=ld, func=AF.Exp), "act")
    dv(lambda e: e.tensor_tensor(out=a_, in0=lr, in1=dt_, op=ALU.mult))
    dv(lambda e: e.tensor_tensor(out=b_, in0=li, in1=dt_, op=ALU.mult))
    dv(lambda e: e.activation(out=ea, in_=a_, func=AF.Exp), "act")
    dv(lambda e: e.activation(out=eai, in_=a_, func=AF.Exp, scale=-1.0), "act")
    dv(lambda e: e.activation(out=s_, in_=b_, func=AF.Sin, scale=1.0 / 16), "act")
    dv(lambda e: e.activation(out=c_, in_=b_, func=AF.Sin, scale=1.0 / 16, bias=C.halfpi[:]), "act")
    for _ in range(4):
        dv(lambda e: e.tensor_tensor(out=cc, in0=c_, in1=c_, op=ALU.mult))
        dv(lambda e: e.tensor_tensor(out=ss, in0=s_, in1=s_, op=ALU.mult))
        dv(lambda e: e.scalar_tensor_tensor(out=s_, in0=c_, scalar=2.0, in1=s_, op0=ALU.mult, op1=ALU.mult))
        dv(lambda e: e.tensor_tensor(out=c_, in0=cc, in1=ss, op=ALU.subtract))
    dv(lambda e: e.tensor_tensor(out=lbr, in0=ea, in1=c_, op=ALU.mult))
    dv(lambda e: e.tensor_tensor(out=lbi, in0=ea, in1=s_, op=ALU.mult))
    dv(lambda e: e.tensor_tensor(out=ibr, in0=eai, in1=c_, op=ALU.mult))
    dv(lambda e: e.scalar_tensor_tensor(out=ibi, in0=eai, scalar=-1.0, in1=s_, op0=ALU.mult, op1=ALU.mult))
    dv(lambda e: e.tensor_scalar(out=t0, in0=lbr, scalar1=-1.0, scalar2=None, op0=ALU.add))
    dv(lambda e: e.tensor_tensor(out=nr, in0=t0, in1=lr, op=ALU.mult))
    dv(lambda e: e.tensor_tensor(out=t1_, in0=lbi, in1=li, op=ALU.mult))
    dv(lambda e: e.tensor_tensor(out=nr, in0=nr, in1=t1_, op=ALU.add))
    dv(lambda e: e.tensor_tensor(out=ni, in0=lbi, in1=lr, op=ALU.mult))
    dv(lambda e: e.tensor_tensor(out=t1_, in0=t0, in1=li, op=ALU.mult))
    dv(lambda e: e.tensor_tensor(out=ni, in0=ni, in1=t1_, op=ALU.subtract))
    dv(lambda e: e.tensor_tensor(out=den, in0=lr, in1=lr, op=ALU.mult))
    dv(lambda e: e.tensor_tensor(out=t1_, in0=li, in1=li, op=ALU.mult))
    dv(lambda e: e.tensor_tensor(out=den, in0=den, in1=t1_, op=ALU.add))
    dv(lambda e: e.reciprocal(out=den, in_=den))
    dv(lambda e: e.tensor_tensor(out=cfr, in0=nr, in1=den, op=ALU.mult))
    dv(lambda e: e.tensor_tensor(out=cfi, in0=ni, in1=den, op=ALU.mult))

    def build_table(Tre, Tim, br, bi, keep128=False):
        dv(lambda e: e.memset(Tre[:, :, 0:1], 1.0))
        dv(lambda e: e.memset(Tim[:, :, 0:1], 0.0))
        dv(lambda e: e.tensor_copy(out=Tre[:, :, 1], in_=br))
        dv(lambda e: e.tensor_copy(out=Tim[:, :, 1], in_=bi))
        dv(lambda e: e.tensor_copy(out=pwr, in_=br))
        dv(lambda e: e.tensor_copy(out=pwi, in_=bi))
        n = 1
        while n < 128:
            dv(lambda e: e.tensor_tensor(out=qwr, in0=pwr, in1=pwr, op=ALU.mult))
            dv(lambda e: e.tensor_tensor(out=qwi, in0=pwi, in1=pwi, op=ALU.mult))
            dv(lambda e: e.scalar_tensor_tensor(out=pwi, in0=pwr, scalar=2.0, in1=pwi, op0=ALU.mult, op1=ALU.mult))
            dv(lambda e: e.tensor_tensor(out=pwr, in0=qwr, in1=qwi, op=ALU.subtract))
            n *= 2
            if n == 128:
                break
            pr_b = pwr.unsqueeze(2).to_broadcast([128, 16, n])
            pi_b = pwi.unsqueeze(2).to_broadcast([128, 16, n])
            A_re, A_im = Tre[:, :, 0:n], Tim[:, :, 0:n]
            O_re, O_im = Tre[:, :, n:2 * n], Tim[:, :, n:2 * n]
            X1 = E[:, :, 0:n]
            X2 = E[:, :, 64:64 + n]
            dv(lambda e, n=n: e.tensor_tensor(out=X1, in0=A_re, in1=pr_b, op=ALU.mult))
            dv(lambda e, n=n: e.tensor_tensor(out=X2, in0=A_im, in1=pi_b, op=ALU.mult))
            dv(lambda e, n=n: e.tensor_tensor(out=O_re, in0=X1, in1=X2, op=ALU.subtract))
            dv(lambda e, n=n: e.tensor_tensor(out=X1, in0=A_re, in1=pi_b, op=ALU.mult))
            dv(lambda e, n=n: e.tensor_tensor(out=X2, in0=A_im, in1=pr_b, op=ALU.mult))
            dv(lambda e, n=n: e.tensor_tensor(out=O_im, in0=X1, in1=X2, op=ALU.add))
        if keep128:
            dv(lambda e: e.tensor_copy(out=L128r, in_=pwr))
            dv(lambda e: e.tensor_copy(out=L128i, in_=pwi))

    build_table(TFre, TFim, lbr, lbi, keep128=True)
    build_table(TIre, TIim, ibr, ibi)
    for (TI, LI) in ((TIre, LIre), (TIim, LIim)):
        for q4 in range(4):
            b = 4 + q4 % 2
            for q in range(4):
                gp = q4 * 4 + q
                sc.op("pe", lambda e, gp=gp, q=q, TI=TI: e.transpose(out=C.PS[:, b, q * 128:(q + 1) * 128], in_=TI[:, gp, :], identity=C.identf[:]),
                      reads=RP + [C.r_const], writes=[C.pr[b]], count=(q == 3))
            sc.op("act", lambda e, LI=LI: e.activation(out=LI[:, q4 * 512:(q4 + 1) * 512], in_=C.PS[:, b, :], func=AF.Copy),
                  reads=[C.pr[b]], writes=RP)
    cr_b = cfr.unsqueeze(2).to_broadcast([128, 16, 16])
    ci_b = cfi.unsqueeze(2).to_broadcast([128, 16, 16])
    Y1, Y2 = E[:, :, 0:16], E[:, :, 64:80]
    dv(lambda e: e.tensor_tensor(out=Y1, in0=bc[:, 0], in1=cr_b, op=ALU.mult))
    dv(lambda e: e.tensor_tensor(out=Y2, in0=bc[:, 1], in1=ci_b, op=ALU.mult))
    dv(lambda e: e.tensor_tensor(out=Bb[:, 0], in0=Y1, in1=Y2, op=ALU.subtract))
    dv(lambda e: e.tensor_tensor(out=Y1, in0=bc[:, 1], in1=cr_b, op=ALU.mult))
    dv(lambda e: e.tensor_tensor(out=Y2, in0=bc[:, 0], in1=ci_b, op=ALU.mult))
    dv(lambda e: e.tensor_tensor(out=Bb[:, 1], in0=Y1, in1=Y2, op=ALU.add))

    def diag(T, h):
        base = T[h * 64:(h + 1) * 64, 0, 0:1]
        ps = base.ap[0][0]
        return bass.AP(base.tensor, base.offset + h * 16, [[ps, 64], [512, 4], [160, 4], [1, 16]])

    def src4(T3, h):
        return T3[h * 64:(h + 1) * 64].rearrange("p (cu q) c -> p cu q c", q=4)

    for part, LB in ((0, LBre), (1, LBim)):
        dv(lambda e: e.memset(E, 0.0))
        for h in range(2):
            dv(lambda e, h=h, part=part: e.tensor_copy(out=diag(E, h), in_=src4(Bb[:, part], h)))
        for q4 in range(4):
            b = 4 + q4 % 2
            for q in range(4):
                gp = q4 * 4 + q
                sc.op("pe", lambda e, gp=gp, q=q: e.transpose(out=C.PS[:, b, q * 128:(q + 1) * 128], in_=E[:, gp, :], identity=C.identf[:]),
                      reads=RP + [C.r_const], writes=[C.pr[b]], count=(q == 3))
            sc.op("act", lambda e, LB=LB: e.activation(out=LB[:, q4 * 4:(q4 + 1) * 4, :], in_=C.PS[:, b, :].rearrange("p (q s) -> p q s", s=128), func=AF.Copy),
                  reads=[C.pr[b]], writes=RP)
    dv(lambda e: e.memset(LCre, 0.0))
    dv(lambda e: e.memset(LCimn, 0.0))
    dv(lambda e: e.memset(LCren, 0.0))
    for h in range(2):
        dv(lambda e, h=h: e.tensor_copy(out=diag(LCre, h), in_=src4(bc[:, 2], h)))
        dv(lambda e, h=h: e.tensor_scalar(out=diag(LCimn, h), in0=src4(bc[:, 3], h), scalar1=-1.0, scalar2=None, op0=ALU.mult))
        dv(lambda e, h=h: e.tensor_scalar(out=diag(LCren, h), in0=src4(bc[:, 2], h), scalar1=-1.0, scalar2=None, op0=ALU.mult))
    cXr = V(30)
    cXi = V(31)
    dv(lambda e: e.memset(cXr, 0.0))
    dv(lambda e: e.memset(cXi, 0.0))
    sc.barrier()
    if DBG_STOP == 2:
        return

    T = [aview(C, 116 * KB + 2048 * k, [128, 512], F32) for k in range(4)]
    btr2 = [aview(C, 124 * KB, [128, 512], BF16), aview(C, 18 * KB, [128, 512], BF16)]
    bti2 = [aview(C, 125 * KB, [128, 512], BF16), aview(C, 19 * KB, [128, 512], BF16)]
    wr_ = aview(C, 126 * KB, [128, 4, 128], F32)
    wi_ = aview(C, 128 * KB, [128, 4, 128], F32)
    Pq = [[aview(C, 130 * KB + 1024 * (2 * k + pp), [128, 4, 128], BF16) for k in range(4)] for pp in range(2)]
    r_Pq = [[Res(f"oPq{pp}_{k}") for k in range(4)] for pp in range(2)]
    xre = [aview(C, 138 * KB + 1024 * k, [128, 4, 128], BF16) for k in range(2)]
    xim = [aview(C, 16 * KB + 1024 * k, [128, 4, 128], BF16) for k in range(2)]
    yv = [aview(C, 115 * KB + 512 * k, [128, 128], F32) for k in range(2)]
    r_T, r_w, r_P, r_P3 = Res("oT"), Res("ow"), Res("oP"), Res("oP3")
    r_bt2 = [Res("obt0"), Res("obt1")]
    r_x = [Res("ox0"), Res("ox1")]
    r_yv = [Res("oyv0"), Res("oyv1")]
    r_cx = [Res(f"ocx{cu}") for cu in range(4)]
    r_yg = [[Res(f"oyg{cu}_{k}") for k in range(NT)] for cu in range(4)]
    units = [(k, cu) for k in range(NT) for cu in range(4)]

    def stage1(n):
        k, cu = units[n]
        par = n % 2
        bA, bB = 2 * par, 2 * par + 1
        ts = slice(k * 128, (k + 1) * 128)
        g4 = slice(4 * cu, 4 * cu + 4)
        ru = [r_u[cu][k // 4]]
        sc.op("pe", lambda e: e.matmul(C.PS[:, bA, :], lhsT=uT[:, cu, ts], rhs=LBre[:, g4, :], start=True, stop=True),
              reads=ru, writes=[C.pr[bA]])
        sc.op("pe", lambda e: e.matmul(C.PS[:, bB, :], lhsT=uT[:, cu, ts], rhs=LBim[:, g4, :], start=True, stop=True),
              reads=ru, writes=[C.pr[bB]])
        cs = slice(cu * 512, (cu + 1) * 512)
        sc.op("dve", lambda e: e.tensor_tensor(out=T[0], in0=C.PS[:, bA, :], in1=LIre[:, cs], op=ALU.mult), reads=[C.pr[bA]], writes=[r_T])
        sc.op("dve", lambda e: e.tensor_tensor(out=T[1], in0=C.PS[:, bB, :], in1=LIim[:, cs], op=ALU.mult), reads=[C.pr[bB]], writes=[r_T])
        sc.op("dve", lambda e: e.tensor_tensor(out=T[2], in0=C.PS[:, bB, :], in1=LIre[:, cs], op=ALU.mult), reads=[C.pr[bB]], writes=[r_T])
        sc.op("dve", lambda e: e.tensor_tensor(out=T[3], in0=C.PS[:, bA, :], in1=LIim[:, cs], op=ALU.mult), reads=[C.pr[bA]], writes=[r_T])
        btr, bti, r_bt = btr2[n % 2], bti2[n % 2], r_bt2[n % 2]
        sc.op("dve", lambda e: e.tensor_tensor(out=btr, in0=T[0], in1=T[1], op=ALU.subtract), reads=[r_T], writes=[r_bt])
        sc.op("pool", lambda e: e.tensor_tensor(out=bti, in0=T[2], in1=T[3], op=ALU.add), reads=[r_T], writes=[r_bt])

    def stage1b(n):
        btr, bti, r_bt = btr2[n % 2], bti2[n % 2], r_bt2[n % 2]
        for (bt, bk) in ((btr, 4), (bti, 5)):
            for q in range(4):
                sc.op("pe", lambda e, bt=bt, bk=bk, q=q: e.matmul(C.PS[:, bk, q * 128:(q + 1) * 128], lhsT=bt[:, q * 128:(q + 1) * 128], rhs=C.triT[:],
                                                                start=True, stop=True),
                      reads=[r_bt, C.r_const], writes=[C.pr[bk]], count=(q == 3))

    def stage2a(n):
        k, cu = units[n]
        g4 = slice(4 * cu, 4 * cu + 4)
        cxr_b = cXr[:, g4].unsqueeze(2).to_broadcast([128, 4, 128])
        cxi_b = cXi[:, g4].unsqueeze(2).to_broadcast([128, 4, 128])
        w4r = C.PS[:, 4, :].rearrange("p (q j) -> p q j", j=128)
        w4i = C.PS[:, 5, :].rearrange("p (q j) -> p q j", j=128)
        for q in range(4):
            sc.op("act", lambda e, q=q: e.activation(out=wr_[:, q, :], in_=w4r[:, q, :], func=AF.Identity, bias=cXr[:, 4 * cu + q:4 * cu + q + 1]),
                  reads=[C.pr[4], r_cx[cu]], writes=[r_w])
            sc.op("act", lambda e, q=q: e.activation(out=wi_[:, q, :], in_=w4i[:, q, :], func=AF.Identity, bias=cXi[:, 4 * cu + q:4 * cu + q + 1]),
                  reads=[C.pr[5], r_cx[cu]], writes=[r_w])
        w127r, w127i = wr_[:, :, 127], wi_[:, :, 127]
        sc.op("pool", lambda e: e.tensor_tensor(out=tq[0][:, 0:4], in0=w127r, in1=L128r[:, g4], op=ALU.mult), reads=[r_w], writes=[r_cx[cu]])
        sc.op("pool", lambda e: e.tensor_tensor(out=tq[1][:, 0:4], in0=w127i, in1=L128i[:, g4], op=ALU.mult), reads=[r_w], writes=[r_cx[cu]])
        sc.op("pool", lambda e: e.tensor_tensor(out=tq[2][:, 0:4], in0=w127i, in1=L128r[:, g4], op=ALU.mult), reads=[r_w], writes=[r_cx[cu]])
        sc.op("pool", lambda e: e.tensor_tensor(out=tq[3][:, 0:4], in0=w127r, in1=L128i[:, g4], op=ALU.mult), reads=[r_w], writes=[r_cx[cu]])
        sc.op("pool", lambda e: e.tensor_tensor(out=cXr[:, g4], in0=tq[0][:, 0:4], in1=tq[1][:, 0:4], op=ALU.subtract), reads=[r_cx[cu]], writes=[r_cx[cu]])
        sc.op("pool", lambda e: e.tensor_tensor(out=cXi[:, g4], in0=tq[2][:, 0:4], in1=tq[3][:, 0:4], op=ALU.add), reads=[r_cx[cu]], writes=[r_cx[cu]])
        par = n % 2
        P = Pq[par]
        rP = r_Pq[par]
        sc.op("pool", lambda e: e.tensor_tensor(out=P[0], in0=TFre[:, g4, :], in1=wr_, op=ALU.mult), reads=[r_w], writes=[rP[0]])
        sc.op("pool", lambda e: e.tensor_tensor(out=P[1], in0=TFim[:, g4, :], in1=wi_, op=ALU.mult), reads=[r_w], writes=[rP[1]])
        sc.op("dve", lambda e: e.tensor_tensor(out=P[2], in0=TFre[:, g4, :], in1=wi_, op=ALU.mult), reads=[r_w], writes=[rP[2]])
        sc.op("dve", lambda e: e.tensor_tensor(out=P[3], in0=TFim[:, g4, :], in1=wr_, op=ALU.mult), reads=[r_w], writes=[rP[3]])

    def stage2b(n):
        k, cu = units[n]
        par = n % 2
        ts = slice(k * 128, (k + 1) * 128)
        ru = [r_u[cu][k // 4]]
        by = 6 + par
        P = Pq[par]
        rP = r_Pq[par]
        terms = [(LCre, 0), (LCren, 1), (LCimn, 2), (LCimn, 3)]
        nmm = 0
        for q in range(4):
            for (LT, kk) in terms:
                nmm += 1
                sc.op("pe", lambda e, q=q, LT=LT, kk=kk, nmm=nmm: e.matmul(C.PS[:, by, 0:128], lhsT=LT[:, 4 * cu + q, :], rhs=P[kk][:, q, :], start=(nmm == 1), stop=(nmm == 16)),
                      reads=[rP[kk]], writes=[C.pr[by]], count=(nmm == 16))
        sc.op("dve", lambda e: e.scalar_tensor_tensor(out=yv[par], in0=uT[:, cu, ts], scalar=C.s5d[:, i * 4 + cu:i * 4 + cu + 1], in1=C.PS[:, by, 0:128],
                                                     op0=ALU.mult, op1=ALU.add),
              reads=[C.pr[by], C.r_const] + ru, writes=[r_yv[par]])

    def stage2c(n):
        k, cu = units[n]
        par = n % 2
        ts = slice(k * 128, (k + 1) * 128)
        sc.op("act", lambda e: e.activation(out=ygT[:, cu, ts], in_=yv[par], func=AF.Gelu_apprx_tanh),
              reads=[r_yv[par]], writes=[r_yg[cu][k]])

    NU = len(units)
    stage1(0)
    stage1b(0)
    if NU > 1:
        stage1(1)
    for n in range(NU):
        stage2a(n)
        if n >= 1:
            stage2c(n - 1)
        if n + 1 < NU:
            stage1b(n + 1)
        if n + 2 < NU:
            stage1(n + 2)
        stage2b(n)
    stage2c(NU - 1)
    sc.barrier()
    if DBG_STOP == 3:
        return

    WGL = aview(C, 64 * KB, [128, 4, 1024], BF16)
    WO = aview(C, 72 * KB, [128, 8, 1024], BF16)
    sgl = [aview(C, 88 * KB + 2048 * k, [128, 512], F32) for k in range(2)]
    r_WGL, r_WO = Res("oWGL"), [Res("oWOa"), Res("oWOb")]
    r_sgl = [Res("osgl0"), Res("osgl1")]
    sc.dma("pool", "dw", WGL, C.s5_w_glu[i].rearrange("(kc p) f -> p kc f", p=128), writes=[r_WGL])
    wov = C.od_w_out[i].rearrange("(kc p) f -> p kc f", p=128)
    sc.dma("pool", "dw", WO[:, 0:4, :], wov[:, 0:4, :], writes=[r_WO[0]])
    sc.dma("pool", "dw", WO[:, 4:8, :], wov[:, 4:8, :], writes=[r_WO[1]])
    n = 0
    for m in range(4):
        for tb in range(4):
            par = n % 2
            n += 1
            bA, bG = par, 2 + par
            for (bk, col) in ((bA, m), (bG, m + 4)):
                for kc in range(4):
                    sc.op("pe", lambda e, kc=kc, bk=bk, col=col: e.matmul(C.PS[:, bk, :], lhsT=WGL[:, kc, col * 128:(col + 1) * 128],
                                                                        rhs=ygT[:, kc, tb * 512:(tb + 1) * 512], start=(kc == 0), stop=(kc == 3)),
                          reads=[r_WGL], writes=[C.pr[bk]], count=(kc == 3))
            sc.op("act", lambda e: e.activation(out=sgl[par], in_=C.PS[:, bG, :], func=AF.Sigmoid), reads=[C.pr[bG]], writes=[r_sgl[par]])
            sc.op("dve", lambda e: e.tensor_tensor(out=mixT[:, 4 + m, tb * 512:(tb + 1) * 512], in0=sgl[par], in1=C.PS[:, bA, :], op=ALU.mult),
                  reads=[r_sgl[par], C.pr[bA]], writes=[r_mix[4 + m][tb]])
    if DBG_STOP == 4:
        return
    out_proj(C, mixT, [r for rr in r_mix for r in rr], WO, r_WO)


def out_proj(C, mixT, r_mix_all, WO, r_WO):
    sc = C.sc
    n = 0
    for t in range(NT):
        for db in range(2):
            b = 4 + n % 4
            n += 1
            for kc in range(8):
                sc.op("pe", lambda e, kc=kc: e.matmul(C.PS[:, b, :], lhsT=mixT[:, kc, t * 128:(t + 1) * 128], rhs=WO[:, kc, db * 512:(db + 1) * 512],
                                                    start=(kc == 0), stop=(kc == 7)),
                      reads=r_mix_all + r_WO, writes=[C.pr[b]], count=(kc == 7))
            xv = C.X[:, t, db * 512:(db + 1) * 512]
            sc.op("dve", lambda e, xv=xv: e.scalar_tensor_tensor(out=xv, in0=C.PS[:, b, :], scalar=1.0, in1=xv, op0=ALU.mult, op1=ALU.add),
                  reads=[C.pr[b], C.xr[t]], writes=[C.xr[t]])


NEG = -30000.0
KBOUND = 12.0


class Bump:
    def __init__(self, C, start, end=ARENA):
        self.C, self.off, self.end = C, start, end

    def __call__(self, shape, dtype):
        esz = 4 if dtype == F32 else 2
        n = 1
        for d in shape[1:]:
            n *= d
        nb = (n * esz + 63) // 64 * 64
        assert self.off + nb <= self.end, ("arena overflow", self.off, nb, self.end)
        v = aview(self.C, self.off, [shape[0], nb // esz], dtype)[:, 0:n]
        self.off += nb
        if len(shape) > 2:
            names = ["a", "b", "c", "d"][:len(shape) - 1]
            pat = "p (" + " ".join(names) + ") -> p " + " ".join(names)
            v = v.rearrange(pat, **{nm: d for nm, d in zip(names[1:], shape[2:])})
        return v


def psb(C, b):
    return C.PS[:, b, :].bitcast(BF16)


def rope_ops(C, src3, dst3, cos_b, sin_b, tmp, rd, r_tmp, r_dst, npart=128):
    sc = C.sc
    x1, x2 = src3[:, :, 0:32], src3[:, :, 32:64]
    sc.op("dve", lambda e: e.tensor_tensor(out=tmp[0], in0=x1, in1=cos_b, op=ALU.mult), reads=rd, writes=[r_tmp])
    sc.op("dve", lambda e: e.tensor_tensor(out=tmp[1], in0=x2, in1=sin_b, op=ALU.mult), reads=rd, writes=[r_tmp])
    sc.op("dve", lambda e: e.tensor_tensor(out=tmp[2], in0=x2, in1=cos_b, op=ALU.mult), reads=rd, writes=[r_tmp])
    sc.op("dve", lambda e: e.tensor_tensor(out=tmp[3], in0=x1, in1=sin_b, op=ALU.mult), reads=rd, writes=[r_tmp])
    sc.op("pool", lambda e: e.tensor_tensor(out=dst3[:, :, 0:32], in0=tmp[0], in1=tmp[1], op=ALU.subtract), reads=[r_tmp], writes=[r_dst])
    sc.op("pool", lambda e: e.tensor_tensor(out=dst3[:, :, 32:64], in0=tmp[2], in1=tmp[3], op=ALU.add), reads=[r_tmp], writes=[r_dst])


def even_mixer(C, l):
    sc = C.sc
    i = l // 2
    gcol = C.gT[:, (l * 3 + 1) * 8:(l * 3 + 2) * 8]
    al = Bump(C, 0)
    mixT = al([128, 8, S], BF16)
    qT = al([128, 4, S], BF16)
    ksT = al([128, S], BF16)
    kwT = al([128, S], BF16)
    Vs = al([128, NT, 2, 65], BF16)
    Vw = al([128, NT, 2, 65], BF16)
    gates = al([128, NT, 24], F32)
    Zall = al([128, NT, 136], BF16)
    kcmpT = al([128, 128], BF16)
    Vc = al([128, 2, 97], BF16)
    kcT = al([128, S], BF16)
    vcT = al([128, S], BF16)
    p_end = al.off
    r_mix = [Res(f"emix{t}") for t in range(NT)]
    r_q = [Res(f"eq{t}") for t in range(NT)]
    r_k = [Res(f"ek{t}") for t in range(NT)]
    r_v = [Res(f"ev{t}") for t in range(NT)]
    r_gate = [Res(f"egate{t}") for t in range(NT)]
    r_Z = [Res(f"eZ{t}") for t in range(NT)]
    r_cv = [Res(f"ecv{b}") for b in range(4)]
    r_init = Res("einit")

    al = Bump(C, p_end)
    hT = al([128, 8, 1024], BF16)
    gus = al([128, 8, 512], BF16)
    cosT = al([128, NT, 32], F32)
    sinT = al([128, NT, 32], F32)
    WsT = al([128, 8, 128], BF16)
    WsR = al([128, 8, 128], F32)
    bT = al([128, 8], F32)
    C.xs = [al([128, D], F32) for _ in range(2)]
    C.junk = al([128, D], BF16)
    gv = al([128, 8, 64], F32)
    cen = al([128, 8, 64], F32)
    sq = al([128, 8, 64], F32)
    vn = al([128, 512], BF16)
    oa = al([128, 512], BF16)
    rtmp = [al([128, 8, 32], F32) for _ in range(4)]
    rq = al([128, 8, 64], F32)
    qA = al([128, 512], BF16)
    kA = al([128, 256], BF16)
    st8 = al([128, 8, 8], F32)
    Wr = [aview(C, 16 * KB + 8192 * k, [128, 8, 512], BF16) for k in range(2)]
    r_Wr = [[Res(f"eWr{k}_{j}") for j in range(4)] for k in range(2)]
    r_h = [Res(f"eh{t}") for t in range(8)]
    r_gus = [Res(f"egus{t}") for t in range(8)]
    r_rope, r_ws = Res("erope"), Res("ews")
    r_gv, r_cen, r_sq, r_vn, r_oa, r_rt, r_rq, r_qA, r_kA, r_st = [Res("e" + n) for n in ("gv", "cen", "sq", "vn", "oa", "rt", "rq", "qA", "kA", "st")]
    sc.dma("sp", "dio", cosT, C.rope_in[0], writes=[r_rope])
    sc.dma("sp", "dio", sinT, C.rope_in[1], reads=[r_rope], writes=[r_rope])
    sc.dma("sp", "dio", WsR, C.gm_ws_in[i], writes=[r_ws])
    sc.dma("sp", "dio", bT, C.gm_b_in[i], reads=[r_ws], writes=[r_ws])
    sc.op("dve", lambda e: e.tensor_tensor(out=WsT, in0=WsR, in1=C.triT[:].unsqueeze(1).to_broadcast([128, 8, 128]), op=ALU.mult),
          reads=[r_ws, C.r_const], writes=[r_ws])
    sc.op("dve", lambda e: e.memset(Zall, 0.0), writes=[r_init])
    sc.op("dve", lambda e: e.memset(Vs[:, :, :, 64:65], 1.0), reads=[r_init], writes=[r_init])
    sc.op("dve", lambda e: e.memset(Vw[:, :, :, 64:65], 1.0), reads=[r_init], writes=[r_init])
    w_in = C.ev_w_in[i].rearrange("(kc p) f -> p kc f", p=128)
    wctr = [0]

    def load_group(cols):
        k = wctr[0] % 2
        wctr[0] += 1
        o = 0
        rs = []
        for j, (c0, w) in enumerate(cols):
            sc.dma("pool", "dw", Wr[k][:, :, o:o + w], w_in[:, :, c0:c0 + w], writes=[r_Wr[k][j]])
            rs.append(r_Wr[k][j])
            o += w
        return Wr[k], rs

    pctr = [0]

    def proj_tm(W, rW, tt, ncols, col0=0):
        b = pctr[0] % 2
        pctr[0] += 1
        for kc in range(8):
            sc.op("pe", lambda e, kc=kc: e.matmul(C.PS[:, b, 0:ncols], lhsT=hT[:, kc, tt * 128:(tt + 1) * 128], rhs=W[:, kc, col0:col0 + ncols],
                                                start=(kc == 0), stop=(kc == 7)),
                  reads=rW + [r_h[tt]], writes=[C.pr[b]], count=(kc == 7))
        return b

    sq2 = WsR[:, 0:4, :].rearrange("p a (b d) -> p (a b) d", d=64)
    r_sq2 = Res("esq2")

    def chain_v(hf, tt, W, rW):
        t = hf * 8 + tt
        b = proj_tm(W, rW, tt, 512)
        sc.op("act", lambda e: e.activation(out=gv, in_=C.PS[:, b, :].rearrange("p (h d) -> p h d", d=64), func=AF.Gelu_apprx_tanh),
              reads=[C.pr[b]], writes=[r_gv])
        yield
        s1, mean, s2, rstd = st8[:, 0, :], st8[:, 1, :], st8[:, 2, :], st8[:, 3, :]
        sc.op("dve", lambda e: e.tensor_reduce(out=s1, in_=gv, axis=AX.X, op=ALU.add), reads=[r_gv], writes=[r_st])
        sc.op("dve", lambda e: e.tensor_scalar(out=mean, in0=s1, scalar1=1.0 / 64, scalar2=None, op0=ALU.mult), reads=[r_st], writes=[r_st])
        yield
        sc.op("dve", lambda e: e.tensor_tensor(out=cen, in0=gv, in1=mean.unsqueeze(2).to_broadcast([128, 8, 64]), op=ALU.subtract),
              reads=[r_gv, r_st], writes=[r_cen])
        sc.op("pool", lambda e: e.tensor_tensor(out=sq, in0=cen, in1=cen, op=ALU.mult), reads=[r_cen], writes=[r_sq])
        yield
        sc.op("dve", lambda e: e.tensor_reduce(out=s2, in_=sq, axis=AX.X, op=ALU.add), reads=[r_sq], writes=[r_st])
        sc.op("dve", lambda e: e.tensor_scalar(out=s2, in0=s2, scalar1=1.0 / 64, scalar2=1e-5, op0=ALU.mult, op1=ALU.add), reads=[r_st], writes=[r_st])
        sc.op("pool", lambda e: e.tensor_tensor(out=rstd, in0=s2, in1=C.cm05[:].to_broadcast([128, 8]), op=ALU.pow), reads=[r_st, C.r_const], writes=[r_st])
        yield
        sc.op("dve", lambda e: e.tensor_tensor(out=vn.rearrange("p (h d) -> p h d", d=64), in0=cen, in1=rstd.unsqueeze(2).to_broadcast([128, 8, 64]), op=ALU.mult),
              reads=[r_cen, r_st], writes=[r_vn])
        for h in range(8):
            sc.op("pe", lambda e, h=h: e.matmul(C.PS[:, 2, h * 64:(h + 1) * 64], lhsT=WsT[:, h, :], rhs=vn[:, h * 64:(h + 1) * 64], start=True, stop=True),
                  reads=[r_ws, r_vn], writes=[C.pr[2]], count=(h == 7))
        yield
        sc.op("dve", lambda e: e.tensor_tensor(out=cen, in0=C.PS[:, 2, :].rearrange("p (h d) -> p h d", d=64), in1=bT.unsqueeze(2).to_broadcast([128, 8, 64]), op=ALU.add),
              reads=[C.pr[2], r_ws], writes=[r_cen])
        sc.op("dve", lambda e: e.tensor_tensor(out=oa, in0=cen.rearrange("p h d -> p (h d)"), in1=gus[:, tt, :], op=ALU.mult),
              reads=[r_cen, r_gus[tt]], writes=[r_oa])
        yield
        for c in range(4):
            sc.op("pe", lambda e, c=c: e.transpose(out=psb(C, 3)[:, c * 128:(c + 1) * 128], in_=oa[:, c * 128:(c + 1) * 128], identity=C.identb[:]),
                  reads=[r_oa, C.r_const], writes=[C.pr[3]], count=(c == 3))
        sc.op("act", lambda e: e.activation(out=mixT[:, 0:4, t * 128:(t + 1) * 128], in_=psb(C, 3)[:, 0:512].rearrange("p (c t) -> p c t", t=128), func=AF.Copy),
              reads=[C.pr[3]], writes=[r_mix[t]])
        yield

    def chain_q(hf, tt, W, rW):
        t = hf * 8 + tt
        b = proj_tm(W, rW, tt, 512)
        cb_ = cosT[:, t, :].unsqueeze(1).to_broadcast([128, 8, 32])
        sb_ = sinT[:, t, :].unsqueeze(1).to_broadcast([128, 8, 32])
        rope_ops(C, C.PS[:, b, :].rearrange("p (h d) -> p h d", d=64), rq, cb_, sb_, rtmp, [C.pr[b], r_rope], r_rt, r_rq)
        yield
        sc.op("pool", lambda e: e.tensor_tensor(out=sq2, in0=rq, in1=rq, op=ALU.mult), reads=[r_rq, r_ws], writes=[r_sq2])
        ssq, nm = st8[:, 4, :], st8[:, 5, :]
        sc.op("dve", lambda e: e.tensor_reduce(out=ssq, in_=sq2, axis=AX.X, op=ALU.add), reads=[r_sq2], writes=[r_st2])
        yield
        sc.op("pool", lambda e: e.tensor_tensor(out=nm, in0=ssq, in1=C.c05[:].to_broadcast([128, 8]), op=ALU.pow), reads=[r_st2, C.r_const], writes=[r_st2])
        zb_ = Zall[:, t, 32:36]
        zout = bass.AP(zb_.tensor, zb_.offset, [[zb_.ap[0][0], 128], [100, 2], [1, 4]])
        sc.op("dve", lambda e, zout=zout: e.tensor_scalar(out=zout, in0=nm.rearrange("p (g h) -> p g h", h=4), scalar1=-0.125 * KBOUND, scalar2=None, op0=ALU.mult),
              reads=[r_st2, r_init], writes=[r_Z[t]])
        sc.op("act", lambda e: e.activation(out=qA.rearrange("p (h g d) -> p g h d", h=4, g=2), in_=rq.rearrange("p (g h) d -> p g h d", g=2), func=AF.Copy, scale=0.125),
              reads=[r_rq], writes=[r_qA])
        yield
        for h in range(4):
            sc.op("pe", lambda e, h=h: e.transpose(out=psb(C, 3)[:, h * 128:(h + 1) * 128], in_=qA[:, h * 128:(h + 1) * 128], identity=C.identb[:]),
                  reads=[r_qA, C.r_const], writes=[C.pr[3]], count=(h == 3))
        sc.op("act", lambda e: e.activation(out=qT[:, :, t * 128:(t + 1) * 128], in_=psb(C, 3)[:, 0:512].rearrange("p (h t) -> p h t", t=128), func=AF.Copy),
              reads=[C.pr[3]], writes=[r_q[t]])
        yield

    def chain_k(hf, tt, W, rW):
        t = hf * 8 + tt
        b = proj_tm(W, rW, tt, 512)
        cb_ = cosT[:, t, :].unsqueeze(1).to_broadcast([128, 4, 32])
        sb_ = sinT[:, t, :].unsqueeze(1).to_broadcast([128, 4, 32])
        sc.op("act", lambda e: e.activation(out=Vs[:, t, :, 0:64], in_=C.PS[:, b, 256:384].rearrange("p (g d) -> p g d", d=64), func=AF.Copy),
              reads=[C.pr[b], r_init], writes=[r_v[t]])
        sc.op("act", lambda e: e.activation(out=Vw[:, t, :, 0:64], in_=C.PS[:, b, 384:512].rearrange("p (g d) -> p g d", d=64), func=AF.Copy),
              reads=[C.pr[b], r_init], writes=[r_v[t]])
        rope_ops(C, C.PS[:, b, 0:256].rearrange("p (h d) -> p h d", d=64), rq[:, 0:4, :], cb_, sb_, [x[:, 0:4, :] for x in rtmp], [C.pr[b], r_rope], r_rt, r_rq)
        yield
        sc.op("act", lambda e: e.activation(out=kA, in_=rq[:, 0:4, :].rearrange("p h d -> p (h d)"), func=AF.Copy), reads=[r_rq], writes=[r_kA])
        yield
        for c in range(2):
            sc.op("pe", lambda e, c=c: e.transpose(out=psb(C, 3)[:, c * 128:(c + 1) * 128], in_=kA[:, c * 128:(c + 1) * 128], identity=C.identb[:]),
                  reads=[r_kA, C.r_const], writes=[C.pr[3]], count=(c == 1))
        sc.op("act", lambda e: e.activation(out=ksT[:, t * 128:(t + 1) * 128], in_=psb(C, 3)[:, 0:128], func=AF.Copy), reads=[C.pr[3]], writes=[r_k[t]])
        sc.op("act", lambda e: e.activation(out=kwT[:, t * 128:(t + 1) * 128], in_=psb(C, 3)[:, 128:256], func=AF.Copy), reads=[C.pr[3]], writes=[r_k[t]])
        yield

    def chain_c(hf, W, rW):
        for tb in range(2):
            for cc, dstT in ((0, kcT), (1, vcT)):
                b = pctr[0] % 2
                pctr[0] += 1
                for kc in range(8):
                    sc.op("pe", lambda e, kc=kc: e.matmul(C.PS[:, b, :], lhsT=W[:, kc, cc * 128:(cc + 1) * 128], rhs=hT[:, kc, tb * 512:(tb + 1) * 512],
                                                        start=(kc == 0), stop=(kc == 7)),
                          reads=rW + r_h[tb * 4:(tb + 1) * 4], writes=[C.pr[b]], count=(kc == 7))
                tok0 = hf * 1024 + tb * 512
                sc.op("act", lambda e: e.activation(out=dstT[:, tok0:tok0 + 512], in_=C.PS[:, b, :], func=AF.Copy), reads=[C.pr[b]], writes=[r_cv[hf * 2 + tb]])
                yield
        for tt in range(8):
            t = hf * 8 + tt
            b = proj_tm(W, rW, tt, 32, col0=256)
            sc.op("act", lambda e: e.activation(out=gates[:, t, :], in_=C.PS[:, b, 8:32], func=AF.Sigmoid), reads=[C.pr[b]], writes=[r_gate[t]])
            yield

    def seq(gens):
        for g_ in gens:
            yield from g_

    def ilv(gens):
        gens = list(gens)
        while gens:
            for g_ in list(gens):
                try:
                    next(g_)
                except StopIteration:
                    gens.remove(g_)

    r_st2 = Res("est2")
    for hf in range(2):
        for tt in range(8):
            norm_transpose(C, hf * 8 + tt, gcol, hT[:, :, tt * 128:(tt + 1) * 128], r_h[tt])
        W, rW = load_group([(0, 512)])
        for tt in range(8):
            b = proj_tm(W, rW, tt, 512)
            sc.op("act", lambda e: e.activation(out=gus[:, tt, :], in_=C.PS[:, b, :], func=AF.Gelu_apprx_tanh), reads=[C.pr[b]], writes=[r_gus[tt]])
        Wv, rWv = load_group([(512, 512)])
        Wq, rWq = load_group([(1024, 512)])
        ilv([seq(chain_v(hf, tt, Wv, rWv) for tt in range(8)), seq(chain_q(hf, tt, Wq, rWq) for tt in range(8))])
        Wk, rWk = load_group([(1792, 128), (2048, 128), (1920, 128), (2176, 128)])
        Wc, rWc = load_group([(1536, 256), (2296, 32)])
        ilv([seq(chain_k(hf, tt, Wk, rWk) for tt in range(8)), chain_c(hf, Wc, rWc)])
    sc.barrier()
    if DBG_STOP == 1:
        return
    even_attention(C, i, p_end, mixT, qT, ksT, kwT, Vs, Vw, gates, Zall, kcmpT, Vc, kcT, vcT, r_mix)


def even_attention(C, i, p_end, mixT, qT, ksT, kwT, Vs, Vw, gates, Zall, kcmpT, Vc, kcT, vcT, r_mix):
    sc = C.sc
    al = Bump(C, p_end)
    w1 = al([128, 2, 32, 64], BF16)
    w2 = al([64, 2, 64], BF16)
    peT = al([128, 2, 32], BF16)
    cbias = al([64, 2], F32)
    gh = al([64, 4, 128], BF16)
    cosc = al([128, 32], F32)
    sinc = al([128, 32], F32)
    rkc = al([128, 2, 64], F32)
    ctmp = [al([128, 2, 32], F32) for _ in range(4)]
    kcA = al([128, 128], BF16)
    Fpos = al([128, NT, 32], F32)
    Fneg = al([128, NT, 32], F32)
    EB = al([100, S], BF16)
    triB = al([128, 512], BF16)
    bandB = al([128, 512], BF16)
    BM = al([100, 512], BF16)
    Mpat = al([73, 512], BF16)
    Bwin = al([73, 256], BF16)
    NP = 8
    Pt = [al([128, 512], BF16) for _ in range(NP)]
    SB1 = [al([100, 512], BF16) for _ in range(2)]
    SB2 = [al([100, 512], BF16) for _ in range(2)]
    oacc = al([128, 4, 64], F32)
    otmp = al([128, 4, 64], F32)
    ob = al([128, 512], BF16)
    sm4 = al([128, 16, 4], F32)
    impn = al([128, 4, 32], F32)
    vsel = al([128, 32], F32)
    top8 = al([128, 8], F32)
    WO = al([128, 8, 1024], BF16)
    rc = Res("ecmp")
    r_cst = [Res(f"ecst{k}") for k in range(9)]
    r_WO = [Res("eWOa"), Res("eWOb")]
    sc.dma("sp", "dio", Fpos, C.fsel_in[0], writes=[r_cst[0]])
    sc.dma("sp", "dio", Fneg, C.fsel_in[1], writes=[r_cst[1]])
    sc.dma("pool", "dio", EB, C.eb_in, writes=[r_cst[2]])
    sc.dma("pool", "dio", triB, C.trib_in, writes=[r_cst[3]])
    sc.dma("pool", "dio", bandB, C.bandb_in, writes=[r_cst[4]])
    sc.dma("pool", "dio", BM, C.bm_in, writes=[r_cst[5]])
    sc.dma("pool", "dio", Mpat, C.mpat_in, writes=[r_cst[6]])
    sc.dma("pool", "dio", Bwin, C.bwin_in, writes=[r_cst[7]])
    sc.dma("sp", "dio", cosc, C.ropec_in[0], writes=[r_cst[8]])
    sc.dma("sp", "dio", sinc, C.ropec_in[1], reads=[r_cst[8]], writes=[r_cst[8]])
    wov = C.ev_w_out[i].rearrange("(kc p) f -> p kc f", p=128)
    sc.dma("pool", "dw", WO[:, 0:4, :], wov[:, 0:4, :], writes=[r_WO[0]])
    sc.dma("pool", "dw", WO[:, 4:8, :], wov[:, 4:8, :], writes=[r_WO[1]])
    r_w1 = [Res(f"ew1_{k}") for k in range(4)]
    for kv in range(2):
        src = C.nsa_w1[i, kv].rearrange("(l d) e -> d l e", d=64)
        for hh in range(2):
            sc.dma("pool", "dw", w1[hh * 64:(hh + 1) * 64, kv], src, writes=[r_w1[kv * 2 + hh]])
    r_w2 = Res("ew2")
    sc.dma("pool", "dw", w2, C.nsa_w2[i].rearrange("k e f -> e k f"), writes=[r_w2])
    r_pe = Res("epe")
    sc.dma("pool", "dw", peT, C.nsa_pe_in[i], writes=[r_pe])
    sc.dma("pool", "dio", Vc[0:127, 0, 65:97], C.ovl_in, writes=[rc])
    sc.dma("pool", "dio", Vc[0:127, 1, 65:97], C.ovl_in, reads=[rc], writes=[rc])
    sc.op("dve", lambda e: e.memset(Vc[:, :, 64:65], 1.0), reads=[rc], writes=[rc])
    sc.op("dve", lambda e: e.memset(kcA, 0.0), writes=[rc])

    RC = [rc]
    if DBG_STOP == 21:
        sc.barrier()
        return
    for kv in range(2):
        for l_ in range(32):
            sc.op("pe", lambda e, kv=kv, l_=l_: e.matmul(C.PS[0:64, 7, kv:kv + 1], lhsT=w1[0:64, kv, l_, :], rhs=peT[0:64, kv, l_:l_ + 1], start=(l_ == 0), stop=(l_ == 31)),
                  reads=r_w1 + [r_pe], writes=[C.pr[7]], count=(l_ == 31))
    sc.op("act", lambda e: e.activation(out=cbias, in_=C.PS[0:64, 7, 0:2], func=AF.Copy), reads=[C.pr[7]], writes=RC)
    if DBG_STOP == 22:
        sc.barrier()
        return
    for kv in range(2):
        srcT = kcT if kv == 0 else vcT
        s3 = srcT.rearrange("p (n s) -> p n s", s=16)
        for g in range(2):
            slot = kv * 2 + g
            for l_ in range(32):
                rhs = s3[64 * g:64 * g + 64, 0:127, l_] if l_ < 16 else s3[64 * g:64 * g + 64, 1:128, l_ - 16]
                sc.op("pe", lambda e, l_=l_, rhs=rhs, slot=slot, kv=kv, g=g: e.matmul(C.PS[0:64, 6 - g, slot * 128:slot * 128 + 127], lhsT=w1[64 * g:64 * g + 64, kv, l_, :], rhs=rhs,
                                                                                 start=(l_ == 0), stop=(l_ == 31)),
                      reads=r_w1, writes=[C.pr[6 - g]], count=(l_ == 31))
        for g in range(2):
            slot = kv * 2 + g
            sc.op("act", lambda e, slot=slot, kv=kv, g=g: e.activation(out=gh[:, slot, 0:127], in_=C.PS[0:64, 6 - g, slot * 128:slot * 128 + 127], func=AF.Gelu_apprx_tanh,
                                                                 bias=cbias[:, kv:kv + 1]),
                  reads=[C.pr[6 - g]] + RC, writes=RC)
    if DBG_STOP == 23:
        sc.barrier()
        return
    for kv in range(2):
        for g in range(2):
            sc.op("pe", lambda e, kv=kv, g=g: e.matmul(C.PS[0:127, 7, 128 + (kv * 2 + g) * 64:128 + (kv * 2 + g) * 64 + 64], lhsT=gh[:, kv * 2 + g, 0:127], rhs=w2[:, kv, :],
                                                     start=True, stop=True),
                  reads=RC + [r_w2], writes=[C.pr[7]], count=True)
    kc_ps = C.PS[0:127, 7, 128:256].rearrange("p (g d) -> p g d", d=64)
    vc_ps = C.PS[0:127, 7, 256:384].rearrange("p (g d) -> p g d", d=64)
    sc.op("act", lambda e: e.activation(out=Vc[0:127, :, 0:64], in_=vc_ps, func=AF.Copy), reads=[C.pr[7]] + RC, writes=RC)
    if DBG_STOP == 24:
        sc.barrier()
        return
    cb_ = cosc[0:127, :].unsqueeze(1).to_broadcast([127, 2, 32])
    sb_ = sinc[0:127, :].unsqueeze(1).to_broadcast([127, 2, 32])
    r_ct = Res("ectmp")
    rope_ops(C, kc_ps, rkc[0:127], cb_, sb_, [x[0:127] for x in ctmp], [C.pr[7], r_cst[8]], r_ct, rc)
    if DBG_STOP == 25:
        sc.barrier()
        return
    sc.op("act", lambda e: e.activation(out=kcA[0:127, :], in_=rkc[0:127].rearrange("p g d -> p (g d)"), func=AF.Copy), reads=RC, writes=RC)
    if DBG_STOP == 26:
        sc.barrier()
        return
    sc.op("pe", lambda e: e.transpose(out=psb(C, 6)[:, 0:127], in_=kcA[0:127, :], identity=C.identb[0:127, 0:127]), reads=RC + [C.r_const], writes=[C.pr[6]])
    sc.op("act", lambda e: e.activation(out=kcmpT[:, 0:127], in_=psb(C, 6)[:, 0:127], func=AF.Copy), reads=[C.pr[6]], writes=RC)
    sc.barrier()
    if DBG_STOP == 2:
        return

    r_P = [Res(f"eP{k}") for k in range(NP)]
    r_SB1 = [Res("eSB1a"), Res("eSB1b")]
    r_SB2 = [Res("eSB2a"), Res("eSB2b")]
    r_o = [Res("eo0"), Res("eo1")]
    r_ot = [Res("eot0"), Res("eot1")]
    r_ob = [Res("eob0"), Res("eob1")]
    r_sm = [Res("esm0"), Res("esm1")]
    r_imp = [Res("eimp0"), Res("eimp1")]
    r_vs = [Res("evs0"), Res("evs1")]
    oacc2 = [oacc, al([128, 4, 64], F32)]
    otmp2 = [otmp, al([128, 4, 64], F32)]
    sm42 = [sm4, al([128, 16, 4], F32)]
    impn2 = [impn, al([128, 4, 32], F32)]
    vsel2 = [vsel, al([128, 32], F32)]
    top82 = [top8, al([128, 8], F32)]
    pc = [0]
    BANK_SC = (0, 1)
    B_C, B_T = 2, 5
    B_Sg, B_Wg = (3, 6), (4, 7)
    scn = [0]

    def score_tile(main_lhsT, main_rhs, extra, npart, rds):
        b = BANK_SC[scn[0] % 2]
        scn[0] += 1
        mm = [(main_lhsT, main_rhs)] + extra
        for k_, (lt, rh) in enumerate(mm):
            sc.op("pe", lambda e, lt=lt, rh=rh, k_=k_: e.matmul(C.PS[0:npart, b, :], lhsT=lt, rhs=rh, start=(k_ == 0), stop=(k_ == len(mm) - 1)),
                  reads=rds, writes=[C.pr[b]], count=(k_ == len(mm) - 1))
        k = pc[0] % NP
        pc[0] += 1
        sc.op("act", lambda e: e.activation(out=Pt[k][0:npart, :], in_=C.PS[0:npart, b, :], func=AF.Exp), reads=[C.pr[b]], writes=[r_P[k]])
        return Pt[k], r_P[k]

    def chain(t, g):
        tsl = slice(t * 128, (t + 1) * 128)
        gp = slice(64 * g, 64 * g + 64)
        rhs_q = qT[gp, :, tsl]
        s1 = g
        B_S, B_W = B_Sg[g], B_Wg[g]
        oacc_, otmp_, sm4_, impn_, vsel_, top8_ = oacc2[g], otmp2[g], sm42[g], impn2[g], vsel2[g], top82[g]
        zc = slice(g * 256, g * 256 + 128)
        zc2 = slice(g * 256 + 128, g * 256 + 256)
        r0 = 64 * g
        zin = Zall[:, t, 0:36] if g == 0 else Zall[:, t, 36:136]
        zrows = 36 if g == 0 else 100
        RS = slice(r0, r0 + 36)
        sc.op("pe", lambda e: e.transpose(out=psb(C, B_T)[0:zrows, zc], in_=zin, identity=C.identb[:]), reads=[C.r_const], writes=[C.pr[B_T]])
        sc.op("dve", lambda e: e.tensor_tensor(out=SB1[s1][RS, :].rearrange("p (h t) -> p h t", t=128), in0=psb(C, B_T)[RS, zc].unsqueeze(1).to_broadcast([36, 4, 128]),
                                              in1=BM[RS, :].rearrange("p (h t) -> p h t", t=128), op=ALU.mult),
              reads=[C.pr[B_T], r_cst[5]], writes=[r_SB1[s1]])
        yield
        bw0 = 127 - 8 * t
        Pc, rPc = score_tile(kcmpT[gp, 0:127], rhs_q, [(EB[RS, 0:127], SB1[s1][RS, :]), (Bwin[r0:r0 + 9, bw0:bw0 + 127], Mpat[r0:r0 + 9, :])], 127,
                             [r_SB1[s1], r_cst[2], r_cst[6], r_cst[7]])
        yield
        for h in range(4):
            sc.op("pe", lambda e, h=h: e.matmul(C.PS[:, B_C, h * 97:(h + 1) * 97], lhsT=Pc[0:127, h * 128:(h + 1) * 128], rhs=Vc[0:127, g, :], start=True, stop=True),
                  reads=[rPc], writes=[C.pr[B_C]], count=(h == 3))
        accc = C.PS[:, B_C, 0:388].rearrange("p (h c) -> p h c", c=97)
        gt = gates[:, t, 12 * g:12 * g + 12].rearrange("p (h b) -> p h b", b=3)
        dc, fc = sm4_[:, 0, :], sm4_[:, 1, :]
        sc.op("dve", lambda e: e.tensor_scalar(out=dc, in0=accc[:, :, 64], scalar1=1e-30, scalar2=None, op0=ALU.add), reads=[C.pr[B_C]], writes=[r_sm[g]])
        sc.op("dve", lambda e: e.reciprocal(out=dc, in_=dc), reads=[r_sm[g]], writes=[r_sm[g]])
        sc.op("dve", lambda e: e.tensor_tensor(out=fc, in0=dc, in1=gt[:, :, 0], op=ALU.mult), reads=[r_sm[g]], writes=[r_sm[g]])
        sc.op("dve", lambda e: e.tensor_tensor(out=oacc_, in0=accc[:, :, 0:64], in1=fc.unsqueeze(2).to_broadcast([128, 4, 64]), op=ALU.mult),
              reads=[C.pr[B_C], r_sm[g]], writes=[r_o[g]])
        sc.op("dve", lambda e: e.tensor_tensor(out=impn_, in0=accc[:, :, 65:97], in1=dc.unsqueeze(2).to_broadcast([128, 4, 32]), op=ALU.mult),
              reads=[C.pr[B_C], r_sm[g]], writes=[r_imp[g]])
        yield
        sc.op("dve", lambda e: e.tensor_reduce(out=vsel_, in_=impn_.rearrange("p h j -> p j h"), axis=AX.X, op=ALU.add), reads=[r_imp[g]], writes=[r_vs[g]])
        sc.op("dve", lambda e: e.tensor_tensor(out=vsel_, in0=vsel_, in1=Fpos[:, t, :], op=ALU.max), reads=[r_vs[g], r_cst[0]], writes=[r_vs[g]])
        sc.op("dve", lambda e: e.tensor_tensor(out=vsel_, in0=vsel_, in1=Fneg[:, t, :], op=ALU.add), reads=[r_vs[g], r_cst[1]], writes=[r_vs[g]])
        sc.op("dve", lambda e: e.max(out=top8_, in_=vsel_), reads=[r_vs[g]], writes=[r_imp[g]])
        sc.op("dve", lambda e: e.tensor_scalar(out=vsel_, in0=vsel_, scalar1=top8_[:, 7:8], scalar2=1.0, op0=ALU.is_ge, op1=ALU.subtract), reads=[r_vs[g], r_imp[g]], writes=[r_vs[g]])
        zsel = Zall[:, t, 0:32] if g == 0 else Zall[:, t, 100:132]
        sc.op("dve", lambda e: e.tensor_scalar(out=zsel, in0=vsel_, scalar1=-NEG, scalar2=None, op0=ALU.mult), reads=[r_vs[g]], writes=[r_SB2[s1]])
        yield
        sc.op("pe", lambda e: e.transpose(out=psb(C, B_T)[0:zrows, zc2], in_=zin, identity=C.identb[:]), reads=[r_SB2[s1], C.r_const], writes=[C.pr[B_T]])
        sc.op("dve", lambda e: e.tensor_tensor(out=SB2[s1][RS, :].rearrange("p (h t) -> p h t", t=128), in0=psb(C, B_T)[RS, zc2].unsqueeze(1).to_broadcast([36, 4, 128]),
                                              in1=BM[RS, :].rearrange("p (h t) -> p h t", t=128), op=ALU.mult),
              reads=[C.pr[B_T], r_cst[5]], writes=[r_SB2[s1]])
        yield
        jobs = []
        kts = list(range(max(0, t - 4), t + 1))
        for n_, kt in enumerate(kts):
            ksl = slice(kt * 128, (kt + 1) * 128)
            extra = [(EB[RS, ksl], SB1[s1][RS, :])]
            if kt == t:
                extra.append((C.identb[:], triB[:, :]))
            elif kt == t - 4:
                extra.append((C.identb[:], bandB[:, :]))
            jobs.append((kwT[gp, ksl], extra, [r_SB1[s1], r_cst[2], r_cst[3], r_cst[4], C.r_const], B_W, Vw, kt, n_ == 0, n_ == len(kts) - 1))
        for kt in range(t + 1):
            ksl = slice(kt * 128, (kt + 1) * 128)
            extra = [(EB[RS, ksl], SB2[s1][RS, :])]
            if kt == t:
                extra.append((C.identb[:], triB[:, :]))
            jobs.append((ksT[gp, ksl], extra, [r_SB2[s1], r_cst[2], r_cst[3], C.r_const], B_S, Vs, kt, kt == 0, kt == t))
        prev = None
        for n_ in range(len(jobs) + 1):
            cur = None
            if n_ < len(jobs):
                lt, extra, rds, bk, Vt, kt, first, last = jobs[n_]
                P_, rP_ = score_tile(lt, rhs_q, extra, 128, rds)
                cur = (P_, rP_, bk, Vt, kt, first, last)
            if prev is not None:
                P_, rP_, bk, Vt, kt, first, last = prev
                for h in range(4):
                    sc.op("pe", lambda e, h=h, P_=P_, bk=bk, Vt=Vt, kt=kt, first=first, last=last: e.matmul(
                        C.PS[:, bk, h * 65:(h + 1) * 65], lhsT=P_[:, h * 128:(h + 1) * 128], rhs=Vt[:, kt, g, :], start=(first and h == 0), stop=(last and h == 3)),
                          reads=[rP_], writes=[C.pr[bk]], count=(h == 3))
            prev = cur
            yield
        for (bk, gi_, dd, ff) in ((B_W, 2, sm4_[:, 2, :], sm4_[:, 3, :]), (B_S, 1, sm4_[:, 4, :], sm4_[:, 5, :])):
            acc = C.PS[:, bk, 0:260].rearrange("p (h c) -> p h c", c=65)
            sc.op("dve", lambda e, acc=acc, dd=dd: e.reciprocal(out=dd, in_=acc[:, :, 64]), reads=[C.pr[bk]], writes=[r_sm[g]])
            sc.op("dve", lambda e, dd=dd, ff=ff, gi_=gi_: e.tensor_tensor(out=ff, in0=dd, in1=gt[:, :, gi_], op=ALU.mult), reads=[r_sm[g]], writes=[r_sm[g]])
            sc.op("dve", lambda e, acc=acc, ff=ff: e.tensor_tensor(out=otmp_, in0=acc[:, :, 0:64], in1=ff.unsqueeze(2).to_broadcast([128, 4, 64]), op=ALU.mult),
                  reads=[C.pr[bk], r_sm[g]], writes=[r_ot[g]])
            sc.op("pool", lambda e: e.tensor_tensor(out=oacc_, in0=oacc_, in1=otmp_, op=ALU.add), reads=[r_ot[g], r_o[g]], writes=[r_o[g]])
            yield
        sc.op("act", lambda e: e.activation(out=ob[:, 256 * g:256 * g + 256], in_=oacc_.rearrange("p h d -> p (h d)"), func=AF.Copy), reads=[r_o[g]], writes=[r_ob[g]])

    def finish_tile(t):
        tsl = slice(t * 128, (t + 1) * 128)
        for c in range(4):
            sc.op("pe", lambda e, c=c: e.transpose(out=psb(C, B_T)[:, 512 + c * 128:512 + (c + 1) * 128], in_=ob[:, c * 128:(c + 1) * 128], identity=C.identb[:]),
                  reads=r_ob + [C.r_const], writes=[C.pr[B_T]], count=(c == 3))
        sc.op("act", lambda e: e.activation(out=mixT[:, 4:8, tsl], in_=psb(C, B_T)[:, 512:1024].rearrange("p (c t) -> p c t", t=128), func=AF.Copy),
              reads=[C.pr[B_T]], writes=[r_mix[t]])

    active = {0: (0, chain(0, 0)), 1: (0, chain(0, 1))}
    done = [0] * NT
    for _ in range(9):
        next(active[0][1])
    while active:
        for g in (0, 1):
            if g not in active:
                continue
            t, gen = active[g]
            try:
                next(gen)
            except StopIteration:
                done[t] += 1
                if done[t] == 2:
                    finish_tile(t)
                if t + 1 < NT:
                    active[g] = (t + 1, chain(t + 1, g))
                else:
                    del active[g]
    if DBG_STOP == 3:
        return
    out_proj(C, mixT, r_mix, WO, r_WO)
```
